# Optimizing a Trainium2 kernel written in Bass

```python
import jax, jax.numpy as jnp
from jax import lax
import numpy as np

D_MODEL = 2048
BATCH = 2
SEQ = 4096
DEPTH = 1
DEC_BATCH = 32
DEC_SEQ = 4
PAST_LEN = 8192
PAGE_SIZE = 128

HEAD_DIM = 128
N_HEADS = D_MODEL // HEAD_DIM
NSA_HEADS = N_HEADS // 2
FOX_HEADS = N_HEADS - NSA_HEADS
NSA_KV_HEADS = NSA_HEADS // 4
NSA_GROUP = NSA_HEADS // NSA_KV_HEADS
NSA_WIDTH = NSA_HEADS * HEAD_DIM
FOX_WIDTH = FOX_HEADS * HEAD_DIM
MIX_WIDTH = NSA_WIDTH + FOX_WIDTH
CMP_BLOCK = 32
SEL_BLOCK = 64
SEL_TOPK = 16
WINDOW = 512
Q_BLOCK = 128
FORCED_SCORE = 1e4
RMS_EPS = 1e-6
IN_SPLITS = (NSA_WIDTH, 6 * NSA_KV_HEADS * HEAD_DIM, 3 * NSA_HEADS, NSA_WIDTH,
             FOX_WIDTH, FOX_WIDTH, FOX_WIDTH, FOX_HEADS, FOX_WIDTH)
IN_COLS = sum(IN_SPLITS)

kernel_name = 'nsa_fox_parallel_heads_step'


def rmsnorm(x, g):
    xf = x.astype(jnp.float32)
    y = xf * lax.rsqrt(jnp.mean(xf * xf, axis=-1, keepdims=True) + RMS_EPS)
    return (y * g.astype(jnp.float32)).astype(x.dtype)


def alibi_slopes(n):
    return jnp.exp2(-8.0 * jnp.arange(1, n + 1, dtype=jnp.float32) / n)


def masked_softmax(s, mask):
    s = jnp.where(mask, s.astype(jnp.float32), -jnp.inf)
    m = jnp.max(s, axis=-1, keepdims=True)
    m = jnp.where(jnp.isfinite(m), m, 0.0)
    p = jnp.exp(s - m)
    return p / jnp.maximum(jnp.sum(p, axis=-1, keepdims=True), 1e-30)


def project(h, w_in, b_gate, b_forget):
    B, T, _ = h.shape
    z = jnp.einsum('btd,dc->btc', h, w_in)
    cuts = [int(c) for c in np.cumsum(IN_SPLITS)[:-1]]
    q_n, kv_n, g_n, z_n, q_f, k_f, v_f, f_f, z_f = jnp.split(z, cuts, axis=-1)
    q_n = q_n.reshape(B, T, NSA_HEADS, HEAD_DIM)
    kv_n = kv_n.reshape(B, T, 6, NSA_KV_HEADS, HEAD_DIM)
    gates = jax.nn.sigmoid((g_n + b_gate).astype(jnp.float32)).reshape(B, T, NSA_HEADS, 3)
    q_f = q_f.reshape(B, T, FOX_HEADS, HEAD_DIM)
    k_f = k_f.reshape(B, T, FOX_HEADS, HEAD_DIM)
    v_f = v_f.reshape(B, T, FOX_HEADS, HEAD_DIM)
    logf = jax.nn.log_sigmoid((f_f + b_forget).astype(jnp.float32))
    return q_n, kv_n, gates, z_n, q_f, k_f, v_f, logf, z_f


def compress(rows, pe, w):
    B, L = rows.shape[:2]
    n_cmp = L // CMP_BLOCK
    blk = rows[:, :n_cmp * CMP_BLOCK].reshape(B, n_cmp, CMP_BLOCK, NSA_KV_HEADS, HEAD_DIM)
    blk = blk + pe[None, None, :, None, :]
    out = jnp.einsum('bnchd,cde->bnhe', blk, w)
    end = (jnp.arange(n_cmp) + 1) * CMP_BLOCK - 1
    return out, end


def to_sel_blocks(rows):
    B, L = rows.shape[:2]
    n_sel = -(-L // SEL_BLOCK)
    rows = jnp.pad(rows, ((0, 0), (0, n_sel * SEL_BLOCK - L), (0, 0), (0, 0)))
    return rows.reshape(B, n_sel, SEL_BLOCK, NSA_KV_HEADS, HEAD_DIM).transpose(0, 3, 1, 2, 4)


def nsa_block(q, q_pos, gates, kc, vc, c_end, ks, vs, kw, vw, w_pos, slopes):
    B, Tb = q.shape[:2]
    f32 = jnp.float32
    scale = HEAD_DIM ** -0.5
    qg = q.reshape(B, Tb, NSA_KV_HEADS, NSA_GROUP, HEAD_DIM)
    sl = slopes.reshape(NSA_KV_HEADS, NSA_GROUP)
    dist_c = (q_pos[:, None] - c_end[None, :]).astype(f32)
    s_c = jnp.einsum('bthgd,bnhd->bhgtn', qg, kc).astype(f32) * scale - sl[:, :, None, None] * dist_c
    p_c = masked_softmax(s_c, dist_c >= 0)
    o_c = jnp.einsum('bhgtn,bnhd->bthgd', p_c.astype(vc.dtype), vc)
    n_cmp, n_sel = kc.shape[1], ks.shape[2]
    ratio = SEL_BLOCK // CMP_BLOCK
    p_pad = jnp.pad(p_c, ((0, 0), (0, 0), (0, 0), (0, 0), (0, n_sel * ratio - n_cmp)))
    score = p_pad.reshape(B, NSA_KV_HEADS, NSA_GROUP, Tb, n_sel, ratio).sum(axis=(2, 5))
    blk = jnp.arange(n_sel)[None, :]
    cur = (q_pos // SEL_BLOCK)[:, None]
    forced = (blk == 0) | (blk == cur) | (blk == cur - 1)
    score = jnp.where(blk <= cur, jnp.where(forced, FORCED_SCORE, score), -1.0)
    top_s, idx = lax.top_k(score, min(SEL_TOPK, n_sel))
    bi = jnp.arange(B)[:, None, None, None]
    hi = jnp.arange(NSA_KV_HEADS)[None, :, None, None]
    kg = ks[bi, hi, idx]
    vg = vs[bi, hi, idx]
    n_k = idx.shape[-1] * SEL_BLOCK
    k_pos = idx[..., None] * SEL_BLOCK + jnp.arange(SEL_BLOCK)
    dist_s = (q_pos[None, None, :, None, None] - k_pos).astype(f32)
    mask_s = ((dist_s >= 0) & (top_s >= 0)[..., None]).reshape(B, NSA_KV_HEADS, 1, Tb, n_k)
    s_s = jnp.einsum('bthgd,bhtkcd->bhgtkc', qg, kg).astype(f32) * scale
    s_s = s_s - sl[None, :, :, None, None, None] * dist_s[:, :, None]
    p_s = masked_softmax(s_s.reshape(B, NSA_KV_HEADS, NSA_GROUP, Tb, n_k), mask_s)
    o_s = jnp.einsum('bhgtn,bhtnd->bthgd', p_s.astype(vg.dtype),
                     vg.reshape(B, NSA_KV_HEADS, Tb, n_k, HEAD_DIM))
    dist_w = q_pos[:, None] - w_pos[None, :]
    mask_w = (dist_w >= 0) & (dist_w < WINDOW) & (w_pos[None, :] >= 0)
    s_w = jnp.einsum('bthgd,bshd->bhgts', qg, kw).astype(f32) * scale - sl[:, :, None, None] * dist_w.astype(f32)
    p_w = masked_softmax(s_w, mask_w)
    o_w = jnp.einsum('bhgts,bshd->bthgd', p_w.astype(vw.dtype), vw)
    g = gates.reshape(B, Tb, NSA_KV_HEADS, NSA_GROUP, 3)
    o = g[..., 0:1] * o_c + g[..., 1:2] * o_s + g[..., 2:3] * o_w
    return o.reshape(B, Tb, NSA_HEADS, HEAD_DIM).astype(q.dtype)


def fox_block(q, q_pos, dq, k, v, dk, k_pos):
    s = jnp.einsum('bthd,bshd->bhts', q, k).astype(jnp.float32) * (HEAD_DIM ** -0.5)
    s = s + jnp.swapaxes(dq, 1, 2)[:, :, :, None] - jnp.swapaxes(dk, 1, 2)[:, :, None, :]
    p = masked_softmax(s, k_pos[None, :] <= q_pos[:, None])
    return jnp.einsum('bhts,bshd->bthd', p.astype(v.dtype), v)


def merge(x, o_n, z_n, o_f, z_f, w_out):
    mix = jnp.concatenate([o_n * jax.nn.silu(z_n), o_f * jax.nn.silu(z_f)], axis=-1)
    return x + jnp.einsum('btc,cd->btd', mix, w_out)


def prompt_layer(x, norm_in, w_in, b_gate, b_forget, cmp_pe, cmp_w, w_out):
    B, T, _ = x.shape
    h = rmsnorm(x, norm_in)
    q_n, kv_n, gates, z_n, q_f, k_f, v_f, logf, z_f = project(h, w_in, b_gate, b_forget)
    slopes = alibi_slopes(NSA_HEADS)
    kc, c_end = compress(kv_n[:, :, 0], cmp_pe[0], cmp_w[0])
    vc, _ = compress(kv_n[:, :, 1], cmp_pe[1], cmp_w[1])
    ks = to_sel_blocks(kv_n[:, :, 2])
    vs = to_sel_blocks(kv_n[:, :, 3])
    pad = ((0, 0), (WINDOW, 0), (0, 0), (0, 0))
    kw_pad = jnp.pad(kv_n[:, :, 4], pad)
    vw_pad = jnp.pad(kv_n[:, :, 5], pad)
    d_f = lax.cumsum(logf, axis=1)
    k_pos = jnp.arange(T)

    def sweep(i):
        s0 = i * Q_BLOCK
        q_pos = s0 + jnp.arange(Q_BLOCK)
        blk = lambda a: lax.dynamic_slice_in_dim(a, s0, Q_BLOCK, axis=1)
        band = lambda a: lax.dynamic_slice_in_dim(a, s0, WINDOW + Q_BLOCK, axis=1)
        w_pos = s0 - WINDOW + jnp.arange(WINDOW + Q_BLOCK)
        o_n = nsa_block(blk(q_n), q_pos, blk(gates), kc, vc, c_end, ks, vs,
                        band(kw_pad), band(vw_pad), w_pos, slopes)
        o_f = fox_block(blk(q_f), q_pos, blk(d_f), k_f, v_f, d_f, k_pos)
        return o_n, o_f

    o_n, o_f = lax.map(sweep, jnp.arange(T // Q_BLOCK))
    o_n = jnp.swapaxes(o_n, 0, 1).reshape(B, T, NSA_WIDTH)
    o_f = jnp.swapaxes(o_f, 0, 1).reshape(B, T, FOX_WIDTH)
    y = merge(x, o_n, z_n, o_f, z_f, w_out)
    w_keep = min(WINDOW, T)
    return y, (kv_n[:, :, :4], jnp.stack([k_f, v_f], axis=2), logf, kv_n[:, T - w_keep:, 4:])


def sample_layer(x, cache_nsa_kv, cache_fox_kv, cache_fox_logf, state_nsa_win, page_table,
                 norm_in, w_in, b_gate, b_forget, cmp_pe, cmp_w, w_out):
    B, T, _ = x.shape
    P = page_table.shape[1] * PAGE_SIZE
    h = rmsnorm(x, norm_in)
    q_n, kv_n, gates, z_n, q_f, k_f, v_f, logf, z_f = project(h, w_in, b_gate, b_forget)
    slopes = alibi_slopes(NSA_HEADS)
    q_pos = P + jnp.arange(T)
    past_nsa = cache_nsa_kv[page_table].reshape(B, P, 4, NSA_KV_HEADS, HEAD_DIM)
    nsa_all = jnp.concatenate([past_nsa, kv_n[:, :, :4]], axis=1)
    kc, c_end = compress(nsa_all[:, :, 0], cmp_pe[0], cmp_w[0])
    vc, _ = compress(nsa_all[:, :, 1], cmp_pe[1], cmp_w[1])
    ks = to_sel_blocks(nsa_all[:, :, 2])
    vs = to_sel_blocks(nsa_all[:, :, 3])
    win_all = jnp.concatenate([state_nsa_win, kv_n[:, :, 4:]], axis=1)
    w_buf = state_nsa_win.shape[1]
    w_pos = P - w_buf + jnp.arange(w_buf + T)
    o_n = nsa_block(q_n, q_pos, gates, kc, vc, c_end, ks, vs,
                    win_all[:, :, 0], win_all[:, :, 1], w_pos, slopes)
    past_fox = cache_fox_kv[page_table].reshape(B, P, 2, FOX_HEADS, HEAD_DIM)
    k_all = jnp.concatenate([past_fox[:, :, 0], k_f], axis=1)
    v_all = jnp.concatenate([past_fox[:, :, 1], v_f], axis=1)
    logf_all = jnp.concatenate([cache_fox_logf[page_table].reshape(B, P, FOX_HEADS), logf],
                               axis=1).astype(jnp.float32)
    d_all = lax.cumsum(logf_all, axis=1)
    o_f = fox_block(q_f, q_pos, d_all[:, P:], k_all, v_all, d_all, jnp.arange(P + T))
    y = merge(x, o_n.reshape(B, T, NSA_WIDTH), z_n, o_f.reshape(B, T, FOX_WIDTH), z_f, w_out)
    return y, (kv_n[:, :, :4], jnp.stack([k_f, v_f], axis=2), logf, win_all[:, T:])


def setup_inputs(seed: int = 0) -> dict:
    key = jax.random.key(seed)
    ks = jax.random.split(key, 16)
    f32 = jnp.float32
    n_pages = PAST_LEN // PAGE_SIZE
    n_used = DEC_BATCH * n_pages
    n_pool = (5 * n_used + 3) // 4
    w_buf = min(WINDOW, PAST_LEN)
    nrm = jax.random.normal
    x_prompt = nrm(ks[0], (BATCH, SEQ, D_MODEL), f32)
    x_sample = nrm(ks[1], (DEC_BATCH, DEC_SEQ, D_MODEL), f32)
    cache_nsa_kv = nrm(ks[2], (DEPTH, n_pool, PAGE_SIZE, 4, NSA_KV_HEADS, HEAD_DIM), f32)
    cache_fox_kv = nrm(ks[3], (DEPTH, n_pool, PAGE_SIZE, 2, FOX_HEADS, HEAD_DIM), f32)
    cache_fox_logf = jax.nn.log_sigmoid(2.0 + nrm(ks[4], (DEPTH, n_pool, PAGE_SIZE, FOX_HEADS), f32))
    state_nsa_win = nrm(ks[5], (DEPTH, DEC_BATCH, w_buf, 2, NSA_KV_HEADS, HEAD_DIM), f32)
    page_table = jax.random.permutation(ks[6], n_pool)[:n_used].reshape(DEC_BATCH, n_pages).astype(jnp.int32)
    norm_in = 1.0 + 0.01 * nrm(ks[7], (DEPTH, D_MODEL), f32)
    w_in = nrm(ks[8], (DEPTH, D_MODEL, IN_COLS), f32) * D_MODEL ** -0.5
    b_gate = 0.1 * nrm(ks[9], (DEPTH, 3 * NSA_HEADS), f32)
    b_forget = 2.0 + 0.1 * nrm(ks[10], (DEPTH, FOX_HEADS), f32)
    cmp_pe = 0.1 * nrm(ks[11], (DEPTH, 2, CMP_BLOCK, HEAD_DIM), f32)
    cmp_w = nrm(ks[12], (DEPTH, 2, CMP_BLOCK, HEAD_DIM, HEAD_DIM), f32) * (CMP_BLOCK * HEAD_DIM) ** -0.5
    w_out = nrm(ks[13], (DEPTH, MIX_WIDTH, D_MODEL), f32) * MIX_WIDTH ** -0.5
    norm_final = 1.0 + 0.01 * nrm(ks[14], (D_MODEL,), f32)
    return {'x_prompt': x_prompt, 'x_sample': x_sample, 'cache_nsa_kv': cache_nsa_kv,
            'cache_fox_kv': cache_fox_kv, 'cache_fox_logf': cache_fox_logf,
            'state_nsa_win': state_nsa_win, 'page_table': page_table,
            'norm_in': norm_in, 'w_in': w_in, 'b_gate': b_gate, 'b_forget': b_forget,
            'cmp_pe': cmp_pe, 'cmp_w': cmp_w, 'w_out': w_out, 'norm_final': norm_final}


def reference(x_prompt, x_sample, cache_nsa_kv, cache_fox_kv, cache_fox_logf, state_nsa_win, page_table,
              norm_in, w_in, b_gate, b_forget, cmp_pe, cmp_w, w_out, norm_final):
    hp, hs = x_prompt, x_sample
    st_p, st_s = [], []
    for d in range(DEPTH):
        lw = (norm_in[d], w_in[d], b_gate[d], b_forget[d], cmp_pe[d], cmp_w[d], w_out[d])
        hp, sp = prompt_layer(hp, *lw)
        hs, ss = sample_layer(hs, cache_nsa_kv[d], cache_fox_kv[d], cache_fox_logf[d],
                              state_nsa_win[d], page_table, *lw)
        st_p.append(sp)
        st_s.append(ss)
    y_prompt = rmsnorm(hp, norm_final)
    y_sample = rmsnorm(hs, norm_final)
    nsa_kv_prompt = jnp.stack([s[0] for s in st_p])
    nsa_kv_sample = jnp.stack([s[0] for s in st_s])
    fox_kv_prompt = jnp.stack([s[1] for s in st_p])
    fox_kv_sample = jnp.stack([s[1] for s in st_s])
    fox_logf_prompt = jnp.stack([s[2] for s in st_p])
    fox_logf_sample = jnp.stack([s[2] for s in st_s])
    nsa_win_prompt = jnp.stack([s[3] for s in st_p])
    nsa_win_sample = jnp.stack([s[3] for s in st_s])
    return (y_prompt, y_sample, nsa_kv_prompt, nsa_kv_sample, fox_kv_prompt, fox_kv_sample,
            fox_logf_prompt, fox_logf_sample, nsa_win_prompt, nsa_win_sample)
```

```python
from contextlib import ExitStack
import numpy as np
import concourse.bass as bass
import concourse.mybir as mybir
from concourse.bass_utils import run_bass_kernel_spmd

F32 = mybir.dt.float32
BF16 = mybir.dt.bfloat16
AF = mybir.ActivationFunctionType
ALU = mybir.AluOpType

ENGS = ("pe", "act", "dve", "pool", "sp")
NSLOT = 6


class Buf:
    __slots__ = ("name", "last_w", "readers")

    def __init__(self, name):
        self.name = name
        self.last_w = None
        self.readers = []


class Sched:
    def __init__(self, nc, stack):
        self.nc = nc
        self.stack = stack
        self.ops = []
        self.ndma = {e: 0 for e in ENGS}
        self.nbuf = 0

    def sb(self, name, shape, dt):
        return self.stack.enter_context(self.nc.sbuf_tensor(name, list(shape), dt))

    def ps(self, name, shape, dt=mybir.dt.float32):
        return self.stack.enter_context(self.nc.psum_tensor(name, list(shape), dt))

    def buf(self, name=None):
        self.nbuf += 1
        return Buf(name or f"b{self.nbuf}")

    def op(self, eng, fn, r=(), w=(), dma=False):
        idx = len(self.ops)
        deps = set()
        for b in r:
            if b.last_w is not None:
                deps.add(b.last_w)
        for b in w:
            if b.last_w is not None:
                deps.add(b.last_w)
            deps.update(b.readers)
        for b in r:
            b.readers.append(idx)
        for b in w:
            b.last_w = idx
            b.readers = []
        slot = None
        if dma:
            n = self.ndma[eng]
            self.ndma[eng] = n + 1
            slot = n
        deps.discard(idx)
        self.ops.append(dict(eng=eng, fn=fn, deps=deps, dma=dma, slot=slot, sig=False))
        return idx

    def dma(self, eng, out, in_, r=(), w=(), **kw):
        return self.op(eng, lambda e: e.dma_start(out=out, in_=in_, **kw), r=r, w=w, dma=True)

    def emit(self, final_wait_all=True):
        nc = self.nc
        ops = self.ops
        for i, o in enumerate(ops):
            if o["eng"] == "pe" and not o["dma"]:
                o["deps"] = {d for d in o["deps"] if not (ops[d]["eng"] == "pe" and not ops[d]["dma"])}
        last_in_slot = {}
        for i, o in enumerate(ops):
            if o["dma"]:
                key = (o["eng"], o["slot"] % NSLOT)
                if key in last_in_slot:
                    o["deps"].add(last_in_slot[key])
                last_in_slot[key] = i
        final = []
        if final_wait_all:
            for key, i in last_in_slot.items():
                final.append(i)
        for o in ops:
            for d in o["deps"]:
                ops[d]["sig"] = True
        for d in final:
            ops[d]["sig"] = True
        sem_e = {e: self.stack.enter_context(nc.semaphore(f"s_{e}")) for e in ENGS}
        sem_d = {}
        for e in ENGS:
            if self.ndma[e]:
                for s in range(min(NSLOT, self.ndma[e])):
                    sem_d[(e, s)] = self.stack.enter_context(nc.semaphore(f"d_{e}{s}"))
        cnt = {e: 0 for e in ENGS}
        for o in ops:
            if o["dma"]:
                o["sem"] = sem_d[(o["eng"], o["slot"] % NSLOT)]
                o["val"] = 16 * (o["slot"] // NSLOT + 1)
            elif o["sig"]:
                cnt[o["eng"]] += 1
                o["sem"] = sem_e[o["eng"]]
                o["val"] = cnt[o["eng"]]
        per = {e: [i for i, o in enumerate(ops) if o["eng"] == e] for e in ENGS}
        self.stats = {e: len(per[e]) for e in ENGS}

        def run(engname, e):
            known = {}
            nw = 0
            for i in per[engname]:
                o = ops[i]
                need = {}
                for d in o["deps"]:
                    od = ops[d]
                    k = id(od["sem"])
                    if known.get(k, 0) >= od["val"]:
                        continue
                    if k not in need or need[k][1] < od["val"]:
                        need[k] = (od["sem"], od["val"])
                for k, (s, v) in need.items():
                    e.wait_ge(s, v)
                    known[k] = v
                    nw += 1
                ins = o["fn"](e)
                if o["dma"]:
                    ins.then_inc(o["sem"], 16)
                elif o["sig"]:
                    ins.then_inc(o["sem"], 1)
            if engname == "sp":
                for d in final:
                    od = ops[d]
                    if known.get(id(od["sem"]), 0) < od["val"]:
                        e.wait_ge(od["sem"], od["val"])
                        known[id(od["sem"])] = od["val"]
            self.stats[engname + "_waits"] = nw

        with nc.Block() as block:
            @block.tensor
            def _(e):
                run("pe", e)

            @block.scalar
            def _(e):
                run("act", e)

            @block.vector
            def _(e):
                run("dve", e)

            @block.gpsimd
            def _(e):
                run("pool", e)

            @block.sync
            def _(e):
                run("sp", e)


D_MODEL = 2048
IN_COLS = 7712
C_QN = 0
C_GATE = 2560
C_ZN = 2584
C_QF = 3608
C_ZF = 6688
C_KVN = 1024
C_KF = 4632
C_VF = 5656
C_FF = 6680
RMS_EPS = 1e-6
W_BUF = 512


def build_nc(SEQ, DB):
    TOK = SEQ // 4
    NT = TOK // 128
    NTT = NT + 1
    assert DB * 4 == 128
    nc = bass.Bass("TRN2", target_bir_lowering=False)
    dt = nc.dram_tensor
    x_own = dt("x_own", [TOK, D_MODEL], F32, kind="ExternalInput").ap()
    x_s = dt("x_s", [128, D_MODEL], F32, kind="ExternalInput").ap()
    w_in = dt("w_in", [D_MODEL, IN_COLS], F32, kind="ExternalInput").ap()
    g_in = dt("norm_in", [1, D_MODEL], F32, kind="ExternalInput").ap()
    b_fg = dt("b_forget", [1, 8], F32, kind="ExternalInput").ap()
    b_gt = dt("b_gate", [1, 24], F32, kind="ExternalInput").ap()
    ident = dt("ident", [128, 128], F32, kind="ExternalInput").ap()
    st_win = dt("state_win", [DB, W_BUF, 512], F32, kind="ExternalInput").ap()
    o_nkv_p = dt("o_nkv_p", [TOK, 1024], F32, kind="ExternalOutput").ap()
    o_fkv_p = dt("o_fkv_p", [TOK, 2048], F32, kind="ExternalOutput").ap()
    o_lf_p = dt("o_lf_p", [TOK, 8], F32, kind="ExternalOutput").ap()
    o_win_p = dt("o_win_p", [TOK, 512], F32, kind="ExternalOutput").ap()
    o_nkv_s = dt("o_nkv_s", [128, 1024], F32, kind="ExternalOutput").ap()
    o_fkv_s = dt("o_fkv_s", [128, 2048], F32, kind="ExternalOutput").ap()
    o_lf_s = dt("o_lf_s", [128, 8], F32, kind="ExternalOutput").ap()
    o_win_s = dt("o_win_s", [DB, W_BUF, 512], F32, kind="ExternalOutput").ap()

    o_q = {False: dt("o_q_p", [TOK, 2048], BF16, kind="ExternalOutput").ap(),
           True: dt("o_q_s", [128, 2048], BF16, kind="ExternalOutput").ap()}
    o_z = {False: dt("o_z_p", [TOK, 2048], BF16, kind="ExternalOutput").ap(),
           True: dt("o_z_s", [128, 2048], BF16, kind="ExternalOutput").ap()}
    o_g = {False: dt("o_g_p", [TOK, 24], F32, kind="ExternalOutput").ap(),
           True: dt("o_g_s", [128, 24], F32, kind="ExternalOutput").ap()}
    w_v = w_in.rearrange("(k p) c -> p k c", p=128)

    with ExitStack() as st:
        S = Sched(nc, st)
        idf = S.sb("idf", [128, 128], F32); b_idf = S.buf()
        idb = S.sb("idb", [128, 128], BF16); b_idb = S.buf()
        gam = S.sb("gam", [128, D_MODEL], F32); b_gam = S.buf()
        bfg = S.sb("bfg", [128, 8], F32); b_bfg = S.buf()
        hT = S.sb("hT", [128, 16, NTT * 128], BF16)
        b_hT = [S.buf() for _ in range(NTT)]
        xt = [S.sb(f"xt{i}", [128, D_MODEL], F32) for i in range(2)]; b_xt = [S.buf() for _ in range(2)]
        hb = [S.sb(f"hb{i}", [128, D_MODEL], BF16) for i in range(2)]; b_hb = [S.buf() for _ in range(2)]
        sq = S.sb("sq", [128, D_MODEL], BF16); b_sq = S.buf()
        ss = [S.sb(f"ss{i}", [128, 4], F32) for i in range(2)]; b_ss = [S.buf() for _ in range(2)]
        pT = [S.ps(f"pT{i}", [128, 512], BF16) for i in range(2)]; b_pT = [S.buf() for _ in range(2)]
        acc = [S.ps(f"acc{i}", [128, 512], F32) for i in range(2)]; b_acc = [S.buf() for _ in range(2)]
        accs = S.ps("accs", [128, 8], F32); b_accs = S.buf()
        wt = [S.sb(f"wt{i}", [128, 16, 512], BF16) for i in range(2)]; b_wt = [S.buf() for _ in range(2)]
        stg = [S.sb(f"stg{i}", [128, 512], F32) for i in range(4)]; b_stg = [S.buf() for _ in range(4)]
        bgt = S.sb("bgt", [128, 24], F32); b_bgt = S.buf()
        gt = [S.sb(f"gt{i}", [128, 24], F32) for i in range(2)]; b_gt_ = [S.buf() for _ in range(2)]
        accg = S.ps("accg", [128, 24], F32); b_accg = S.buf()
        stb = [S.sb(f"stb{i}", [128, 512], BF16) for i in range(4)]; b_stb = [S.buf() for _ in range(4)]
        lft = [S.sb(f"lft{i}", [128, 8], F32) for i in range(2)]; b_lft = [S.buf() for _ in range(2)]

        S.dma("sp", idf[:], ident, w=[b_idf])
        S.dma("sp", gam[:], g_in.partition_broadcast(128), w=[b_gam])
        S.dma("sp", bfg[:], b_fg.partition_broadcast(128), w=[b_bfg])
        S.op("dve", lambda e: e.tensor_copy(out=idb[:], in_=idf[:]), r=[b_idf], w=[b_idb])
        S.dma("sp", bgt[:], b_gt.partition_broadcast(128), w=[b_bgt])

        bw = S.buf()
        S.dma("sp", o_win_s[:, 0:W_BUF - 4, :], st_win[:, 4:W_BUF, :], w=[bw])

        for tt in range(NTT):
            i = tt % 2
            src = x_own[tt * 128:(tt + 1) * 128, :] if tt < NT else x_s
            S.dma("sp", xt[i][:], src, w=[b_xt[i]])
            S.op("dve", lambda e, i=i: e.memset(ss[i][:], 0.0), w=[b_ss[i]])
            S.op("act", lambda e, i=i: e.activation(out=sq[:], in_=xt[i][:], func=AF.Square,
                                                    accum_out=ss[i][:, 0:1]),
                 r=[b_xt[i]], w=[b_sq, b_ss[i]])
            S.op("act", lambda e, i=i: e.activation(out=ss[i][:, 1:2], in_=ss[i][:, 0:1], func=AF.Sqrt,
                                                    scale=1.0 / D_MODEL, bias=RMS_EPS),
                 r=[b_ss[i]], w=[b_ss[i]])
            S.op("dve", lambda e, i=i: e.reciprocal(out=ss[i][:, 2:3], in_=ss[i][:, 1:2]),
                 r=[b_ss[i]], w=[b_ss[i]])
            S.op("dve", lambda e, i=i: e.scalar_tensor_tensor(out=hb[i][:], in0=xt[i][:], scalar=ss[i][:, 2:3],
                                                              in1=gam[:], op0=ALU.mult, op1=ALU.mult),
                 r=[b_xt[i], b_ss[i], b_gam], w=[b_hb[i]])
            for g4 in range(4):
                p = g4 % 2
                for q in range(4):
                    k = g4 * 4 + q
                    S.op("pe", lambda e, i=i, p=p, q=q, k=k: e.transpose(
                        out=pT[p][:, q * 128:(q + 1) * 128], in_=hb[i][:, k * 128:(k + 1) * 128], identity=idb[:]),
                         r=[b_hb[i], b_idb], w=[b_pT[p]])
                eng = "act" if g4 % 2 == 0 else "dve"
                dst = hT[:, g4 * 4:(g4 + 1) * 4, tt * 128:(tt + 1) * 128]
                srcp = pT[p][:].rearrange("p (q t) -> p q t", q=4)
                if eng == "act":
                    S.op("act", lambda e, dst=dst, srcp=srcp: e.copy(out=dst, in_=srcp), r=[b_pT[p]], w=[b_hT[tt]])
                else:
                    S.op("dve", lambda e, dst=dst, srcp=srcp: e.tensor_copy(out=dst, in_=srcp), r=[b_pT[p]], w=[b_hT[tt]])

        groups = [(C_KVN + 512 * j, 512, "nkv", j) for j in range(3)] + \
                 [(C_KF + 512 * j, 512, "fkv", j) for j in range(2)] + \
                 [(C_VF + 512 * j, 512, "fkv", 2 + j) for j in range(2)] + \
                 [(C_FF, 8, "lf", 0), (C_GATE, 24, "gate", 0)] + \
                 [(C_QN + 512 * j, 512, "q", j) for j in range(2)] + \
                 [(C_QF + 512 * j, 512, "q", 2 + j) for j in range(2)] + \
                 [(C_ZN + 512 * j, 512, "z", j) for j in range(2)] + \
                 [(C_ZF + 512 * j, 512, "z", 2 + j) for j in range(2)]
        nstg = 0
        nstb = 0
        for gi, (c0, wd, kind, j) in enumerate(groups):
            wi = gi % 2
            S.dma("pool", wt[wi][:, :, 0:wd], w_v[:, :, c0:c0 + wd], w=[b_wt[wi]])
            for tt in range(NTT):
                samp = tt == NT
                rows = slice(tt * 128, (tt + 1) * 128)
                if kind == "lf":
                    for k in range(16):
                        S.op("pe", lambda e, wi=wi, k=k, tt=tt: e.matmul(
                            accs[:], lhsT=hT[:, k, tt * 128:(tt + 1) * 128], rhs=wt[wi][:, k, 0:8],
                            start=(k == 0), stop=(k == 15)), r=[b_hT[tt], b_wt[wi]], w=[b_accs])
                    li = tt % 2
                    S.op("dve", lambda e, li=li: e.tensor_tensor(out=lft[li][:], in0=accs[:], in1=bfg[:], op=ALU.add),
                         r=[b_accs, b_bfg], w=[b_lft[li]])
                    S.op("act", lambda e, li=li: e.activation(out=lft[li][:], in_=lft[li][:], func=AF.Exp, scale=-1.0),
                         r=[b_lft[li]], w=[b_lft[li]])
                    S.op("act", lambda e, li=li: e.activation(out=lft[li][:], in_=lft[li][:], func=AF.Ln, bias=1.0),
                         r=[b_lft[li]], w=[b_lft[li]])
                    S.op("dve", lambda e, li=li: e.tensor_scalar_mul(out=lft[li][:], in0=lft[li][:], scalar1=-1.0),
                         r=[b_lft[li]], w=[b_lft[li]])
                    S.dma("sp", o_lf_s if samp else o_lf_p[rows, :], lft[li][:], r=[b_lft[li]])
                    continue
                if kind == "gate":
                    for k in range(16):
                        S.op("pe", lambda e, wi=wi, k=k, tt=tt: e.matmul(
                            accg[:], lhsT=hT[:, k, tt * 128:(tt + 1) * 128], rhs=wt[wi][:, k, 0:24],
                            start=(k == 0), stop=(k == 15)), r=[b_hT[tt], b_wt[wi]], w=[b_accg])
                    li = tt % 2
                    S.op("dve", lambda e, li=li: e.tensor_tensor(out=gt[li][:], in0=accg[:], in1=bgt[:], op=ALU.add),
                         r=[b_accg, b_bgt], w=[b_gt_[li]])
                    S.op("act", lambda e, li=li: e.activation(out=gt[li][:], in_=gt[li][:], func=AF.Exp, scale=-1.0),
                         r=[b_gt_[li]], w=[b_gt_[li]])
                    S.op("dve", lambda e, li=li: e.tensor_scalar_add(out=gt[li][:], in0=gt[li][:], scalar1=1.0),
                         r=[b_gt_[li]], w=[b_gt_[li]])
                    S.op("dve", lambda e, li=li: e.reciprocal(out=gt[li][:], in_=gt[li][:]),
                         r=[b_gt_[li]], w=[b_gt_[li]])
                    S.dma("sp", o_g[samp] if samp else o_g[samp][rows, :], gt[li][:], r=[b_gt_[li]])
                    continue
                a = (gi * NTT + tt) % 2
                for k in range(16):
                    S.op("pe", lambda e, a=a, wi=wi, k=k, tt=tt: e.matmul(
                        acc[a][:], lhsT=hT[:, k, tt * 128:(tt + 1) * 128], rhs=wt[wi][:, k, :],
                        start=(k == 0), stop=(k == 15)), r=[b_hT[tt], b_wt[wi]], w=[b_acc[a]])
                if kind in ("q", "z"):
                    bi = nstb % 4
                    nstb += 1
                    if kind == "q":
                        S.op("dve", lambda e, bi=bi, a=a: e.tensor_copy(out=stb[bi][:], in_=acc[a][:]),
                             r=[b_acc[a]], w=[b_stb[bi]])
                    else:
                        S.op("act", lambda e, bi=bi, a=a: e.activation(out=stb[bi][:], in_=acc[a][:], func=AF.Silu),
                             r=[b_acc[a]], w=[b_stb[bi]])
                    od = (o_q if kind == "q" else o_z)[samp]
                    od = (od if samp else od[rows, :])[:, j * 512:(j + 1) * 512]
                    S.dma("sp", od, stb[bi][:], r=[b_stb[bi]])
                    continue
                si = nstg % 4
                nstg += 1
                if nstg % 2:
                    S.op("act", lambda e, si=si, a=a: e.copy(out=stg[si][:], in_=acc[a][:]), r=[b_acc[a]], w=[b_stg[si]])
                else:
                    S.op("dve", lambda e, si=si, a=a: e.tensor_copy(out=stg[si][:], in_=acc[a][:]), r=[b_acc[a]], w=[b_stg[si]])
                if kind == "nkv":
                    if j < 2:
                        dst = (o_nkv_s if samp else o_nkv_p[rows, :])[:, j * 512:(j + 1) * 512]
                        S.dma("sp", dst, stg[si][:], r=[b_stg[si]])
                    elif not samp:
                        S.dma("sp", o_win_p[rows, :], stg[si][:], r=[b_stg[si]])
                    else:
                        for bb in range(DB):
                            S.dma("sp", o_win_s[bb, W_BUF - 4:W_BUF, :], stg[si][bb * 4:(bb + 1) * 4, :],
                                  r=[b_stg[si]], w=[bw])
                else:
                    dst = (o_fkv_s if samp else o_fkv_p[rows, :])[:, j * 512:(j + 1) * 512]
                    S.dma("sp", dst, stg[si][:], r=[b_stg[si]])
        S.emit()
    return nc


def launch1(x_prompt, x_sample, state_nsa_win, norm_in, w_in, b_gate, b_forget):
    f32 = np.float32
    x_prompt = np.asarray(x_prompt, f32)
    B, SEQ, D = x_prompt.shape
    DB, DS = np.asarray(x_sample).shape[:2]
    TOK = SEQ // 4
    nc = build_nc(SEQ, DB)
    ident = np.eye(128, dtype=f32)
    xs = np.ascontiguousarray(np.asarray(x_sample, f32).reshape(DB * DS, D))
    stw = np.ascontiguousarray(np.asarray(state_nsa_win, f32)[0].reshape(DB, W_BUF, 512))
    common = dict(x_s=xs, w_in=np.ascontiguousarray(np.asarray(w_in, f32)[0]),
                  norm_in=np.asarray(norm_in, f32).reshape(1, D), b_forget=np.asarray(b_forget, f32).reshape(1, 8),
                  b_gate=np.asarray(b_gate, f32).reshape(1, 24), ident=ident, state_win=stw)
    in_maps = []
    for c in range(8):
        b, r = c // 4, c % 4
        m = dict(common)
        m["x_own"] = np.ascontiguousarray(x_prompt[b, r * TOK:(r + 1) * TOK])
        in_maps.append(m)
    res = run_bass_kernel_spmd(nc, in_maps, core_ids=list(range(8))).results

    def cat(name):
        return np.stack([np.concatenate([res[b * 4 + r][name] for r in range(4)], axis=0) for b in range(2)])

    r0 = res[0]
    out = dict(B=B, SEQ=SEQ, DB=DB, DS=DS)
    for nm in ("nkv", "fkv", "lf", "win", "q", "z", "g"):
        out[nm + "_p"] = cat(f"o_{nm}_p")
        out[nm + "_s"] = r0[f"o_{nm}_s"]
    return out


NEG = -30000.0
SCALE = 128 ** -0.5


def build_fox(SEQ, DB, NPG, NPOOL):
    NTS = SEQ // 128
    B = 2
    nc = bass.Bass("TRN2", target_bir_lowering=False)
    dt = nc.dram_tensor
    qT_p = dt("qT_p", [B, 128, SEQ], BF16, kind="ExternalInput").ap()
    kT_p = dt("kT_p", [B, 128, SEQ], F32, kind="ExternalInput").ap()
    v_p = dt("v_p", [B, SEQ, 128], F32, kind="ExternalInput").ap()
    lfT_p = dt("lfT_p", [128, B * NTS], F32, kind="ExternalInput").ap()
    qT_s = dt("qT_s", [128, 128], BF16, kind="ExternalInput").ap()
    kTn = dt("kTn", [128, 128], F32, kind="ExternalInput").ap()
    vn = dt("vn", [128, 128], F32, kind="ExternalInput").ap()
    lfn = dt("lfn", [DB, 4], F32, kind="ExternalInput").ap()
    kvpool = dt("kvpool", [NPOOL * 128, 256], F32, kind="ExternalInput").ap()
    lfpool = dt("lfpool", [NPOOL, 128], F32, kind="ExternalInput").ap()
    ptab = dt("ptab", [DB, NPG], mybir.dt.int32, kind="ExternalInput").ap()
    c_tri = dt("c_tri", [128, 128], F32, kind="ExternalInput").ap()
    c_sut = dt("c_sut", [128, 128], F32, kind="ExternalInput").ap()
    c_cneg = dt("c_cneg", [128, 128], F32, kind="ExternalInput").ap()
    c_id = dt("c_id", [128, 128], F32, kind="ExternalInput").ap()
    c_iota = dt("c_iota", [128, 1], F32, kind="ExternalInput").ap()
    c_n4 = dt("c_n4", [4, 4], F32, kind="ExternalInput").ap()
    c_tri4 = dt("c_tri4", [4, 4], F32, kind="ExternalInput").ap()
    o_p = dt("o_p", [B, SEQ, 128], F32, kind="ExternalOutput").ap()
    o_s = dt("o_s", [128, 128], F32, kind="ExternalOutput").ap()

    with ExitStack() as st:
        S = Sched(nc, st)
        tri = S.sb("tri", [128, 128], F32); sut = S.sb("sut", [128, 128], F32)
        cnf = S.sb("cnf", [128, 128], F32); cnb = S.sb("cnb", [128, 128], BF16)
        idf = S.sb("idf", [128, 128], F32); idb = S.sb("idb", [128, 128], BF16)
        iot = S.sb("iot", [128, 1], F32); ones = S.sb("ones", [128, 128], F32)
        n4f = S.sb("n4f", [4, 4], F32); n4b = S.sb("n4b", [4, 4], BF16); tri4 = S.sb("tri4", [4, 4], F32)
        b_c = S.buf("consts")
        for t_, a_ in ((tri, c_tri), (sut, c_sut), (cnf, c_cneg), (idf, c_id), (iot, c_iota), (n4f, c_n4), (tri4, c_tri4)):
            S.dma("sp", t_[:], a_, w=[b_c])
        S.op("dve", lambda e: e.tensor_copy(out=cnb[:], in_=cnf[:]), r=[b_c], w=[b_c])
        S.op("dve", lambda e: e.tensor_copy(out=idb[:], in_=idf[:]), r=[b_c], w=[b_c])
        S.op("dve", lambda e: e.tensor_copy(out=n4b[:], in_=n4f[:]), r=[b_c], w=[b_c])
        S.op("dve", lambda e: e.memset(ones[:], 1.0), w=[b_c])

        pst = [S.ps(f"pst{i}", [128, 128], F32) for i in range(2)]; b_pst = [S.buf() for _ in range(2)]
        pso = [S.ps(f"pso{i}", [128, 132], F32) for i in range(2)]; b_pso = [S.buf() for _ in range(2)]
        psd = S.ps("psd", [128, 128], F32); b_psd = S.buf()
        psd2 = S.ps("psd2", [128, 128], F32); b_psd2 = S.buf()
        ptl = [S.sb(f"ptl{i}", [128, 128], BF16) for i in range(3)]; b_ptl = [S.buf() for _ in range(3)]
        osb = [S.sb(f"osb{i}", [128, 132], F32) for i in range(2)]; b_osb = [S.buf() for _ in range(2)]
        cnt = dict(pst=0, ptl=0, pso=0, osb=0)

        kT = S.sb("kT", [128, SEQ], BF16); b_kT = S.buf()
        qT = S.sb("qT", [128, SEQ], BF16); b_qT = S.buf()
        vx = S.sb("vx", [128, NTS, 132], BF16); b_vx = S.buf()
        lf = S.sb("lf", [128, B * NTS], F32); b_lf = S.buf()
        Dm = S.sb("Dm", [128, NTS], F32); b_Dm = S.buf()
        Of = S.sb("Of", [128, NTS], F32); b_Of = S.buf()
        tot = S.sb("tot", [128, 128], F32); b_tot = S.buf()
        bq = S.sb("bq", [128, NTS], F32); b_bq = S.buf()
        totc = S.sb("totc", [128, 2], F32); b_totc = S.buf()
        S.dma("sp", lf[:], lfT_p, w=[b_lf])
        S.op("dve", lambda e: e.memset(vx[:, :, 128:129], 1.0), w=[b_vx])

        def attend(kt_ap, ns, q_ap, nq, v_ap, bias_ap, mask, acc_i, first, last, rd, extra_w=()):
            pi = cnt["pst"] % 2; cnt["pst"] += 1
            li = cnt["ptl"] % 3; cnt["ptl"] += 1
            S.op("pe", lambda e: e.matmul(pst[pi][0:ns, 0:nq], lhsT=kt_ap, rhs=q_ap, start=True, stop=(mask is None)),
                 r=rd, w=[b_pst[pi]])
            if mask is not None:
                S.op("pe", lambda e: e.matmul(pst[pi][0:ns, 0:nq], lhsT=mask[0], rhs=mask[1], start=False, stop=True),
                     r=[b_c], w=[b_pst[pi]])
            S.op("act", lambda e: e.activation(out=ptl[li][0:ns, 0:nq], in_=pst[pi][0:ns, 0:nq], func=AF.Exp,
                                               bias=bias_ap, scale=SCALE),
                 r=[b_pst[pi]] + rd, w=[b_ptl[li]])
            S.op("pe", lambda e: e.matmul(pso[acc_i][0:nq, 0:129], lhsT=ptl[li][0:ns, 0:nq], rhs=v_ap,
                                          start=first, stop=last),
                 r=[b_ptl[li]] + rd, w=[b_pso[acc_i]])

        def finish(acc_i, nq, out_ap):
            oi = cnt["osb"] % 2; cnt["osb"] += 1
            S.op("dve", lambda e: e.reciprocal(out=osb[oi][0:nq, 129:130], in_=pso[acc_i][0:nq, 128:129]),
                 r=[b_pso[acc_i]], w=[b_osb[oi]])
            S.op("dve", lambda e: e.tensor_scalar_mul(out=osb[oi][0:nq, 0:128], in0=pso[acc_i][0:nq, 0:128],
                                                      scalar1=osb[oi][0:nq, 129:130]),
                 r=[b_pso[acc_i], b_osb[oi]], w=[b_osb[oi]])
            S.dma("sp", out_ap, osb[oi][0:nq, 0:128], r=[b_osb[oi]])

        def cumsum(lf_ap, nt, D_t, O_t):
            S.op("pe", lambda e: e.matmul(psd[0:nt, 0:1], lhsT=lf_ap, rhs=ones[:, 0:1], start=True, stop=True),
                 r=[b_lf, b_c], w=[b_psd])
            S.op("dve", lambda e: e.tensor_copy(out=totc[0:nt, 0:1], in_=psd[0:nt, 0:1]), r=[b_psd], w=[b_totc])
            S.op("dve", lambda e: e.tensor_scalar_mul(out=tot[0:nt, :], in0=ones[0:nt, :], scalar1=totc[0:nt, 0:1]),
                 r=[b_totc, b_c], w=[b_tot])
            S.op("pe", lambda e: e.matmul(psd2[:, 0:nt], lhsT=tot[0:nt, :], rhs=sut[0:nt, 0:nt], start=True, stop=True),
                 r=[b_tot, b_c], w=[b_psd2])
            S.op("dve", lambda e: e.tensor_copy(out=O_t, in_=psd2[:, 0:nt]), r=[b_psd2], w=[b_Of])
            S.op("pe", lambda e: e.matmul(psd[:, 0:nt], lhsT=tri[:], rhs=lf_ap, start=True, stop=True),
                 r=[b_lf, b_c, b_tot], w=[b_psd])
            S.op("dve", lambda e: e.tensor_tensor(out=D_t, in0=psd[:, 0:nt], in1=O_t, op=ALU.add),
                 r=[b_psd, b_Of], w=[b_Dm])

        for b in range(B):
            S.dma("pool", kT[:], kT_p[b], w=[b_kT])
            S.dma("sp", qT[:], qT_p[b], w=[b_qT])
            S.dma("pool", vx[:, :, 0:128], v_p[b].rearrange("(n p) d -> p n d", p=128), w=[b_vx])
            cumsum(lf[:, b * NTS:(b + 1) * NTS], NTS, Dm[:, 0:NTS], Of[:, 0:NTS])
            for qi in range(NTS):
                S.op("dve", lambda e, qi=qi: e.tensor_scalar(out=bq[:, 0:qi + 1], in0=Dm[:, 0:qi + 1], scalar1=-1.0,
                                                             scalar2=Of[:, qi:qi + 1], op0=ALU.mult, op1=ALU.add),
                     r=[b_Dm, b_Of], w=[b_bq])
                ai = cnt["pso"] % 2; cnt["pso"] += 1
                for ki in range(qi + 1):
                    attend(kT[:, ki * 128:(ki + 1) * 128], 128, qT[:, qi * 128:(qi + 1) * 128], 128,
                           vx[:, ki, 0:129], bq[:, ki:ki + 1],
                           (idb[:], cnb[:]) if ki == qi else None, ai, ki == 0, ki == qi,
                           [b_kT, b_qT, b_vx, b_bq])
                finish(ai, 128, o_p[b, qi * 128:(qi + 1) * 128, :])
        NK = NPG + 1
        assert NPG <= 128
        I32 = mybir.dt.int32
        pti = S.sb("pti", [128, NPG], I32); b_pti = S.buf()
        ptf = S.sb("ptf", [128, NPG], F32); b_ptf = S.buf()
        idx = [S.sb(f"idx{i}", [128, NPG], I32) for i in range(2)]; b_idx = [S.buf() for _ in range(2)]
        pcol = [S.sb(f"pcol{i}", [128, 1], I32) for i in range(2)]; b_pcol = [S.buf() for _ in range(2)]
        lfP = [S.sb(f"lfP{i}", [128, 128], F32) for i in range(2)]; b_lfP = [S.buf() for _ in range(2)]
        lfS = S.sb("lfS", [128, NK], F32)
        Ds = S.sb("Ds", [128, NK], F32); Os = S.sb("Os", [128, NK], F32); bqs = S.sb("bqs", [128, NK], F32)
        gt_ = [S.sb(f"gth{i}", [128, 256], F32) for i in range(4)]; b_gt = [S.buf() for _ in range(4)]
        kb = [S.sb(f"kb{i}", [128, 128], BF16) for i in range(3)]; b_kb = [S.buf() for _ in range(3)]
        vb = [S.sb(f"vb{i}", [128, 132], BF16) for i in range(3)]; b_vb = [S.buf() for _ in range(3)]
        qs = S.sb("qs", [128, 128], BF16); b_qs = S.buf()
        knf = S.sb("knf", [128, 128], F32); knb = S.sb("knb", [128, 128], BF16); b_kn = S.buf()
        vn4 = [S.sb(f"vn4{i}", [4, 132], BF16) for i in range(2)]; b_vn4 = [S.buf() for _ in range(2)]
        S.dma("sp", qs[:], qT_s, w=[b_qs])
        S.dma("sp", knf[:], kTn, w=[b_kn])
        S.op("dve", lambda e: e.tensor_copy(out=knb[:], in_=knf[:]), r=[b_kn], w=[b_kn])
        for i in range(3):
            S.op("dve", lambda e, i=i: e.memset(vb[i][:, 128:129], 1.0), w=[b_vb[i]])
        for i in range(2):
            S.op("dve", lambda e, i=i: e.memset(vn4[i][:, 128:129], 1.0), w=[b_vn4[i]])
        ng = 0
        for b in range(DB):
            ii = b % 2
            S.dma("sp", pti[:], ptab[b:b + 1, :].partition_broadcast(128), w=[b_pti])
            S.op("dve", lambda e: e.tensor_copy(out=ptf[:], in_=pti[:]), r=[b_pti], w=[b_ptf])
            S.op("dve", lambda e: e.tensor_scalar(out=ptf[:], in0=ptf[:], scalar1=128.0, scalar2=iot[:, 0:1],
                                                  op0=ALU.mult, op1=ALU.add), r=[b_ptf, b_c], w=[b_ptf])
            S.op("dve", lambda e, ii=ii: e.tensor_copy(out=idx[ii][:], in_=ptf[:]), r=[b_ptf], w=[b_idx[ii]])
            S.dma("sp", pcol[ii][0:NPG, :], ptab[b:b + 1, :].rearrange("o n -> n o"), w=[b_pcol[ii]])
            S.op("pool", lambda e, ii=ii: e.indirect_dma_start(
                out=lfP[ii][0:NPG, :], out_offset=None, in_=lfpool,
                in_offset=bass.IndirectOffsetOnAxis(ap=pcol[ii][0:NPG, 0:1], axis=0)),
                 r=[b_pcol[ii]], w=[b_lfP[ii]], dma=True)
            S.op("pe", lambda e, ii=ii: e.transpose(out=psd2[:, 0:NPG], in_=lfP[ii][0:NPG, :], identity=idf[0:NPG, 0:NPG]),
                 r=[b_lfP[ii], b_c], w=[b_psd2])
            S.op("dve", lambda e: e.memset(lfS[:, NPG:NK], 0.0), w=[b_lf])
            S.op("dve", lambda e: e.tensor_copy(out=lfS[:, 0:NPG], in_=psd2[:, 0:NPG]), r=[b_psd2], w=[b_lf])
            S.dma("sp", lfS[0:4, NPG:NK], lfn[b:b + 1, :].rearrange("o t -> t o"), w=[b_lf])
            cumsum(lfS[:, 0:NK], NK, Ds[:, 0:NK], Os[:, 0:NK])
            S.op("dve", lambda e: e.tensor_scalar(out=bqs[:, 0:NK], in0=Ds[:, 0:NK], scalar1=-1.0,
                                                  scalar2=Os[:, NPG:NK], op0=ALU.mult, op1=ALU.add),
                 r=[b_Dm, b_Of], w=[b_bq])
            vi = b % 2
            S.dma("pool", vn4[vi][0:4, 0:128], vn[b * 4:(b + 1) * 4, :], w=[b_vn4[vi]])
            ai = cnt["pso"] % 2; cnt["pso"] += 1
            qcol = qs[:, b * 4:(b + 1) * 4]
            for j in range(NPG):
                gi = ng % 4; ki_ = ng % 3; ng += 1
                S.op("pool", lambda e, gi=gi, ii=ii, j=j: e.indirect_dma_start(
                    out=gt_[gi][:], out_offset=None, in_=kvpool,
                    in_offset=bass.IndirectOffsetOnAxis(ap=idx[ii][:, j:j + 1], axis=0)),
                     r=[b_idx[ii]], w=[b_gt[gi]], dma=True)
                S.op("act", lambda e, gi=gi, ki_=ki_: e.copy(out=kb[ki_][:], in_=gt_[gi][:, 0:128]),
                     r=[b_gt[gi]], w=[b_kb[ki_]])
                S.op("dve", lambda e, gi=gi, ki_=ki_: e.tensor_copy(out=vb[ki_][:, 0:128], in_=gt_[gi][:, 128:256]),
                     r=[b_gt[gi]], w=[b_vb[ki_]])
                attend(kb[ki_][:], 128, qcol, 4, vb[ki_][:, 0:129], bqs[:, j:j + 1], None, ai, j == 0, False,
                       [b_kb[ki_], b_vb[ki_], b_qs, b_bq])
            attend(knb[:, b * 4:(b + 1) * 4], 4, qcol, 4, vn4[vi][0:4, 0:129], bqs[0:4, NPG:NK],
                   (idb[0:4, 0:4], n4b[0:4, 0:4]), ai, False, True, [b_kn, b_vn4[vi], b_qs, b_bq])
            finish(ai, 4, o_s[b * 4:(b + 1) * 4, :])
        S.emit()
    return nc


def _fox_consts():
    f32 = np.float32
    i = np.arange(128)
    return dict(c_tri=(i[:, None] <= i[None, :]).astype(f32), c_sut=(i[:, None] < i[None, :]).astype(f32),
                c_cneg=np.where(i[:, None] > i[None, :], NEG, 0.0).astype(f32), c_id=np.eye(128, dtype=f32),
                c_iota=i.astype(f32)[:, None],
                c_n4=np.where(np.arange(4)[:, None] > np.arange(4)[None, :], NEG, 0.0).astype(f32),
                c_tri4=(np.arange(4)[:, None] <= np.arange(4)[None, :]).astype(f32))


def launch_fox(L1, cache_fox_kv, cache_fox_logf, page_table):
    f32 = np.float32
    B, SEQ, DB, DS = L1["B"], L1["SEQ"], L1["DB"], L1["DS"]
    NTS = SEQ // 128
    ckv = np.asarray(cache_fox_kv)[0]
    clf = np.asarray(cache_fox_logf)[0]
    NPOOL = ckv.shape[0]
    pt = np.ascontiguousarray(np.asarray(page_table, np.int32))
    NPG = pt.shape[1]
    nc = build_fox(SEQ, DB, NPG, NPOOL)
    consts = _fox_consts()
    fkv_p = L1["fkv_p"].reshape(B, SEQ, 2, 8, 128)
    fkv_s = L1["fkv_s"].reshape(DB * DS, 2, 8, 128)
    q_p = L1["q_p"].reshape(B, SEQ, 16, 128)
    q_s = L1["q_s"].reshape(DB * DS, 16, 128)
    in_maps = []
    for c in range(8):
        m = dict(consts)
        m["qT_p"] = np.ascontiguousarray(q_p[:, :, 8 + c, :].transpose(0, 2, 1))
        m["kT_p"] = np.ascontiguousarray(fkv_p[:, :, 0, c, :].transpose(0, 2, 1))
        m["v_p"] = np.ascontiguousarray(fkv_p[:, :, 1, c, :])
        m["lfT_p"] = np.ascontiguousarray(L1["lf_p"][:, :, c].reshape(B * NTS, 128).T)
        m["qT_s"] = np.ascontiguousarray(q_s[:, 8 + c, :].T)
        m["kTn"] = np.ascontiguousarray(fkv_s[:, 0, c, :].T)
        m["vn"] = np.ascontiguousarray(fkv_s[:, 1, c, :])
        m["lfn"] = np.ascontiguousarray(L1["lf_s"][:, c].reshape(DB, DS))
        m["kvpool"] = np.ascontiguousarray(np.concatenate(
            [ckv[:, :, 0, c, :].transpose(0, 2, 1), ckv[:, :, 1, c, :]], axis=2)).reshape(NPOOL * 128, 256)
        m["lfpool"] = np.ascontiguousarray(clf[:, :, c])
        m["ptab"] = pt
        in_maps.append(m)
    res = run_bass_kernel_spmd(nc, in_maps, core_ids=list(range(8))).results
    o_p = np.stack([res[c]["o_p"] for c in range(8)], axis=2)
    o_s = np.stack([res[c]["o_s"] for c in range(8)], axis=1)
    return o_p, o_s


def build_nsa_p(SEQ):
    B = 2
    NTS = SEQ // 128
    NB = SEQ // 32
    NSEL = SEQ // 64
    assert NB <= 128 and NSEL <= 128
    nc = bass.Bass("TRN2", target_bir_lowering=False)
    dt = nc.dram_tensor
    qg = dt("qg", [B, 4, 128, SEQ], BF16, kind="ExternalInput").ap()
    qoT = dt("qoT", [B, 128, SEQ], BF16, kind="ExternalInput").ap()
    c_cbo = dt("c_cbo", [128, NTS], F32, kind="ExternalInput").ap()
    kcin = dt("kcin", [B, 2, 32, 128, NB], F32, kind="ExternalInput").ap()
    pe_in = dt("pe_in", [2, 128, 32], F32, kind="ExternalInput").ap()
    cw = dt("cw", [2, 128, 32, 128], F32, kind="ExternalInput").ap()
    ksT = dt("ksT", [B, 128, SEQ], F32, kind="ExternalInput").ap()
    kwT = dt("kwT", [B, 128, SEQ], F32, kind="ExternalInput").ap()
    vs = dt("vs", [B, SEQ, 128], F32, kind="ExternalInput").ap()
    vw = dt("vw", [B, SEQ, 128], F32, kind="ExternalInput").ap()
    gts = dt("gts", [B, SEQ, 3], F32, kind="ExternalInput").ap()
    c_id = dt("c_id", [128, 128], F32, kind="ExternalInput").ap()
    c_cneg = dt("c_cneg", [128, 128], F32, kind="ExternalInput").ap()
    c_wneg = dt("c_wneg", [128, 128], F32, kind="ExternalInput").ap()
    c_cm = dt("c_cm", [128, SEQ], F32, kind="ExternalInput").ap()
    c_cb = dt("c_cb", [128, NTS * 4], F32, kind="ExternalInput").ap()
    c_ab = dt("c_ab", [128, NTS], F32, kind="ExternalInput").ap()
    c_A = dt("c_A", [128, NSEL + 1], F32, kind="ExternalInput").ap()
    c_TA = dt("c_TA", [128, NTS * NSEL], F32, kind="ExternalInput").ap()
    c_TB = dt("c_TB", [128, NTS * NSEL], F32, kind="ExternalInput").ap()
    c_E = dt("c_E", [128, NTS * 128], F32, kind="ExternalInput").ap()
    o_n = dt("o_n", [B, SEQ, 128], F32, kind="ExternalOutput").ap()
    o_sel = dt("o_sel", [B, SEQ, NSEL], F32, kind="ExternalOutput").ap()

    with ExitStack() as st:
        S = Sched(nc, st)
        b_c = S.buf("consts")

        def cst(name, ap, shape, to_bf=False):
            t = S.sb(name, shape, F32)
            S.dma("sp", t[:], ap, w=[b_c])
            if to_bf:
                tb = S.sb(name + "b", shape, BF16)
                S.op("dve", lambda e: e.tensor_copy(out=tb[:], in_=t[:]), r=[b_c], w=[b_c])
                return tb
            return t
        idf = cst("idf", c_id, [128, 128])
        idb = S.sb("idb", [128, 128], BF16)
        S.op("dve", lambda e: e.tensor_copy(out=idb[:], in_=idf[:]), r=[b_c], w=[b_c])
        cnb = cst("cn", c_cneg, [128, 128], True)
        wnb = cst("wn", c_wneg, [128, 128], True)
        cb = cst("cb", c_cb, [128, NTS * 4])
        ab = cst("ab", c_ab, [128, NTS])
        Am = cst("Am", c_A, [128, NSEL + 1])
        TA = cst("TA", c_TA, [128, NTS * NSEL])
        TB = cst("TB", c_TB, [128, NTS * NSEL])
        cbo = cst("cbo", c_cbo, [128, NTS])
        cmb = S.sb("cmb", [128, SEQ], BF16)
        S.dma("pool", cmb[:], c_cm, w=[b_c])
        Eb = S.sb("Eb", [128, NTS * 128], BF16)
        S.dma("pool", Eb[:], c_E, w=[b_c])
        cwb = S.sb("cwb", [128, 2, 32, 128], BF16)
        for kv in range(2):
            S.dma("pool", cwb[:, kv], cw[kv], w=[b_c])
        peT = S.sb("peT", [128, 2, 32], F32)
        for kv in range(2):
            S.dma("sp", peT[:, kv, :], pe_in[kv], w=[b_c])

        qT = S.sb("qT", [128, 4, SEQ], BF16); b_q = S.buf()
        qo = S.sb("qo", [128, SEQ], BF16); b_qo = S.buf()
        ks = S.sb("ks", [128, SEQ], BF16); kw = S.sb("kw", [128, SEQ], BF16); b_k = S.buf()
        vsx = S.sb("vsx", [128, NTS, 132], BF16); vwx = S.sb("vwx", [128, NTS, 132], BF16); b_v = S.buf()
        kin = [S.sb(f"kin{i}", [128, NB], F32) for i in range(2)]; b_kin = [S.buf() for _ in range(2)]
        kib = [S.sb(f"kib{i}", [128, NB], BF16) for i in range(2)]; b_kib = [S.buf() for _ in range(2)]
        kcb = S.sb("kcb", [128, 128], BF16); b_kcb = S.buf()
        kcT = S.sb("kcT", [128, 128], BF16); b_kcT = S.buf()
        vcx = S.sb("vcx", [128, 132], BF16); b_vcx = S.buf()
        gt = S.sb("gt", [128, NTS, 3], F32); b_gt = S.buf()
        S.op("dve", lambda e: e.memset(vsx[:, :, 128:129], 1.0), w=[b_v])
        S.op("dve", lambda e: e.memset(vwx[:, :, 128:129], 1.0), w=[b_v])
        S.op("dve", lambda e: e.memset(vcx[:, 128:129], 1.0), w=[b_vcx])

        pst = [S.ps(f"pst{i}", [128, 128], F32) for i in range(2)]; b_pst = [S.buf() for _ in range(2)]
        pO = {k: S.ps("pO" + k, [128, 132], F32) for k in "csw"}; b_pO = {k: S.buf() for k in "csw"}
        psc = S.ps("psc", [128, NSEL + 1], F32); b_psc = S.buf()
        pmi = S.ps("pmi", [128, 128], F32); b_pmi = S.buf()
        pmb = S.ps("pmb", [128, 128], BF16); b_pmb = S.buf()
        ptl = [S.sb(f"ptl{i}", [128, 128], BF16) for i in range(3)]; b_ptl = [S.buf() for _ in range(3)]
        ptf = [S.sb(f"ptf{i}", [128, 128], F32) for i in range(2)]; b_ptf = [S.buf() for _ in range(2)]
        sc = S.sb("sc", [128, NSEL], F32); b_sc = S.buf()
        sw = S.sb("sw", [128, NSEL], F32); b_sw = S.buf()
        sw2 = S.sb("sw2", [128, NSEL], F32); b_sw2 = S.buf()
        mx = S.sb("mx", [128, 16], F32); b_mx = S.buf()
        rr = S.sb("rr", [128, 8], F32); b_rr = S.buf()
        selb = S.sb("selb", [128, NSEL], BF16); b_selb = S.buf()
        self_ = S.sb("self", [128, NSEL], F32); b_self = S.buf()
        negT = S.sb("negT", [128, 128], BF16); b_negT = S.buf()
        oc = S.sb("oc", [128, 132], F32); b_oc = S.buf()
        cf = S.sb("cf", [128, 8], F32); b_cf = S.buf()
        cnt = dict(pst=0, ptl=0, ptf=0)

        def unit(kt_ap, ns, q_ap, v_ap, bias_ap, masks, okey, first, last, rd):
            pi = cnt["pst"] % 2; cnt["pst"] += 1
            li = cnt["ptl"] % 3; cnt["ptl"] += 1
            nm = len(masks)
            S.op("pe", lambda e: e.matmul(pst[pi][0:ns, :], lhsT=kt_ap, rhs=q_ap, start=True, stop=(nm == 0)),
                 r=rd, w=[b_pst[pi]])
            for mi, (ml, mr, mrd) in enumerate(masks):
                S.op("pe", lambda e, ml=ml, mr=mr, mi=mi: e.matmul(pst[pi][0:ns, :], lhsT=ml, rhs=mr, start=False,
                                                                    stop=(mi == nm - 1)), r=[b_c] + mrd, w=[b_pst[pi]])
            S.op("act", lambda e: e.activation(out=ptl[li][0:ns, :], in_=pst[pi][0:ns, :], func=AF.Exp,
                                               bias=bias_ap, scale=SCALE), r=[b_pst[pi], b_c], w=[b_ptl[li]])
            S.op("pe", lambda e: e.matmul(pO[okey][:, 0:129], lhsT=ptl[li][0:ns, :], rhs=v_ap, start=first, stop=last),
                 r=[b_ptl[li]] + rd, w=[b_pO[okey]])

        for b in range(B):
            S.dma("sp", qT[:], qg[b].rearrange("h d s -> d h s"), w=[b_q])
            S.dma("sp", qo[:], qoT[b], w=[b_qo])
            S.dma("pool", ks[:], ksT[b], w=[b_k])
            S.dma("pool", kw[:], kwT[b], w=[b_k])
            S.dma("pool", vsx[:, :, 0:128], vs[b].rearrange("(n p) d -> p n d", p=128), w=[b_v])
            S.dma("pool", vwx[:, :, 0:128], vw[b].rearrange("(n p) d -> p n d", p=128), w=[b_v])
            S.dma("sp", gt[:], gts[b].rearrange("(n p) g -> p n g", p=128), w=[b_gt])
            for kv in range(2):
                for c in range(32):
                    i2 = c % 2
                    S.dma("sp", kin[i2][:], kcin[b, kv, c], w=[b_kin[i2]])
                    S.op("dve", lambda e, i2=i2, kv=kv, c=c: e.tensor_scalar_add(out=kib[i2][:], in0=kin[i2][:],
                                                                                 scalar1=peT[:, kv, c:c + 1]),
                         r=[b_kin[i2], b_c], w=[b_kib[i2]])
                    S.op("pe", lambda e, i2=i2, kv=kv, c=c: e.matmul(pmi[0:NB, :], lhsT=kib[i2][:], rhs=cwb[:, kv, c, :],
                                                                     start=(c == 0), stop=(c == 31)),
                         r=[b_kib[i2], b_c], w=[b_pmi])
                if kv == 0:
                    S.op("dve", lambda e: e.tensor_copy(out=kcb[0:NB, :], in_=pmi[0:NB, :]), r=[b_pmi], w=[b_kcb])
                    S.op("pe", lambda e: e.transpose(out=pmb[:, 0:NB], in_=kcb[0:NB, :], identity=idb[0:NB, 0:NB]),
                         r=[b_kcb, b_c], w=[b_pmb])
                    S.op("dve", lambda e: e.tensor_copy(out=kcT[:, 0:NB], in_=pmb[:, 0:NB]), r=[b_pmb], w=[b_kcT])
                else:
                    S.op("dve", lambda e: e.tensor_copy(out=vcx[0:NB, 0:128], in_=pmi[0:NB, :]), r=[b_pmi], w=[b_vcx])

            for qi in range(NTS):
                qs_ = slice(qi * 128, (qi + 1) * 128)
                for hh in range(4):
                    pi = cnt["pst"] % 2; cnt["pst"] += 1
                    fi = cnt["ptf"] % 2; cnt["ptf"] += 1
                    q_ap = qT[:, hh, qs_]
                    cm_ap = cmb[0:NB, qs_]
                    cb_ap = cb[0:NB, qi * 4 + hh:qi * 4 + hh + 1]
                    S.op("pe", lambda e, pi=pi, q_ap=q_ap: e.matmul(pst[pi][0:NB, :], lhsT=kcT[:, 0:NB], rhs=q_ap,
                                                                    start=True, stop=False), r=[b_kcT, b_q], w=[b_pst[pi]])
                    S.op("pe", lambda e, pi=pi, cm_ap=cm_ap: e.matmul(pst[pi][0:NB, :], lhsT=idb[0:NB, 0:NB], rhs=cm_ap,
                                                                      start=False, stop=True), r=[b_c], w=[b_pst[pi]])
                    S.op("act", lambda e, pi=pi, fi=fi, cb_ap=cb_ap: e.activation(
                        out=ptf[fi][0:NB, :], in_=pst[pi][0:NB, :], func=AF.Exp,
                        bias=cb_ap, scale=SCALE), r=[b_pst[pi], b_c], w=[b_ptf[fi]])
                    S.op("pe", lambda e, fi=fi: e.matmul(psc[:, :], lhsT=ptf[fi][0:NB, :], rhs=Am[0:NB, :],
                                                         start=True, stop=True), r=[b_ptf[fi], b_c], w=[b_psc])
                    S.op("dve", lambda e: e.tensor_scalar_max(out=rr[:, 0:1], in0=psc[:, NSEL:NSEL + 1], scalar1=1e-30),
                         r=[b_psc], w=[b_rr])
                    S.op("dve", lambda e: e.reciprocal(out=rr[:, 1:2], in_=rr[:, 0:1]), r=[b_rr], w=[b_rr])
                    if hh == 0:
                        S.op("dve", lambda e: e.tensor_scalar_mul(out=sc[:], in0=psc[:, 0:NSEL], scalar1=rr[:, 1:2]),
                             r=[b_psc, b_rr], w=[b_sc])
                    else:
                        S.op("dve", lambda e: e.scalar_tensor_tensor(out=sc[:], in0=psc[:, 0:NSEL], scalar=rr[:, 1:2],
                                                                     in1=sc[:], op0=ALU.mult, op1=ALU.add),
                             r=[b_psc, b_rr, b_sc], w=[b_sc])
                unit(kcT[:, 0:NB], NB, qo[:, qs_], vcx[0:NB, 0:129], cbo[0:NB, qi:qi + 1],
                     [(idb[0:NB, 0:NB], cmb[0:NB, qs_], [])], "c", True, True, [b_kcT, b_qo, b_vcx])
                so = slice(qi * NSEL, (qi + 1) * NSEL)
                S.op("dve", lambda e, so=so: e.tensor_tensor(out=sw[:], in0=sc[:], in1=TA[:, so], op=ALU.mult),
                     r=[b_sc, b_c], w=[b_sw])
                S.op("dve", lambda e, so=so: e.tensor_tensor(out=sw[:], in0=sw[:], in1=TB[:, so], op=ALU.add),
                     r=[b_sw, b_c], w=[b_sw])
                S.op("dve", lambda e: e.max(out=mx[:, 0:8], in_=sw[:]), r=[b_sw], w=[b_mx])
                S.op("dve", lambda e: e.match_replace(out=sw2[:], in_to_replace=mx[:, 0:8], in_values=sw[:], imm_value=-2.0),
                     r=[b_sw, b_mx], w=[b_sw2])
                S.op("dve", lambda e: e.max(out=mx[:, 8:16], in_=sw2[:]), r=[b_sw2], w=[b_mx])
                S.op("dve", lambda e: e.tensor_scalar(out=self_[:], in0=sw[:], scalar1=mx[:, 15:16], scalar2=None, op0=ALU.is_ge),
                     r=[b_sw, b_mx], w=[b_self])
                S.op("dve", lambda e: e.tensor_scalar(out=sw2[:], in0=sw[:], scalar1=0.0, scalar2=None, op0=ALU.is_ge),
                     r=[b_sw], w=[b_sw2])
                S.op("dve", lambda e: e.tensor_tensor(out=self_[:], in0=self_[:], in1=sw2[:], op=ALU.mult),
                     r=[b_self, b_sw2], w=[b_self])
                S.dma("sp", o_sel[b, qs_, :], self_[:], r=[b_self])
                S.op("dve", lambda e: e.tensor_copy(out=selb[:], in_=self_[:]), r=[b_self], w=[b_selb])
                S.op("pe", lambda e: e.transpose(out=pmb[0:NSEL, :], in_=selb[:], identity=idb[:]), r=[b_selb, b_c], w=[b_pmb])
                S.op("dve", lambda e: e.tensor_scalar(out=negT[0:NSEL, :], in0=pmb[0:NSEL, :], scalar1=-1.0, scalar2=-NEG,
                                                      op0=ALU.add, op1=ALU.mult), r=[b_pmb], w=[b_negT])
                for ki in range(qi + 1):
                    masks = [(Eb[0:NSEL, ki * 128:(ki + 1) * 128], negT[0:NSEL, :], [b_negT])]
                    if ki == qi:
                        masks.append((idb[:], cnb[:], []))
                    unit(ks[:, ki * 128:(ki + 1) * 128], 128, qo[:, qs_], vsx[:, ki, 0:129],
                         ab[:, ki - qi + NTS - 1:ki - qi + NTS], masks, "s", ki == 0, ki == qi, [b_k, b_qo, b_v])
                k0 = max(0, qi - 4)
                for ki in range(k0, qi + 1):
                    masks = []
                    if ki == qi - 4:
                        masks.append((idb[:], wnb[:], []))
                    if ki == qi:
                        masks.append((idb[:], cnb[:], []))
                    unit(kw[:, ki * 128:(ki + 1) * 128], 128, qo[:, qs_], vwx[:, ki, 0:129],
                         ab[:, ki - qi + NTS - 1:ki - qi + NTS], masks, "w", ki == k0, ki == qi, [b_k, b_qo, b_v])
                for bi, key in enumerate("csw"):
                    S.op("dve", lambda e, key=key, bi=bi: e.tensor_scalar_max(out=cf[:, bi:bi + 1], in0=pO[key][:, 128:129],
                                                                             scalar1=1e-30), r=[b_pO[key]], w=[b_cf])
                S.op("dve", lambda e: e.reciprocal(out=cf[:, 3:6], in_=cf[:, 0:3]), r=[b_cf], w=[b_cf])
                g_ap = gt[:, qi, :]
                S.op("dve", lambda e, g_ap=g_ap: e.tensor_tensor(out=cf[:, 3:6], in0=cf[:, 3:6], in1=g_ap, op=ALU.mult),
                     r=[b_cf, b_gt], w=[b_cf])
                S.op("dve", lambda e: e.tensor_scalar_mul(out=oc[:, 0:128], in0=pO["c"][:, 0:128], scalar1=cf[:, 3:4]),
                     r=[b_pO["c"], b_cf], w=[b_oc])
                for bi, key in ((1, "s"), (2, "w")):
                    S.op("dve", lambda e, key=key, bi=bi: e.scalar_tensor_tensor(
                        out=oc[:, 0:128], in0=pO[key][:, 0:128], scalar=cf[:, 3 + bi:4 + bi], in1=oc[:, 0:128],
                        op0=ALU.mult, op1=ALU.add), r=[b_pO[key], b_cf, b_oc], w=[b_oc])
                S.dma("sp", o_n[b, qs_, :], oc[:, 0:128], r=[b_oc])
        S.emit()
    return nc


def _nsa_consts(SEQ, head):
    f32 = np.float32
    NTS, NB, NSEL = SEQ // 128, SEQ // 32, SEQ // 64
    p = np.arange(128)
    slopes = np.exp2(-8.0 * np.arange(1, 9, dtype=np.float64) / 8)
    g = head // 4
    c_end = 32 * (p + 1) - 1
    t = np.arange(SEQ)
    cm = np.where(c_end[:, None] > t[None, :], NEG, 0.0).astype(f32)
    cb = np.zeros((128, NTS, 4), f32)
    for qi in range(NTS):
        for hh in range(4):
            cb[:, qi, hh] = slopes[4 * g + hh] * (c_end - 128 * qi)
    cb = np.clip(cb, -1e4, 1e4)
    cbo = np.ascontiguousarray(cb[:, :, head % 4])
    ab = np.zeros((128, NTS), f32)
    for d in range(-(NTS - 1), 1):
        ab[:, d + NTS - 1] = slopes[head] * (128 * d + p)
    A = np.zeros((128, NSEL + 1), f32)
    for n in range(min(128, NB)):
        A[n, n // 2] = 1.0
    A[:, NSEL] = 1.0
    TA = np.zeros((128, NTS, NSEL), f32); TB = np.zeros((128, NTS, NSEL), f32)
    blk = np.arange(NSEL)
    for qi in range(NTS):
        tq = 128 * qi + p
        cur = tq // 64
        vis = blk[None, :] <= cur[:, None]
        f0 = blk[None, :] == 0
        f1 = blk[None, :] == cur[:, None]
        f2 = blk[None, :] == (cur[:, None] - 1)
        forced = f0 | f1 | f2
        TA[:, qi] = (vis & ~forced).astype(f32)
        fv = np.where(f1, 10002.0, np.where(f2, 10001.0, 10000.0))
        TB[:, qi] = np.where(vis, np.where(forced, fv, 0.0), -1.0)
    E = np.zeros((128, NTS, 128), f32)
    for j in range(NTS):
        E[2 * j, j, 0:64] = 1.0
        E[2 * j + 1, j, 64:128] = 1.0
    i = np.arange(128)
    return dict(c_id=np.eye(128, dtype=f32), c_cneg=np.where(i[:, None] > i[None, :], NEG, 0.0).astype(f32),
                c_wneg=np.where(i[:, None] <= i[None, :], NEG, 0.0).astype(f32), c_cm=cm,
                c_cb=cb.reshape(128, NTS * 4), c_cbo=cbo, c_ab=ab, c_A=A, c_TA=TA.reshape(128, -1),
                c_TB=TB.reshape(128, -1), c_E=E.reshape(128, -1))


def launch_nsa_p(L1, cmp_pe, cmp_w):
    f32 = np.float32
    B, SEQ = L1["B"], L1["SEQ"]
    NB = SEQ // 32
    nc = build_nsa_p(SEQ)
    nkv = L1["nkv_p"].reshape(B, SEQ, 4, 2, 128)
    win = L1["win_p"].reshape(B, SEQ, 2, 2, 128)
    q = L1["q_p"].reshape(B, SEQ, 16, 128)
    gates = L1["g_p"].reshape(B, SEQ, 8, 3)
    pe = np.asarray(cmp_pe, f32)[0]
    cwt = np.asarray(cmp_w, f32)[0]
    in_maps = []
    for c in range(8):
        g = c // 4
        m = _nsa_consts(SEQ, c)
        m["qg"] = np.ascontiguousarray(q[:, :, 4 * g:4 * g + 4, :].transpose(0, 2, 3, 1))
        m["qoT"] = np.ascontiguousarray(q[:, :, c, :].transpose(0, 2, 1))
        xc = nkv[:, :, 0:2, g, :].reshape(B, NB, 32, 2, 128)
        m["kcin"] = np.ascontiguousarray(xc.transpose(0, 3, 2, 4, 1))
        m["pe_in"] = np.ascontiguousarray(pe.transpose(0, 2, 1))
        m["cw"] = np.ascontiguousarray(cwt.transpose(0, 2, 1, 3))
        m["ksT"] = np.ascontiguousarray(nkv[:, :, 2, g, :].transpose(0, 2, 1))
        m["vs"] = np.ascontiguousarray(nkv[:, :, 3, g, :])
        m["kwT"] = np.ascontiguousarray(win[:, :, 0, g, :].transpose(0, 2, 1))
        m["vw"] = np.ascontiguousarray(win[:, :, 1, g, :])
        m["gts"] = np.ascontiguousarray(gates[:, :, c, :])
        in_maps.append(m)
    res = run_bass_kernel_spmd(nc, in_maps, core_ids=list(range(8))).results
    o_n = np.stack([res[c]["o_n"] for c in range(8)], axis=2)
    sel = np.stack([res[c]["o_sel"] for c in range(8)], axis=2)
    return o_n, sel


def build_nsa_s(DB, NPG, NPOOL):
    P = NPG * 128
    NB = P // 32
    NBT = (NB + 127) // 128
    NSEL = NPG * 2 + 1
    assert NPG * 2 <= 128
    I32 = mybir.dt.int32
    nc = bass.Bass("TRN2", target_bir_lowering=False)
    dt = nc.dram_tensor
    qg = dt("qg", [128, 4, 128], BF16, kind="ExternalInput").ap()
    qoT = dt("qoT", [128, 128], BF16, kind="ExternalInput").ap()
    pool = dt("pool", [NPOOL * 128, 512], F32, kind="ExternalInput").ap()
    ptab = dt("ptab", [DB, NPG], I32, kind="ExternalInput").ap()
    pe_in = dt("pe_in", [2, 128, 32], F32, kind="ExternalInput").ap()
    cw = dt("cw", [2, 128, 32, 128], F32, kind="ExternalInput").ap()
    ksn = dt("ksn", [128, 128], F32, kind="ExternalInput").ap()
    vsn = dt("vsn", [128, 128], F32, kind="ExternalInput").ap()
    kwn = dt("kwn", [128, 128], F32, kind="ExternalInput").ap()
    vwn = dt("vwn", [128, 128], F32, kind="ExternalInput").ap()
    kwT = dt("kwT", [DB, 128, W_BUF], F32, kind="ExternalInput").ap()
    vw = dt("vw", [DB, W_BUF, 128], F32, kind="ExternalInput").ap()
    gts = dt("gts", [128, 3], F32, kind="ExternalInput").ap()
    c_id = dt("c_id", [128, 128], F32, kind="ExternalInput").ap()
    c_iota = dt("c_iota", [128, 1], F32, kind="ExternalInput").ap()
    c_n4 = dt("c_n4", [4, 4], F32, kind="ExternalInput").ap()
    c_w4 = dt("c_w4", [128, 4], F32, kind="ExternalInput").ap()
    c_cb = dt("c_cb", [128, NBT * 4], F32, kind="ExternalInput").ap()
    c_cbo = dt("c_cbo", [128, NBT], F32, kind="ExternalInput").ap()
    c_ab = dt("c_ab", [128, NPG + 1], F32, kind="ExternalInput").ap()
    c_aw = dt("c_aw", [128, 5], F32, kind="ExternalInput").ap()
    c_A = dt("c_A", [128, NBT * (NSEL + 1)], F32, kind="ExternalInput").ap()
    c_TA = dt("c_TA", [4, NSEL], F32, kind="ExternalInput").ap()
    c_TB = dt("c_TB", [4, NSEL], F32, kind="ExternalInput").ap()
    c_E = dt("c_E", [128, NPG * 128], F32, kind="ExternalInput").ap()
    o_n = dt("o_n", [128, 128], F32, kind="ExternalOutput").ap()
    o_sel = dt("o_sel", [128, NSEL], F32, kind="ExternalOutput").ap()

    with ExitStack() as st:
        S = Sched(nc, st)
        b_c = S.buf("consts")

        def cst(name, ap, shape, to_bf=False, eng="sp"):
            if to_bf:
                tb = S.sb(name + "b", shape, BF16)
                S.dma("pool", tb[:], ap, w=[b_c])
                return tb
            t = S.sb(name, shape, F32)
            S.dma(eng, t[:], ap, w=[b_c])
            return t
        idf = cst("idf", c_id, [128, 128]); idb = cst("id", c_id, [128, 128], True)
        iot = cst("iot", c_iota, [128, 1])
        n4b = cst("n4", c_n4, [4, 4], True); w4b = cst("w4", c_w4, [128, 4], True)
        cb = cst("cb", c_cb, [128, NBT * 4]); cbo = cst("cbo", c_cbo, [128, NBT])
        ab = cst("ab", c_ab, [128, NPG + 1]); aw = cst("aw", c_aw, [128, 5])
        Am = cst("Am", c_A, [128, NBT * (NSEL + 1)])
        TA = cst("TA", c_TA, [4, NSEL]); TB = cst("TB", c_TB, [4, NSEL])
        Eb = cst("E", c_E, [128, NPG * 128], True)
        cwb = S.sb("cwb", [128, 2, 32, 128], BF16)
        for kv in range(2):
            S.dma("pool", cwb[:, kv], cw[kv], w=[b_c])
        peb = S.sb("peb", [128, 2, 32], BF16)
        for kv in range(2):
            S.dma("pool", peb[:, kv, :], pe_in[kv], w=[b_c])
        onesb = S.sb("onesb", [1, 128], BF16)
        S.op("dve", lambda e: e.memset(onesb[:], 1.0), w=[b_c])
        qT = cst("qT", qg, [128, 4, 128], True)
        qo = cst("qo", qoT, [128, 128], True)
        ksnb = cst("ksn", ksn, [128, 128], True); kwnb = cst("kwn", kwn, [128, 128], True)
        gt = cst("gt", gts, [128, 3])

        pst = [S.ps(f"pst{i}", [128, 16], F32) for i in range(2)]; b_pst = [S.buf() for _ in range(2)]
        pO = {k: S.ps("pO" + k, [4, 132], F32) for k in "csw"}; b_pO = {k: S.buf() for k in "csw"}
        psc = S.ps("psc", [4, NSEL + 1], F32); b_psc = S.buf()
        pmi = S.ps("pmi", [128, 128], F32); b_pmi = S.buf()
        pmb = S.ps("pmb", [128, 128], BF16); b_pmb = S.buf()
        cnt = dict(pst=0, ptl=0, ptf=0, g=0)
        ptl = [S.sb(f"ptl{i}", [128, 4], BF16) for i in range(3)]; b_ptl = [S.buf() for _ in range(3)]
        ptf = [S.sb(f"ptf{i}", [128, 4], F32) for i in range(2)]; b_ptf = [S.buf() for _ in range(2)]

        pw = S.sb("pw", [1, 2, 128], BF16)
        for kv in range(2):
            for c in range(32):
                S.op("pe", lambda e, kv=kv, c=c: e.matmul(pmi[0:1, :], lhsT=peb[:, kv, c:c + 1], rhs=cwb[:, kv, c, :],
                                                          start=(c == 0), stop=(c == 31)), r=[b_c], w=[b_pmi])
            S.op("dve", lambda e, kv=kv: e.tensor_copy(out=pw[0:1, kv, :], in_=pmi[0:1, :]), r=[b_pmi], w=[b_c])

        pti = S.sb("pti", [128, NPG], I32); b_pti = S.buf()
        ptf32 = S.sb("ptf32", [128, NPG], F32); b_ptf32 = S.buf()
        idx = [S.sb(f"idx{i}", [128, NPG], I32) for i in range(2)]; b_idx = [S.buf() for _ in range(2)]
        gth = [S.sb(f"gth{i}", [128, 512], F32) for i in range(3)]; b_gth = [S.buf() for _ in range(3)]
        XT = S.sb("XT", [128, 2, P], BF16); b_XT = S.buf()
        KS = S.sb("KS", [128, P], BF16); b_KS = S.buf()
        VS = S.sb("VS", [128, NPG, 132], BF16); b_VS = S.buf()
        kcb = S.sb("kcb", [128, NBT, 128], BF16); b_kcb = S.buf()
        kcT = S.sb("kcT", [128, NBT * 128], BF16); b_kcT = S.buf()
        vcx = S.sb("vcx", [128, NBT, 132], BF16); b_vcx = S.buf()
        kwb = S.sb("kwb", [128, W_BUF], BF16); b_kwb = S.buf()
        vwx = S.sb("vwx", [128, W_BUF // 128, 132], BF16); b_vwx = S.buf()
        vn4 = S.sb("vn4", [4, 2, 132], BF16); b_vn4 = S.buf()
        sc = S.sb("sc", [4, NSEL], F32); b_sc = S.buf()
        sw = S.sb("sw", [4, NSEL], F32); b_sw = S.buf()
        sw2 = S.sb("sw2", [4, NSEL], F32); b_sw2 = S.buf()
        mx = S.sb("mx", [4, 16], F32); b_mx = S.buf()
        rr = S.sb("rr", [4, 8], F32); b_rr = S.buf()
        self_ = S.sb("self", [4, NSEL], F32); b_self = S.buf()
        selb = S.sb("selb", [4, 128], BF16); b_selb = S.buf()
        negT = S.sb("negT", [128, 4], BF16); b_negT = S.buf()
        oc = S.sb("oc", [4, 132], F32); b_oc = S.buf()
        cf = S.sb("cf", [4, 8], F32); b_cf = S.buf()
        gt4 = S.sb("gt4", [4, 3], F32); b_gt4 = S.buf()
        S.op("dve", lambda e: e.memset(VS[:, :, 128:129], 1.0), w=[b_VS])
        S.op("dve", lambda e: e.memset(vcx[:, :, 128:129], 1.0), w=[b_vcx])
        S.op("dve", lambda e: e.memset(vwx[:, :, 128:129], 1.0), w=[b_vwx])
        S.op("dve", lambda e: e.memset(vn4[:, :, 128:129], 1.0), w=[b_vn4])

        def unit(kt_ap, ns, q_ap, v_ap, bias_ap, masks, okey, first, last, rd):
            pi = cnt["pst"] % 2; cnt["pst"] += 1
            li = cnt["ptl"] % 3; cnt["ptl"] += 1
            nm = len(masks)
            S.op("pe", lambda e: e.matmul(pst[pi][0:ns, 0:4], lhsT=kt_ap, rhs=q_ap, start=True, stop=(nm == 0)),
                 r=rd, w=[b_pst[pi]])
            for mi, (ml, mr, mrd) in enumerate(masks):
                S.op("pe", lambda e, ml=ml, mr=mr, mi=mi: e.matmul(pst[pi][0:ns, 0:4], lhsT=ml, rhs=mr, start=False,
                                                                    stop=(mi == nm - 1)), r=[b_c] + mrd, w=[b_pst[pi]])
            S.op("act", lambda e: e.activation(out=ptl[li][0:ns, :], in_=pst[pi][0:ns, 0:4], func=AF.Exp,
                                               bias=bias_ap, scale=SCALE), r=[b_pst[pi], b_c], w=[b_ptl[li]])
            S.op("pe", lambda e: e.matmul(pO[okey][:, 0:129], lhsT=ptl[li][0:ns, :], rhs=v_ap, start=first, stop=last),
                 r=[b_ptl[li]] + rd, w=[b_pO[okey]])

        for b in range(DB):
            ii = b % 2
            tk = slice(b * 4, (b + 1) * 4)
            S.dma("sp", pti[:], ptab[b:b + 1, :].partition_broadcast(128), w=[b_pti])
            S.op("dve", lambda e: e.tensor_copy(out=ptf32[:], in_=pti[:]), r=[b_pti], w=[b_ptf32])
            S.op("dve", lambda e: e.tensor_scalar(out=ptf32[:], in0=ptf32[:], scalar1=128.0, scalar2=iot[:, 0:1],
                                                  op0=ALU.mult, op1=ALU.add), r=[b_ptf32, b_c], w=[b_ptf32])
            S.op("dve", lambda e, ii=ii: e.tensor_copy(out=idx[ii][:], in_=ptf32[:]), r=[b_ptf32], w=[b_idx[ii]])
            for j in range(NPG):
                gi = cnt["g"] % 3; cnt["g"] += 1
                S.op("pool", lambda e, gi=gi, ii=ii, j=j: e.indirect_dma_start(
                    out=gth[gi][:], out_offset=None, in_=pool,
                    in_offset=bass.IndirectOffsetOnAxis(ap=idx[ii][:, j:j + 1], axis=0)),
                     r=[b_idx[ii]], w=[b_gth[gi]], dma=True)
                pj = slice(j * 128, (j + 1) * 128)
                S.op("act", lambda e, gi=gi, pj=pj: e.copy(out=XT[:, 0, pj], in_=gth[gi][:, 0:128]), r=[b_gth[gi]], w=[b_XT])
                S.op("dve", lambda e, gi=gi, pj=pj: e.tensor_copy(out=XT[:, 1, pj], in_=gth[gi][:, 128:256]), r=[b_gth[gi]], w=[b_XT])
                S.op("act", lambda e, gi=gi, pj=pj: e.copy(out=KS[:, pj], in_=gth[gi][:, 256:384]), r=[b_gth[gi]], w=[b_KS])
                S.op("dve", lambda e, gi=gi, j=j: e.tensor_copy(out=VS[:, j, 0:128], in_=gth[gi][:, 384:512]), r=[b_gth[gi]], w=[b_VS])
            S.dma("pool", kwb[:], kwT[b], w=[b_kwb])
            S.dma("pool", vwx[:, :, 0:128], vw[b].rearrange("(n p) d -> p n d", p=128), w=[b_vwx])
            S.dma("pool", vn4[0:4, 0, 0:128], vsn[tk, :], w=[b_vn4])
            S.dma("pool", vn4[0:4, 1, 0:128], vwn[tk, :], w=[b_vn4])
            S.dma("sp", gt4[:], gts[tk, :], w=[b_gt4])
            for kv in range(2):
                for nt in range(NBT):
                    nb = min(128, NB - nt * 128)
                    for c in range(32):
                        lhs = XT[:, kv, nt * 4096 + c:nt * 4096 + c + 32 * (nb - 1) + 1:32]
                        S.op("pe", lambda e, lhs=lhs, kv=kv, c=c, nb=nb: e.matmul(
                            pmi[0:nb, :], lhsT=lhs, rhs=cwb[:, kv, c, :], start=(c == 0), stop=False),
                             r=[b_XT, b_c], w=[b_pmi])
                    S.op("pe", lambda e, kv=kv, nb=nb: e.matmul(pmi[0:nb, :], lhsT=onesb[0:1, 0:nb], rhs=pw[0:1, kv, :],
                                                                start=False, stop=True), r=[b_c], w=[b_pmi])
                    if kv == 0:
                        S.op("dve", lambda e, nt=nt, nb=nb: e.tensor_copy(out=kcb[0:nb, nt, :], in_=pmi[0:nb, :]),
                             r=[b_pmi], w=[b_kcb])
                        S.op("pe", lambda e, nt=nt, nb=nb: e.transpose(out=pmb[:, 0:nb], in_=kcb[0:nb, nt, :],
                                                                        identity=idb[0:nb, 0:nb]), r=[b_kcb, b_c], w=[b_pmb])
                        S.op("dve", lambda e, nt=nt, nb=nb: e.tensor_copy(out=kcT[:, nt * 128:nt * 128 + nb], in_=pmb[:, 0:nb]),
                             r=[b_pmb], w=[b_kcT])
                    else:
                        S.op("dve", lambda e, nt=nt, nb=nb: e.tensor_copy(out=vcx[0:nb, nt, 0:128], in_=pmi[0:nb, :]),
                             r=[b_pmi], w=[b_vcx])
            for hh in range(4):
                q_ap = qT[:, hh, tk]
                for nt in range(NBT):
                    nb = min(128, NB - nt * 128)
                    pi = cnt["pst"] % 2; cnt["pst"] += 1
                    fi = cnt["ptf"] % 2; cnt["ptf"] += 1
                    k_ap = kcT[:, nt * 128:nt * 128 + nb]
                    cb_ap = cb[0:nb, nt * 4 + hh:nt * 4 + hh + 1]
                    a_ap = Am[0:nb, nt * (NSEL + 1):(nt + 1) * (NSEL + 1)]
                    S.op("pe", lambda e, pi=pi, k_ap=k_ap, q_ap=q_ap, nb=nb: e.matmul(
                        pst[pi][0:nb, 0:4], lhsT=k_ap, rhs=q_ap, start=True, stop=True), r=[b_kcT, b_c], w=[b_pst[pi]])
                    S.op("act", lambda e, pi=pi, fi=fi, cb_ap=cb_ap, nb=nb: e.activation(
                        out=ptf[fi][0:nb, :], in_=pst[pi][0:nb, 0:4], func=AF.Exp, bias=cb_ap, scale=SCALE),
                         r=[b_pst[pi], b_c], w=[b_ptf[fi]])
                    S.op("pe", lambda e, fi=fi, a_ap=a_ap, nb=nb, nt=nt: e.matmul(
                        psc[:, :], lhsT=ptf[fi][0:nb, :], rhs=a_ap, start=(nt == 0), stop=(nt == NBT - 1)),
                         r=[b_ptf[fi], b_c], w=[b_psc])
                S.op("dve", lambda e: e.tensor_scalar_max(out=rr[:, 0:1], in0=psc[:, NSEL:NSEL + 1], scalar1=1e-30),
                     r=[b_psc], w=[b_rr])
                S.op("dve", lambda e: e.reciprocal(out=rr[:, 1:2], in_=rr[:, 0:1]), r=[b_rr], w=[b_rr])
                if hh == 0:
                    S.op("dve", lambda e: e.tensor_scalar_mul(out=sc[:], in0=psc[:, 0:NSEL], scalar1=rr[:, 1:2]),
                         r=[b_psc, b_rr], w=[b_sc])
                else:
                    S.op("dve", lambda e: e.scalar_tensor_tensor(out=sc[:], in0=psc[:, 0:NSEL], scalar=rr[:, 1:2],
                                                                 in1=sc[:], op0=ALU.mult, op1=ALU.add),
                         r=[b_psc, b_rr, b_sc], w=[b_sc])
            qo_ap = qo[:, tk]
            for nt in range(NBT):
                nb = min(128, NB - nt * 128)
                unit(kcT[:, nt * 128:nt * 128 + nb], nb, qo_ap, vcx[0:nb, nt, 0:129], cbo[0:nb, nt:nt + 1], [], "c",
                     nt == 0, nt == NBT - 1, [b_kcT, b_vcx, b_c])
            S.op("dve", lambda e: e.tensor_tensor(out=sw[:], in0=sc[:], in1=TA[:], op=ALU.mult), r=[b_sc, b_c], w=[b_sw])
            S.op("dve", lambda e: e.tensor_tensor(out=sw[:], in0=sw[:], in1=TB[:], op=ALU.add), r=[b_sw, b_c], w=[b_sw])
            S.op("dve", lambda e: e.max(out=mx[:, 0:8], in_=sw[:]), r=[b_sw], w=[b_mx])
            S.op("dve", lambda e: e.match_replace(out=sw2[:], in_to_replace=mx[:, 0:8], in_values=sw[:], imm_value=-2.0),
                 r=[b_sw, b_mx], w=[b_sw2])
            S.op("dve", lambda e: e.max(out=mx[:, 8:16], in_=sw2[:]), r=[b_sw2], w=[b_mx])
            S.op("dve", lambda e: e.tensor_scalar(out=self_[:], in0=sw[:], scalar1=mx[:, 15:16], scalar2=None, op0=ALU.is_ge),
                 r=[b_sw, b_mx], w=[b_self])
            S.op("dve", lambda e: e.tensor_scalar(out=sw2[:], in0=sw[:], scalar1=0.0, scalar2=None, op0=ALU.is_ge),
                 r=[b_sw], w=[b_sw2])
            S.op("dve", lambda e: e.tensor_tensor(out=self_[:], in0=self_[:], in1=sw2[:], op=ALU.mult),
                 r=[b_self, b_sw2], w=[b_self])
            S.dma("sp", o_sel[tk, :], self_[:], r=[b_self])
            nblk = NPG * 2
            S.op("dve", lambda e: e.memset(selb[:], 0.0), w=[b_selb])
            S.op("dve", lambda e: e.tensor_copy(out=selb[:, 0:nblk], in_=self_[:, 0:nblk]), r=[b_self], w=[b_selb])
            S.op("pe", lambda e: e.transpose(out=pmb[:, 0:4], in_=selb[0:4, :], identity=idb[0:4, 0:4]),
                 r=[b_selb, b_c], w=[b_pmb])
            S.op("dve", lambda e: e.tensor_scalar(out=negT[:, :], in0=pmb[:, 0:4], scalar1=-1.0, scalar2=-NEG,
                                                  op0=ALU.add, op1=ALU.mult), r=[b_pmb], w=[b_negT])
            for j in range(NPG):
                unit(KS[:, j * 128:(j + 1) * 128], 128, qo_ap, VS[:, j, 0:129], ab[:, j:j + 1],
                     [(Eb[0:nblk, j * 128:(j + 1) * 128], negT[0:nblk, :], [b_negT])], "s", j == 0, False,
                     [b_KS, b_VS, b_c])
            unit(ksnb[:, tk], 4, qo_ap, vn4[0:4, 0, 0:129], ab[0:4, NPG:NPG + 1], [(idb[0:4, 0:4], n4b[0:4, 0:4], [])],
                 "s", False, True, [b_c, b_vn4])
            for wi in range(W_BUF // 128):
                masks = [(idb[:], w4b[:, 0:4], [])] if wi == 0 else []
                unit(kwb[:, wi * 128:(wi + 1) * 128], 128, qo_ap, vwx[:, wi, 0:129], aw[:, wi:wi + 1], masks, "w",
                     wi == 0, False, [b_kwb, b_vwx, b_c])
            unit(kwnb[:, tk], 4, qo_ap, vn4[0:4, 1, 0:129], aw[0:4, 4:5], [(idb[0:4, 0:4], n4b[0:4, 0:4], [])],
                 "w", False, True, [b_c, b_vn4])
            for bi, key in enumerate("csw"):
                S.op("dve", lambda e, key=key, bi=bi: e.tensor_scalar_max(out=cf[:, bi:bi + 1], in0=pO[key][:, 128:129],
                                                                         scalar1=1e-30), r=[b_pO[key]], w=[b_cf])
            S.op("dve", lambda e: e.reciprocal(out=cf[:, 3:6], in_=cf[:, 0:3]), r=[b_cf], w=[b_cf])
            S.op("dve", lambda e: e.tensor_tensor(out=cf[:, 3:6], in0=cf[:, 3:6], in1=gt4[:], op=ALU.mult),
                 r=[b_cf, b_gt4], w=[b_cf])
            S.op("dve", lambda e: e.tensor_scalar_mul(out=oc[:, 0:128], in0=pO["c"][:, 0:128], scalar1=cf[:, 3:4]),
                 r=[b_pO["c"], b_cf], w=[b_oc])
            for bi, key in ((1, "s"), (2, "w")):
                S.op("dve", lambda e, key=key, bi=bi: e.scalar_tensor_tensor(
                    out=oc[:, 0:128], in0=pO[key][:, 0:128], scalar=cf[:, 3 + bi:4 + bi], in1=oc[:, 0:128],
                    op0=ALU.mult, op1=ALU.add), r=[b_pO[key], b_cf, b_oc], w=[b_oc])
            S.dma("sp", o_n[tk, :], oc[:, 0:128], r=[b_oc])
        S.emit()
    return nc


def _nsa_s_consts(head, NPG):
    f32 = np.float32
    P = NPG * 128
    NB = P // 32
    NBT = (NB + 127) // 128
    NSEL = NPG * 2 + 1
    p = np.arange(128)
    slopes = np.exp2(-8.0 * np.arange(1, 9, dtype=np.float64) / 8)
    g = head // 4
    cb = np.zeros((128, NBT, 4), f32)
    for nt in range(NBT):
        c_end = 32 * (nt * 128 + p + 1) - 1
        for hh in range(4):
            cb[:, nt, hh] = slopes[4 * g + hh] * (c_end - P)
    cbo = np.ascontiguousarray(cb[:, :, head % 4])
    ab = np.zeros((128, NPG + 1), f32)
    for j in range(NPG):
        ab[:, j] = slopes[head] * (128 * j + p - P)
    ab[:, NPG] = slopes[head] * p
    aw = np.zeros((128, 5), f32)
    for wi in range(4):
        aw[:, wi] = slopes[head] * (-W_BUF + 128 * wi + p)
    aw[:, 4] = slopes[head] * p
    A = np.zeros((128, NBT, NSEL + 1), f32)
    for nt in range(NBT):
        for n in range(128):
            if nt * 128 + n < NB:
                A[n, nt, (nt * 128 + n) // 2] = 1.0
    A[:, :, NSEL] = 1.0
    blk = np.arange(NSEL)
    cur = NPG * 2
    f1 = blk == cur; f2 = blk == cur - 1; f0 = blk == 0
    forced = f0 | f1 | f2
    TA = np.tile((~forced).astype(f32)[None], (4, 1))
    TB = np.tile(np.where(forced, np.where(f1, 10002.0, np.where(f2, 10001.0, 10000.0)), 0.0).astype(f32)[None], (4, 1))
    E = np.zeros((128, NPG, 128), f32)
    for j in range(NPG):
        E[2 * j, j, 0:64] = 1.0
        E[2 * j + 1, j, 64:128] = 1.0
    t4 = np.arange(4)
    return dict(c_id=np.eye(128, dtype=f32), c_iota=p.astype(f32)[:, None],
                c_n4=np.where(t4[:, None] > t4[None, :], NEG, 0.0).astype(f32),
                c_w4=np.where(p[:, None] <= t4[None, :], NEG, 0.0).astype(f32),
                c_cb=cb.reshape(128, -1), c_cbo=cbo, c_ab=ab, c_aw=aw, c_A=A.reshape(128, -1), c_TA=TA, c_TB=TB,
                c_E=E.reshape(128, -1))


def launch_nsa_s(L1, cache_nsa_kv, state_nsa_win, page_table, cmp_pe, cmp_w):
    f32 = np.float32
    DB, DS = L1["DB"], L1["DS"]
    ck = np.asarray(cache_nsa_kv)[0]
    NPOOL = ck.shape[0]
    pt = np.ascontiguousarray(np.asarray(page_table, np.int32))
    NPG = pt.shape[1]
    nc = build_nsa_s(DB, NPG, NPOOL)
    nkv = L1["nkv_s"].reshape(DB * DS, 4, 2, 128)
    win_new = L1["win_s"].reshape(DB, W_BUF, 2, 2, 128)[:, W_BUF - DS:]
    win_new = win_new.reshape(DB * DS, 2, 2, 128)
    stw = np.asarray(state_nsa_win, f32)[0]
    q = L1["q_s"].reshape(DB * DS, 16, 128)
    gates = L1["g_s"].reshape(DB * DS, 8, 3)
    pe = np.asarray(cmp_pe, f32)[0]; cwt = np.asarray(cmp_w, f32)[0]
    in_maps = []
    for c in range(8):
        g = c // 4
        m = _nsa_s_consts(c, NPG)
        m["qg"] = np.ascontiguousarray(q[:, 4 * g:4 * g + 4, :].transpose(2, 1, 0))
        m["qoT"] = np.ascontiguousarray(q[:, c, :].T)
        tr = lambda a: a.transpose(0, 2, 1)
        m["pool"] = np.ascontiguousarray(np.concatenate(
            [tr(ck[:, :, 0, g, :]), tr(ck[:, :, 1, g, :]), tr(ck[:, :, 2, g, :]), ck[:, :, 3, g, :]], axis=2)
        ).reshape(NPOOL * 128, 512)
        m["ptab"] = pt
        m["pe_in"] = np.ascontiguousarray(pe.transpose(0, 2, 1))
        m["cw"] = np.ascontiguousarray(cwt.transpose(0, 2, 1, 3))
        m["ksn"] = np.ascontiguousarray(nkv[:, 2, g, :].T)
        m["vsn"] = np.ascontiguousarray(nkv[:, 3, g, :])
        m["kwn"] = np.ascontiguousarray(win_new[:, 0, g, :].T)
        m["vwn"] = np.ascontiguousarray(win_new[:, 1, g, :])
        m["kwT"] = np.ascontiguousarray(stw[:, :, 0, g, :].transpose(0, 2, 1))
        m["vw"] = np.ascontiguousarray(stw[:, :, 1, g, :])
        m["gts"] = np.ascontiguousarray(gates[:, c, :])
        in_maps.append(m)
    res = run_bass_kernel_spmd(nc, in_maps, core_ids=list(range(8))).results
    o_n = np.stack([res[c]["o_n"] for c in range(8)], axis=1)
    sel = np.stack([res[c]["o_sel"] for c in range(8)], axis=1)
    return o_n, sel


def build_out(SEQ):
    TOK = SEQ // 4
    NT = TOK // 128
    NTT = NT + 1
    nc = bass.Bass("TRN2", target_bir_lowering=False)
    dt = nc.dram_tensor
    x_own = dt("x_own", [TOK, D_MODEL], F32, kind="ExternalInput").ap()
    x_s = dt("x_s", [128, D_MODEL], F32, kind="ExternalInput").ap()
    o_own = dt("o_own", [TOK, D_MODEL], F32, kind="ExternalInput").ap()
    o_s = dt("o_s", [128, D_MODEL], F32, kind="ExternalInput").ap()
    z_own = dt("z_own", [TOK, D_MODEL], BF16, kind="ExternalInput").ap()
    z_s = dt("z_s", [128, D_MODEL], BF16, kind="ExternalInput").ap()
    w_out = dt("w_out", [D_MODEL, D_MODEL], F32, kind="ExternalInput").ap()
    g_fin = dt("norm_final", [1, D_MODEL], F32, kind="ExternalInput").ap()
    ident = dt("ident", [128, 128], F32, kind="ExternalInput").ap()
    y_own = dt("y_own", [TOK, D_MODEL], F32, kind="ExternalOutput").ap()
    y_s = dt("y_s", [128, D_MODEL], F32, kind="ExternalOutput").ap()
    w_v = w_out.rearrange("(k p) c -> p k c", p=128)
    with ExitStack() as st:
        S = Sched(nc, st)
        b_c = S.buf("consts")
        idb = S.sb("idb", [128, 128], BF16)
        S.dma("pool", idb[:], ident, w=[b_c])
        gam = S.sb("gam", [128, D_MODEL], F32)
        S.dma("sp", gam[:], g_fin.partition_broadcast(128), w=[b_c])
        wo = S.sb("wo", [128, 16, D_MODEL], BF16)
        for cg in range(4):
            S.dma("pool", wo[:, :, cg * 512:(cg + 1) * 512], w_v[:, :, cg * 512:(cg + 1) * 512], w=[b_c])
        xt = [S.sb(f"xt{i}", [128, D_MODEL], F32) for i in range(2)]; b_xt = [S.buf() for _ in range(2)]
        ot = [S.sb(f"ot{i}", [128, D_MODEL], F32) for i in range(2)]; b_ot = [S.buf() for _ in range(2)]
        zt = [S.sb(f"zt{i}", [128, D_MODEL], BF16) for i in range(2)]; b_zt = [S.buf() for _ in range(2)]
        mixb = [S.sb(f"mix{i}", [128, D_MODEL], BF16) for i in range(2)]; b_mix = [S.buf() for _ in range(2)]
        mT = [S.sb(f"mT{i}", [128, 16, 128], BF16) for i in range(2)]; b_mT = [S.buf() for _ in range(2)]
        yt = [S.sb(f"yt{i}", [128, D_MODEL], F32) for i in range(2)]; b_yt = [S.buf() for _ in range(2)]
        sq = S.sb("sq", [128, D_MODEL], BF16); b_sq = S.buf()
        ss = [S.sb(f"ss{i}", [128, 4], F32) for i in range(2)]; b_ss = [S.buf() for _ in range(2)]
        pT = [S.ps(f"pT{i}", [128, 512], BF16) for i in range(2)]; b_pT = [S.buf() for _ in range(2)]
        acc = [S.ps(f"acc{i}", [128, 512], F32) for i in range(2)]; b_acc = [S.buf() for _ in range(2)]
        na = 0
        for tt in range(NTT):
            i = tt % 2
            samp = tt == NT
            rows = slice(tt * 128, (tt + 1) * 128)
            S.dma("sp", xt[i][:], x_s if samp else x_own[rows, :], w=[b_xt[i]])
            S.dma("sp", ot[i][:], o_s if samp else o_own[rows, :], w=[b_ot[i]])
            S.dma("sp", zt[i][:], z_s if samp else z_own[rows, :], w=[b_zt[i]])
            S.op("dve", lambda e, i=i: e.tensor_tensor(out=mixb[i][:], in0=ot[i][:], in1=zt[i][:], op=ALU.mult),
                 r=[b_ot[i], b_zt[i]], w=[b_mix[i]])
            for g4 in range(4):
                p = g4 % 2
                for q in range(4):
                    k = g4 * 4 + q
                    S.op("pe", lambda e, i=i, p=p, q=q, k=k: e.transpose(
                        out=pT[p][:, q * 128:(q + 1) * 128], in_=mixb[i][:, k * 128:(k + 1) * 128], identity=idb[:]),
                         r=[b_mix[i], b_c], w=[b_pT[p]])
                dst = mT[i][:, g4 * 4:(g4 + 1) * 4, :]
                srcp = pT[p][:].rearrange("p (q t) -> p q t", q=4)
                if g4 % 2 == 0:
                    S.op("act", lambda e, dst=dst, srcp=srcp: e.copy(out=dst, in_=srcp), r=[b_pT[p]], w=[b_mT[i]])
                else:
                    S.op("dve", lambda e, dst=dst, srcp=srcp: e.tensor_copy(out=dst, in_=srcp), r=[b_pT[p]], w=[b_mT[i]])
            for cg in range(4):
                a = na % 2; na += 1
                cs = slice(cg * 512, (cg + 1) * 512)
                for k in range(16):
                    S.op("pe", lambda e, a=a, i=i, k=k, cs=cs: e.matmul(acc[a][:], lhsT=mT[i][:, k, :], rhs=wo[:, k, cs],
                                                                        start=(k == 0), stop=(k == 15)),
                         r=[b_mT[i], b_c], w=[b_acc[a]])
                S.op("dve", lambda e, a=a, i=i, cs=cs: e.tensor_tensor(out=yt[i][:, cs], in0=acc[a][:], in1=xt[i][:, cs], op=ALU.add),
                     r=[b_acc[a], b_xt[i]], w=[b_yt[i]])
            S.op("dve", lambda e, i=i: e.memset(ss[i][:], 0.0), w=[b_ss[i]])
            S.op("act", lambda e, i=i: e.activation(out=sq[:], in_=yt[i][:], func=AF.Square, accum_out=ss[i][:, 0:1]),
                 r=[b_yt[i]], w=[b_sq, b_ss[i]])
            S.op("act", lambda e, i=i: e.activation(out=ss[i][:, 1:2], in_=ss[i][:, 0:1], func=AF.Sqrt,
                                                    scale=1.0 / D_MODEL, bias=RMS_EPS), r=[b_ss[i]], w=[b_ss[i]])
            S.op("dve", lambda e, i=i: e.reciprocal(out=ss[i][:, 2:3], in_=ss[i][:, 1:2]), r=[b_ss[i]], w=[b_ss[i]])
            S.op("dve", lambda e, i=i: e.scalar_tensor_tensor(out=yt[i][:], in0=yt[i][:], scalar=ss[i][:, 2:3], in1=gam[:],
                                                              op0=ALU.mult, op1=ALU.mult),
                 r=[b_yt[i], b_ss[i], b_c], w=[b_yt[i]])
            S.dma("sp", y_s if samp else y_own[rows, :], yt[i][:], r=[b_yt[i]])
        S.emit()
    return nc


def launch_out(L1, x_prompt, x_sample, o_p, o_s, w_out, norm_final):
    f32 = np.float32
    B, SEQ, DB, DS = L1["B"], L1["SEQ"], L1["DB"], L1["DS"]
    TOK = SEQ // 4
    nc = build_out(SEQ)
    x_prompt = np.asarray(x_prompt, f32)
    common = dict(x_s=np.ascontiguousarray(np.asarray(x_sample, f32).reshape(DB * DS, D_MODEL)),
                  o_s=np.ascontiguousarray(o_s.reshape(DB * DS, D_MODEL)), z_s=np.ascontiguousarray(L1["z_s"]),
                  w_out=np.ascontiguousarray(np.asarray(w_out, f32)[0]),
                  norm_final=np.asarray(norm_final, f32).reshape(1, D_MODEL), ident=np.eye(128, dtype=f32))
    o_p = o_p.reshape(B, SEQ, D_MODEL)
    in_maps = []
    for c in range(8):
        b, r = c // 4, c % 4
        sl = slice(r * TOK, (r + 1) * TOK)
        m = dict(common)
        m["x_own"] = np.ascontiguousarray(x_prompt[b, sl])
        m["o_own"] = np.ascontiguousarray(o_p[b, sl])
        m["z_own"] = np.ascontiguousarray(L1["z_p"][b, sl])
        in_maps.append(m)
    res = run_bass_kernel_spmd(nc, in_maps, core_ids=list(range(8))).results
    y_p = np.stack([np.concatenate([res[b * 4 + r]["y_own"] for r in range(4)], axis=0) for b in range(2)])
    y_s = res[0]["y_s"].reshape(DB, DS, D_MODEL)
    return y_p, y_s


def kernel(x_prompt, x_sample, cache_nsa_kv, cache_fox_kv, cache_fox_logf, state_nsa_win, page_table,
           norm_in, w_in, b_gate, b_forget, cmp_pe, cmp_w, w_out, norm_final):
    L1 = launch1(x_prompt, x_sample, state_nsa_win, norm_in, w_in, b_gate, b_forget)
    B, SEQ, DB, DS = L1["B"], L1["SEQ"], L1["DB"], L1["DS"]
    of_p, of_s = launch_fox(L1, cache_fox_kv, cache_fox_logf, page_table)
    on_p, _ = launch_nsa_p(L1, cmp_pe, cmp_w)
    on_s, _ = launch_nsa_s(L1, cache_nsa_kv, state_nsa_win, page_table, cmp_pe, cmp_w)
    o_p = np.concatenate([on_p, of_p], axis=2)
    o_s = np.concatenate([on_s, of_s], axis=1)
    y_p, y_s = launch_out(L1, x_prompt, x_sample, o_p, o_s, w_out, norm_final)
    nkv_p = L1["nkv_p"].reshape(1, B, SEQ, 4, 2, 128)
    fkv_p = L1["fkv_p"].reshape(1, B, SEQ, 2, 8, 128)
    lf_p = L1["lf_p"].reshape(1, B, SEQ, 8)
    wk = min(W_BUF, SEQ)
    win_p = L1["win_p"][:, SEQ - wk:].reshape(1, B, wk, 2, 2, 128)
    nkv_s = L1["nkv_s"].reshape(1, DB, DS, 4, 2, 128)
    fkv_s = L1["fkv_s"].reshape(1, DB, DS, 2, 8, 128)
    lf_s = L1["lf_s"].reshape(1, DB, DS, 8)
    win_s = L1["win_s"].reshape(1, DB, W_BUF, 2, 2, 128)
    return (y_p, y_s, nkv_p, nkv_s, fkv_p, fkv_s, lf_p, lf_s, win_p, win_s)
```

```python
from contextlib import ExitStack
import numpy as np
import concourse.bass as bass
import concourse.mybir as mybir
from concourse.bass_utils import run_bass_kernel_spmd

F32 = mybir.dt.float32
BF16 = mybir.dt.bfloat16
AF = mybir.ActivationFunctionType
ALU = mybir.AluOpType

ENGS = ("pe", "act", "dve", "pool", "sp")
NSLOT = 6


class Buf:
    __slots__ = ("name", "last_w", "readers")

    def __init__(self, name):
        self.name = name
        self.last_w = None
        self.readers = []


class Sched:
    def __init__(self, nc, stack):
        self.nc = nc
        self.stack = stack
        self.ops = []
        self.ndma = {e: 0 for e in ENGS}
        self.nbuf = 0

    def sb(self, name, shape, dt):
        return self.stack.enter_context(self.nc.sbuf_tensor(name, list(shape), dt))

    def ps(self, name, shape, dt=mybir.dt.float32):
        return self.stack.enter_context(self.nc.psum_tensor(name, list(shape), dt))

    def buf(self, name=None):
        self.nbuf += 1
        return Buf(name or f"b{self.nbuf}")

    def op(self, eng, fn, r=(), w=(), dma=False):
        idx = len(self.ops)
        deps = set()
        for b in r:
            if b.last_w is not None:
                deps.add(b.last_w)
        for b in w:
            if b.last_w is not None:
                deps.add(b.last_w)
            deps.update(b.readers)
        for b in r:
            b.readers.append(idx)
        for b in w:
            b.last_w = idx
            b.readers = []
        slot = None
        if dma:
            n = self.ndma[eng]
            self.ndma[eng] = n + 1
            slot = n
        deps.discard(idx)
        self.ops.append(dict(eng=eng, fn=fn, deps=deps, dma=dma, slot=slot, sig=False))
        return idx

    def dma(self, eng, out, in_, r=(), w=(), **kw):
        return self.op(eng, lambda e: e.dma_start(out=out, in_=in_, **kw), r=r, w=w, dma=True)

    def emit(self, final_wait_all=True):
        nc = self.nc
        ops = self.ops
        for i, o in enumerate(ops):
            if o["eng"] == "pe" and not o["dma"]:
                o["deps"] = {d for d in o["deps"] if not (ops[d]["eng"] == "pe" and not ops[d]["dma"])}
        last_in_slot = {}
        for i, o in enumerate(ops):
            if o["dma"]:
                key = (o["eng"], o["slot"] % NSLOT)
                if key in last_in_slot:
                    o["deps"].add(last_in_slot[key])
                last_in_slot[key] = i
        final = []
        if final_wait_all:
            for key, i in last_in_slot.items():
                final.append(i)
        for o in ops:
            for d in o["deps"]:
                ops[d]["sig"] = True
        for d in final:
            ops[d]["sig"] = True
        sem_e = {e: self.stack.enter_context(nc.semaphore(f"s_{e}")) for e in ENGS}
        sem_d = {}
        for e in ENGS:
            if self.ndma[e]:
                for s in range(min(NSLOT, self.ndma[e])):
                    sem_d[(e, s)] = self.stack.enter_context(nc.semaphore(f"d_{e}{s}"))
        cnt = {e: 0 for e in ENGS}
        for o in ops:
            if o["dma"]:
                o["sem"] = sem_d[(o["eng"], o["slot"] % NSLOT)]
                o["val"] = 16 * (o["slot"] // NSLOT + 1)
            elif o["sig"]:
                cnt[o["eng"]] += 1
                o["sem"] = sem_e[o["eng"]]
                o["val"] = cnt[o["eng"]]
        per = {e: [i for i, o in enumerate(ops) if o["eng"] == e] for e in ENGS}
        self.stats = {e: len(per[e]) for e in ENGS}

        def run(engname, e):
            known = {}
            nw = 0
            for i in per[engname]:
                o = ops[i]
                need = {}
                for d in o["deps"]:
                    od = ops[d]
                    k = id(od["sem"])
                    if known.get(k, 0) >= od["val"]:
                        continue
                    if k not in need or need[k][1] < od["val"]:
                        need[k] = (od["sem"], od["val"])
                for k, (s, v) in need.items():
                    e.wait_ge(s, v)
                    known[k] = v
                    nw += 1
                ins = o["fn"](e)
                if o["dma"]:
                    ins.then_inc(o["sem"], 16)
                elif o["sig"]:
                    ins.then_inc(o["sem"], 1)
            if engname == "sp":
                for d in final:
                    od = ops[d]
                    if known.get(id(od["sem"]), 0) < od["val"]:
                        e.wait_ge(od["sem"], od["val"])
                        known[id(od["sem"])] = od["val"]
            self.stats[engname + "_waits"] = nw

        with nc.Block() as block:
            @block.tensor
            def _(e):
                run("pe", e)

            @block.scalar
            def _(e):
                run("act", e)

            @block.vector
            def _(e):
                run("dve", e)

            @block.gpsimd
            def _(e):
                run("pool", e)

            @block.sync
            def _(e):
                run("sp", e)


D_MODEL = 2048
IN_COLS = 7712
C_QN = 0
C_GATE = 2560
C_ZN = 2584
C_QF = 3608
C_ZF = 6688
C_KVN = 1024
C_KF = 4632
C_VF = 5656
C_FF = 6680
RMS_EPS = 1e-6
W_BUF = 512


def build_nc(SEQ, DB):
    TOK = SEQ // 4
    NT = TOK // 128
    NTT = NT + 1
    assert DB * 4 == 128
    nc = bass.Bass("TRN2", target_bir_lowering=False)
    dt = nc.dram_tensor
    x_own = dt("x_own", [TOK, D_MODEL], F32, kind="ExternalInput").ap()
    x_s = dt("x_s", [128, D_MODEL], F32, kind="ExternalInput").ap()
    w_in = dt("w_in", [D_MODEL, IN_COLS], F32, kind="ExternalInput").ap()
    g_in = dt("norm_in", [1, D_MODEL], F32, kind="ExternalInput").ap()
    b_fg = dt("b_forget", [1, 8], F32, kind="ExternalInput").ap()
    b_gt = dt("b_gate", [1, 24], F32, kind="ExternalInput").ap()
    ident = dt("ident", [128, 128], F32, kind="ExternalInput").ap()
    st_win = dt("state_win", [DB, W_BUF, 512], F32, kind="ExternalInput").ap()
    o_nkv_p = dt("o_nkv_p", [TOK, 1024], F32, kind="ExternalOutput").ap()
    o_fkv_p = dt("o_fkv_p", [TOK, 2048], F32, kind="ExternalOutput").ap()
    o_lf_p = dt("o_lf_p", [TOK, 8], F32, kind="ExternalOutput").ap()
    o_win_p = dt("o_win_p", [TOK, 512], F32, kind="ExternalOutput").ap()
    o_nkv_s = dt("o_nkv_s", [128, 1024], F32, kind="ExternalOutput").ap()
    o_fkv_s = dt("o_fkv_s", [128, 2048], F32, kind="ExternalOutput").ap()
    o_lf_s = dt("o_lf_s", [128, 8], F32, kind="ExternalOutput").ap()
    o_win_s = dt("o_win_s", [DB, W_BUF, 512], F32, kind="ExternalOutput").ap()

    o_q = {False: dt("o_q_p", [TOK, 2048], BF16, kind="ExternalOutput").ap(),
           True: dt("o_q_s", [128, 2048], BF16, kind="ExternalOutput").ap()}
    o_z = {False: dt("o_z_p", [TOK, 2048], BF16, kind="ExternalOutput").ap(),
           True: dt("o_z_s", [128, 2048], BF16, kind="ExternalOutput").ap()}
    o_g = {False: dt("o_g_p", [TOK, 24], F32, kind="ExternalOutput").ap(),
           True: dt("o_g_s", [128, 24], F32, kind="ExternalOutput").ap()}
    w_v = w_in.rearrange("(k p) c -> p k c", p=128)

    with ExitStack() as st:
        S = Sched(nc, st)
        idf = S.sb("idf", [128, 128], F32); b_idf = S.buf()
        idb = S.sb("idb", [128, 128], BF16); b_idb = S.buf()
        gam = S.sb("gam", [128, D_MODEL], F32); b_gam = S.buf()
        bfg = S.sb("bfg", [128, 8], F32); b_bfg = S.buf()
        hT = S.sb("hT", [128, 16, NTT * 128], BF16)
        b_hT = [S.buf() for _ in range(NTT)]
        xt = [S.sb(f"xt{i}", [128, D_MODEL], F32) for i in range(2)]; b_xt = [S.buf() for _ in range(2)]
        hb = [S.sb(f"hb{i}", [128, D_MODEL], BF16) for i in range(2)]; b_hb = [S.buf() for _ in range(2)]
        sq = S.sb("sq", [128, D_MODEL], BF16); b_sq = S.buf()
        ss = [S.sb(f"ss{i}", [128, 4], F32) for i in range(2)]; b_ss = [S.buf() for _ in range(2)]
        pT = [S.ps(f"pT{i}", [128, 512], BF16) for i in range(2)]; b_pT = [S.buf() for _ in range(2)]
        acc = [S.ps(f"acc{i}", [128, 512], F32) for i in range(2)]; b_acc = [S.buf() for _ in range(2)]
        accs = S.ps("accs", [128, 8], F32); b_accs = S.buf()
        wt = [S.sb(f"wt{i}", [128, 16, 512], BF16) for i in range(2)]; b_wt = [S.buf() for _ in range(2)]
        stg = [S.sb(f"stg{i}", [128, 512], F32) for i in range(4)]; b_stg = [S.buf() for _ in range(4)]
        bgt = S.sb("bgt", [128, 24], F32); b_bgt = S.buf()
        gt = [S.sb(f"gt{i}", [128, 24], F32) for i in range(2)]; b_gt_ = [S.buf() for _ in range(2)]
        accg = S.ps("accg", [128, 24], F32); b_accg = S.buf()
        stb = [S.sb(f"stb{i}", [128, 512], BF16) for i in range(4)]; b_stb = [S.buf() for _ in range(4)]
        lft = [S.sb(f"lft{i}", [128, 8], F32) for i in range(2)]; b_lft = [S.buf() for _ in range(2)]

        S.dma("sp", idf[:], ident, w=[b_idf])
        S.dma("sp", gam[:], g_in.partition_broadcast(128), w=[b_gam])
        S.dma("sp", bfg[:], b_fg.partition_broadcast(128), w=[b_bfg])
        S.op("dve", lambda e: e.tensor_copy(out=idb[:], in_=idf[:]), r=[b_idf], w=[b_idb])
        S.dma("sp", bgt[:], b_gt.partition_broadcast(128), w=[b_bgt])

        bw = S.buf()
        S.dma("sp", o_win_s[:, 0:W_BUF - 4, :], st_win[:, 4:W_BUF, :], w=[bw])

        for tt in range(NTT):
            i = tt % 2
            src = x_own[tt * 128:(tt + 1) * 128, :] if tt < NT else x_s
            S.dma("sp", xt[i][:], src, w=[b_xt[i]])
            S.op("dve", lambda e, i=i: e.memset(ss[i][:], 0.0), w=[b_ss[i]])
            S.op("act", lambda e, i=i: e.activation(out=sq[:], in_=xt[i][:], func=AF.Square,
                                                    accum_out=ss[i][:, 0:1]),
                 r=[b_xt[i]], w=[b_sq, b_ss[i]])
            S.op("act", lambda e, i=i: e.activation(out=ss[i][:, 1:2], in_=ss[i][:, 0:1], func=AF.Sqrt,
                                                    scale=1.0 / D_MODEL, bias=RMS_EPS),
                 r=[b_ss[i]], w=[b_ss[i]])
            S.op("dve", lambda e, i=i: e.reciprocal(out=ss[i][:, 2:3], in_=ss[i][:, 1:2]),
                 r=[b_ss[i]], w=[b_ss[i]])
            S.op("dve", lambda e, i=i: e.scalar_tensor_tensor(out=hb[i][:], in0=xt[i][:], scalar=ss[i][:, 2:3],
                                                              in1=gam[:], op0=ALU.mult, op1=ALU.mult),
                 r=[b_xt[i], b_ss[i], b_gam], w=[b_hb[i]])
            for g4 in range(4):
                p = g4 % 2
                for q in range(4):
                    k = g4 * 4 + q
                    S.op("pe", lambda e, i=i, p=p, q=q, k=k: e.transpose(
                        out=pT[p][:, q * 128:(q + 1) * 128], in_=hb[i][:, k * 128:(k + 1) * 128], identity=idb[:]),
                         r=[b_hb[i], b_idb], w=[b_pT[p]])
                eng = "act" if g4 % 2 == 0 else "dve"
                dst = hT[:, g4 * 4:(g4 + 1) * 4, tt * 128:(tt + 1) * 128]
                srcp = pT[p][:].rearrange("p (q t) -> p q t", q=4)
                if eng == "act":
                    S.op("act", lambda e, dst=dst, srcp=srcp: e.copy(out=dst, in_=srcp), r=[b_pT[p]], w=[b_hT[tt]])
                else:
                    S.op("dve", lambda e, dst=dst, srcp=srcp: e.tensor_copy(out=dst, in_=srcp), r=[b_pT[p]], w=[b_hT[tt]])

        groups = [(C_KVN + 512 * j, 512, "nkv", j) for j in range(3)] + \
                 [(C_KF + 512 * j, 512, "fkv", j) for j in range(2)] + \
                 [(C_VF + 512 * j, 512, "fkv", 2 + j) for j in range(2)] + \
                 [(C_FF, 8, "lf", 0), (C_GATE, 24, "gate", 0)] + \
                 [(C_QN + 512 * j, 512, "q", j) for j in range(2)] + \
                 [(C_QF + 512 * j, 512, "q", 2 + j) for j in range(2)] + \
                 [(C_ZN + 512 * j, 512, "z", j) for j in range(2)] + \
                 [(C_ZF + 512 * j, 512, "z", 2 + j) for j in range(2)]
        nstg = 0
        nstb = 0
        for gi, (c0, wd, kind, j) in enumerate(groups):
            wi = gi % 2
            S.dma("pool", wt[wi][:, :, 0:wd], w_v[:, :, c0:c0 + wd], w=[b_wt[wi]])
            for tt in range(NTT):
                samp = tt == NT
                rows = slice(tt * 128, (tt + 1) * 128)
                if kind == "lf":
                    for k in range(16):
                        S.op("pe", lambda e, wi=wi, k=k, tt=tt: e.matmul(
                            accs[:], lhsT=hT[:, k, tt * 128:(tt + 1) * 128], rhs=wt[wi][:, k, 0:8],
                            start=(k == 0), stop=(k == 15)), r=[b_hT[tt], b_wt[wi]], w=[b_accs])
                    li = tt % 2
                    S.op("dve", lambda e, li=li: e.tensor_tensor(out=lft[li][:], in0=accs[:], in1=bfg[:], op=ALU.add),
                         r=[b_accs, b_bfg], w=[b_lft[li]])
                    S.op("act", lambda e, li=li: e.activation(out=lft[li][:], in_=lft[li][:], func=AF.Exp, scale=-1.0),
                         r=[b_lft[li]], w=[b_lft[li]])
                    S.op("act", lambda e, li=li: e.activation(out=lft[li][:], in_=lft[li][:], func=AF.Ln, bias=1.0),
                         r=[b_lft[li]], w=[b_lft[li]])
                    S.op("dve", lambda e, li=li: e.tensor_scalar_mul(out=lft[li][:], in0=lft[li][:], scalar1=-1.0),
                         r=[b_lft[li]], w=[b_lft[li]])
                    S.dma("sp", o_lf_s if samp else o_lf_p[rows, :], lft[li][:], r=[b_lft[li]])
                    continue
                if kind == "gate":
                    for k in range(16):
                        S.op("pe", lambda e, wi=wi, k=k, tt=tt: e.matmul(
                            accg[:], lhsT=hT[:, k, tt * 128:(tt + 1) * 128], rhs=wt[wi][:, k, 0:24],
                            start=(k == 0), stop=(k == 15)), r=[b_hT[tt], b_wt[wi]], w=[b_accg])
                    li = tt % 2
                    S.op("dve", lambda e, li=li: e.tensor_tensor(out=gt[li][:], in0=accg[:], in1=bgt[:], op=ALU.add),
                         r=[b_accg, b_bgt], w=[b_gt_[li]])
                    S.op("act", lambda e, li=li: e.activation(out=gt[li][:], in_=gt[li][:], func=AF.Exp, scale=-1.0),
                         r=[b_gt_[li]], w=[b_gt_[li]])
                    S.op("dve", lambda e, li=li: e.tensor_scalar_add(out=gt[li][:], in0=gt[li][:], scalar1=1.0),
                         r=[b_gt_[li]], w=[b_gt_[li]])
                    S.op("dve", lambda e, li=li: e.reciprocal(out=gt[li][:], in_=gt[li][:]),
                         r=[b_gt_[li]], w=[b_gt_[li]])
                    S.dma("sp", o_g[samp] if samp else o_g[samp][rows, :], gt[li][:], r=[b_gt_[li]])
                    continue
                a = (gi * NTT + tt) % 2
                for k in range(16):
                    S.op("pe", lambda e, a=a, wi=wi, k=k, tt=tt: e.matmul(
                        acc[a][:], lhsT=hT[:, k, tt * 128:(tt + 1) * 128], rhs=wt[wi][:, k, :],
                        start=(k == 0), stop=(k == 15)), r=[b_hT[tt], b_wt[wi]], w=[b_acc[a]])
                if kind in ("q", "z"):
                    bi = nstb % 4
                    nstb += 1
                    if kind == "q":
                        S.op("dve", lambda e, bi=bi, a=a: e.tensor_copy(out=stb[bi][:], in_=acc[a][:]),
                             r=[b_acc[a]], w=[b_stb[bi]])
                    else:
                        S.op("act", lambda e, bi=bi, a=a: e.activation(out=stb[bi][:], in_=acc[a][:], func=AF.Silu),
                             r=[b_acc[a]], w=[b_stb[bi]])
                    od = (o_q if kind == "q" else o_z)[samp]
                    od = (od if samp else od[rows, :])[:, j * 512:(j + 1) * 512]
                    S.dma("sp", od, stb[bi][:], r=[b_stb[bi]])
                    continue
                si = nstg % 4
                nstg += 1
                if nstg % 2:
                    S.op("act", lambda e, si=si, a=a: e.copy(out=stg[si][:], in_=acc[a][:]), r=[b_acc[a]], w=[b_stg[si]])
                else:
                    S.op("dve", lambda e, si=si, a=a: e.tensor_copy(out=stg[si][:], in_=acc[a][:]), r=[b_acc[a]], w=[b_stg[si]])
                if kind == "nkv":
                    if j < 2:
                        dst = (o_nkv_s if samp else o_nkv_p[rows, :])[:, j * 512:(j + 1) * 512]
                        S.dma("sp", dst, stg[si][:], r=[b_stg[si]])
                    elif not samp:
                        S.dma("sp", o_win_p[rows, :], stg[si][:], r=[b_stg[si]])
                    else:
                        for bb in range(DB):
                            S.dma("sp", o_win_s[bb, W_BUF - 4:W_BUF, :], stg[si][bb * 4:(bb + 1) * 4, :],
                                  r=[b_stg[si]], w=[bw])
                else:
                    dst = (o_fkv_s if samp else o_fkv_p[rows, :])[:, j * 512:(j + 1) * 512]
                    S.dma("sp", dst, stg[si][:], r=[b_stg[si]])
        S.emit()
    return nc


def launch1(x_prompt, x_sample, state_nsa_win, norm_in, w_in, b_gate, b_forget):
    f32 = np.float32
    x_prompt = np.asarray(x_prompt, f32)
    B, SEQ, D = x_prompt.shape
    DB, DS = np.asarray(x_sample).shape[:2]
    TOK = SEQ // 4
    nc = build_nc(SEQ, DB)
    ident = np.eye(128, dtype=f32)
    xs = np.ascontiguousarray(np.asarray(x_sample, f32).reshape(DB * DS, D))
    stw = np.ascontiguousarray(np.asarray(state_nsa_win, f32)[0].reshape(DB, W_BUF, 512))
    common = dict(x_s=xs, w_in=np.ascontiguousarray(np.asarray(w_in, f32)[0]),
                  norm_in=np.asarray(norm_in, f32).reshape(1, D), b_forget=np.asarray(b_forget, f32).reshape(1, 8),
                  b_gate=np.asarray(b_gate, f32).reshape(1, 24), ident=ident, state_win=stw)
    in_maps = []
    for c in range(8):
        b, r = c // 4, c % 4
        m = dict(common)
        m["x_own"] = np.ascontiguousarray(x_prompt[b, r * TOK:(r + 1) * TOK])
        in_maps.append(m)
    res = run_bass_kernel_spmd(nc, in_maps, core_ids=list(range(8))).results

    def cat(name):
        return np.stack([np.concatenate([res[b * 4 + r][name] for r in range(4)], axis=0) for b in range(2)])

    r0 = res[0]
    out = dict(B=B, SEQ=SEQ, DB=DB, DS=DS)
    for nm in ("nkv", "fkv", "lf", "win", "q", "z", "g"):
        out[nm + "_p"] = cat(f"o_{nm}_p")
        out[nm + "_s"] = r0[f"o_{nm}_s"]
    return out


NEG = -30000.0
SCALE = 128 ** -0.5


def build_fox(SEQ, DB, NPG, NPOOL):
    NTS = SEQ // 128
    B = 2
    nc = bass.Bass("TRN2", target_bir_lowering=False)
    dt = nc.dram_tensor
    qT_p = dt("qT_p", [B, 128, SEQ], BF16, kind="ExternalInput").ap()
    kT_p = dt("kT_p", [B, 128, SEQ], F32, kind="ExternalInput").ap()
    v_p = dt("v_p", [B, SEQ, 128], F32, kind="ExternalInput").ap()
    lfT_p = dt("lfT_p", [128, B * NTS], F32, kind="ExternalInput").ap()
    qT_s = dt("qT_s", [128, 128], BF16, kind="ExternalInput").ap()
    kTn = dt("kTn", [128, 128], F32, kind="ExternalInput").ap()
    vn = dt("vn", [128, 128], F32, kind="ExternalInput").ap()
    lfn = dt("lfn", [DB, 4], F32, kind="ExternalInput").ap()
    kvpool = dt("kvpool", [NPOOL * 128, 256], F32, kind="ExternalInput").ap()
    lfpool = dt("lfpool", [NPOOL, 128], F32, kind="ExternalInput").ap()
    ptab = dt("ptab", [DB, NPG], mybir.dt.int32, kind="ExternalInput").ap()
    c_tri = dt("c_tri", [128, 128], F32, kind="ExternalInput").ap()
    c_sut = dt("c_sut", [128, 128], F32, kind="ExternalInput").ap()
    c_cneg = dt("c_cneg", [128, 128], F32, kind="ExternalInput").ap()
    c_id = dt("c_id", [128, 128], F32, kind="ExternalInput").ap()
    c_iota = dt("c_iota", [128, 1], F32, kind="ExternalInput").ap()
    c_n4 = dt("c_n4", [4, 4], F32, kind="ExternalInput").ap()
    c_tri4 = dt("c_tri4", [4, 4], F32, kind="ExternalInput").ap()
    o_p = dt("o_p", [B, SEQ, 128], F32, kind="ExternalOutput").ap()
    o_s = dt("o_s", [128, 128], F32, kind="ExternalOutput").ap()

    with ExitStack() as st:
        S = Sched(nc, st)
        tri = S.sb("tri", [128, 128], F32); sut = S.sb("sut", [128, 128], F32)
        cnf = S.sb("cnf", [128, 128], F32); cnb = S.sb("cnb", [128, 128], BF16)
        idf = S.sb("idf", [128, 128], F32); idb = S.sb("idb", [128, 128], BF16)
        iot = S.sb("iot", [128, 1], F32); ones = S.sb("ones", [128, 128], F32)
        n4f = S.sb("n4f", [4, 4], F32); n4b = S.sb("n4b", [4, 4], BF16); tri4 = S.sb("tri4", [4, 4], F32)
        b_c = S.buf("consts")
        for t_, a_ in ((tri, c_tri), (sut, c_sut), (cnf, c_cneg), (idf, c_id), (iot, c_iota), (n4f, c_n4), (tri4, c_tri4)):
            S.dma("sp", t_[:], a_, w=[b_c])
        S.op("dve", lambda e: e.tensor_copy(out=cnb[:], in_=cnf[:]), r=[b_c], w=[b_c])
        S.op("dve", lambda e: e.tensor_copy(out=idb[:], in_=idf[:]), r=[b_c], w=[b_c])
        S.op("dve", lambda e: e.tensor_copy(out=n4b[:], in_=n4f[:]), r=[b_c], w=[b_c])
        S.op("dve", lambda e: e.memset(ones[:], 1.0), w=[b_c])

        pst = [S.ps(f"pst{i}", [128, 128], F32) for i in range(2)]; b_pst = [S.buf() for _ in range(2)]
        pso = [S.ps(f"pso{i}", [128, 132], F32) for i in range(2)]; b_pso = [S.buf() for _ in range(2)]
        psd = S.ps("psd", [128, 128], F32); b_psd = S.buf()
        psd2 = S.ps("psd2", [128, 128], F32); b_psd2 = S.buf()
        ptl = [S.sb(f"ptl{i}", [128, 128], BF16) for i in range(3)]; b_ptl = [S.buf() for _ in range(3)]
        osb = [S.sb(f"osb{i}", [128, 132], F32) for i in range(2)]; b_osb = [S.buf() for _ in range(2)]
        cnt = dict(pst=0, ptl=0, pso=0, osb=0)

        kT = S.sb("kT", [128, SEQ], BF16); b_kT = S.buf()
        qT = S.sb("qT", [128, SEQ], BF16); b_qT = S.buf()
        vx = S.sb("vx", [128, NTS, 132], BF16); b_vx = S.buf()
        lf = S.sb("lf", [128, B * NTS], F32); b_lf = S.buf()
        Dm = S.sb("Dm", [128, NTS], F32); b_Dm = S.buf()
        Of = S.sb("Of", [128, NTS], F32); b_Of = S.buf()
        tot = S.sb("tot", [128, 128], F32); b_tot = S.buf()
        bq = S.sb("bq", [128, NTS], F32); b_bq = S.buf()
        totc = S.sb("totc", [128, 2], F32); b_totc = S.buf()
        S.dma("sp", lf[:], lfT_p, w=[b_lf])
        S.op("dve", lambda e: e.memset(vx[:, :, 128:129], 1.0), w=[b_vx])

        def attend(kt_ap, ns, q_ap, nq, v_ap, bias_ap, mask, acc_i, first, last, rd, extra_w=()):
            pi = cnt["pst"] % 2; cnt["pst"] += 1
            li = cnt["ptl"] % 3; cnt["ptl"] += 1
            S.op("pe", lambda e: e.matmul(pst[pi][0:ns, 0:nq], lhsT=kt_ap, rhs=q_ap, start=True, stop=(mask is None)),
                 r=rd, w=[b_pst[pi]])
            if mask is not None:
                S.op("pe", lambda e: e.matmul(pst[pi][0:ns, 0:nq], lhsT=mask[0], rhs=mask[1], start=False, stop=True),
                     r=[b_c], w=[b_pst[pi]])
            S.op("act", lambda e: e.activation(out=ptl[li][0:ns, 0:nq], in_=pst[pi][0:ns, 0:nq], func=AF.Exp,
                                               bias=bias_ap, scale=SCALE),
                 r=[b_pst[pi]] + rd, w=[b_ptl[li]])
            S.op("pe", lambda e: e.matmul(pso[acc_i][0:nq, 0:129], lhsT=ptl[li][0:ns, 0:nq], rhs=v_ap,
                                          start=first, stop=last),
                 r=[b_ptl[li]] + rd, w=[b_pso[acc_i]])

        def finish(acc_i, nq, out_ap):
            oi = cnt["osb"] % 2; cnt["osb"] += 1
            S.op("dve", lambda e: e.reciprocal(out=osb[oi][0:nq, 129:130], in_=pso[acc_i][0:nq, 128:129]),
                 r=[b_pso[acc_i]], w=[b_osb[oi]])
            S.op("dve", lambda e: e.tensor_scalar_mul(out=osb[oi][0:nq, 0:128], in0=pso[acc_i][0:nq, 0:128],
                                                      scalar1=osb[oi][0:nq, 129:130]),
                 r=[b_pso[acc_i], b_osb[oi]], w=[b_osb[oi]])
            S.dma("sp", out_ap, osb[oi][0:nq, 0:128], r=[b_osb[oi]])

        def cumsum(lf_ap, nt, D_t, O_t):
            S.op("pe", lambda e: e.matmul(psd[0:nt, 0:1], lhsT=lf_ap, rhs=ones[:, 0:1], start=True, stop=True),
                 r=[b_lf, b_c], w=[b_psd])
            S.op("dve", lambda e: e.tensor_copy(out=totc[0:nt, 0:1], in_=psd[0:nt, 0:1]), r=[b_psd], w=[b_totc])
            S.op("dve", lambda e: e.tensor_scalar_mul(out=tot[0:nt, :], in0=ones[0:nt, :], scalar1=totc[0:nt, 0:1]),
                 r=[b_totc, b_c], w=[b_tot])
            S.op("pe", lambda e: e.matmul(psd2[:, 0:nt], lhsT=tot[0:nt, :], rhs=sut[0:nt, 0:nt], start=True, stop=True),
                 r=[b_tot, b_c], w=[b_psd2])
            S.op("dve", lambda e: e.tensor_copy(out=O_t, in_=psd2[:, 0:nt]), r=[b_psd2], w=[b_Of])
            S.op("pe", lambda e: e.matmul(psd[:, 0:nt], lhsT=tri[:], rhs=lf_ap, start=True, stop=True),
                 r=[b_lf, b_c, b_tot], w=[b_psd])
            S.op("dve", lambda e: e.tensor_tensor(out=D_t, in0=psd[:, 0:nt], in1=O_t, op=ALU.add),
                 r=[b_psd, b_Of], w=[b_Dm])

        for b in range(B):
            S.dma("pool", kT[:], kT_p[b], w=[b_kT])
            S.dma("sp", qT[:], qT_p[b], w=[b_qT])
            S.dma("pool", vx[:, :, 0:128], v_p[b].rearrange("(n p) d -> p n d", p=128), w=[b_vx])
            cumsum(lf[:, b * NTS:(b + 1) * NTS], NTS, Dm[:, 0:NTS], Of[:, 0:NTS])
            for qi in range(NTS):
                S.op("dve", lambda e, qi=qi: e.tensor_scalar(out=bq[:, 0:qi + 1], in0=Dm[:, 0:qi + 1], scalar1=-1.0,
                                                             scalar2=Of[:, qi:qi + 1], op0=ALU.mult, op1=ALU.add),
                     r=[b_Dm, b_Of], w=[b_bq])
                ai = cnt["pso"] % 2; cnt["pso"] += 1
                for ki in range(qi + 1):
                    attend(kT[:, ki * 128:(ki + 1) * 128], 128, qT[:, qi * 128:(qi + 1) * 128], 128,
                           vx[:, ki, 0:129], bq[:, ki:ki + 1],
                           (idb[:], cnb[:]) if ki == qi else None, ai, ki == 0, ki == qi,
                           [b_kT, b_qT, b_vx, b_bq])
                finish(ai, 128, o_p[b, qi * 128:(qi + 1) * 128, :])
        NK = NPG + 1
        assert NPG <= 128
        I32 = mybir.dt.int32
        pti = S.sb("pti", [128, NPG], I32); b_pti = S.buf()
        ptf = S.sb("ptf", [128, NPG], F32); b_ptf = S.buf()
        idx = [S.sb(f"idx{i}", [128, NPG], I32) for i in range(2)]; b_idx = [S.buf() for _ in range(2)]
        pcol = [S.sb(f"pcol{i}", [128, 1], I32) for i in range(2)]; b_pcol = [S.buf() for _ in range(2)]
        lfP = [S.sb(f"lfP{i}", [128, 128], F32) for i in range(2)]; b_lfP = [S.buf() for _ in range(2)]
        lfS = S.sb("lfS", [128, NK], F32)
        Ds = S.sb("Ds", [128, NK], F32); Os = S.sb("Os", [128, NK], F32); bqs = S.sb("bqs", [128, NK], F32)
        gt_ = [S.sb(f"gth{i}", [128, 256], F32) for i in range(4)]; b_gt = [S.buf() for _ in range(4)]
        kb = [S.sb(f"kb{i}", [128, 128], BF16) for i in range(3)]; b_kb = [S.buf() for _ in range(3)]
        vb = [S.sb(f"vb{i}", [128, 132], BF16) for i in range(3)]; b_vb = [S.buf() for _ in range(3)]
        qs = S.sb("qs", [128, 128], BF16); b_qs = S.buf()
        knf = S.sb("knf", [128, 128], F32); knb = S.sb("knb", [128, 128], BF16); b_kn = S.buf()
        vn4 = [S.sb(f"vn4{i}", [4, 132], BF16) for i in range(2)]; b_vn4 = [S.buf() for _ in range(2)]
        S.dma("sp", qs[:], qT_s, w=[b_qs])
        S.dma("sp", knf[:], kTn, w=[b_kn])
        S.op("dve", lambda e: e.tensor_copy(out=knb[:], in_=knf[:]), r=[b_kn], w=[b_kn])
        for i in range(3):
            S.op("dve", lambda e, i=i: e.memset(vb[i][:, 128:129], 1.0), w=[b_vb[i]])
        for i in range(2):
            S.op("dve", lambda e, i=i: e.memset(vn4[i][:, 128:129], 1.0), w=[b_vn4[i]])
        ng = 0
        for b in range(DB):
            ii = b % 2
            S.dma("sp", pti[:], ptab[b:b + 1, :].partition_broadcast(128), w=[b_pti])
            S.op("dve", lambda e: e.tensor_copy(out=ptf[:], in_=pti[:]), r=[b_pti], w=[b_ptf])
            S.op("dve", lambda e: e.tensor_scalar(out=ptf[:], in0=ptf[:], scalar1=128.0, scalar2=iot[:, 0:1],
                                                  op0=ALU.mult, op1=ALU.add), r=[b_ptf, b_c], w=[b_ptf])
            S.op("dve", lambda e, ii=ii: e.tensor_copy(out=idx[ii][:], in_=ptf[:]), r=[b_ptf], w=[b_idx[ii]])
            S.dma("sp", pcol[ii][0:NPG, :], ptab[b:b + 1, :].rearrange("o n -> n o"), w=[b_pcol[ii]])
            S.op("pool", lambda e, ii=ii: e.indirect_dma_start(
                out=lfP[ii][0:NPG, :], out_offset=None, in_=lfpool,
                in_offset=bass.IndirectOffsetOnAxis(ap=pcol[ii][0:NPG, 0:1], axis=0)),
                 r=[b_pcol[ii]], w=[b_lfP[ii]], dma=True)
            S.op("pe", lambda e, ii=ii: e.transpose(out=psd2[:, 0:NPG], in_=lfP[ii][0:NPG, :], identity=idf[0:NPG, 0:NPG]),
                 r=[b_lfP[ii], b_c], w=[b_psd2])
            S.op("dve", lambda e: e.memset(lfS[:, NPG:NK], 0.0), w=[b_lf])
            S.op("dve", lambda e: e.tensor_copy(out=lfS[:, 0:NPG], in_=psd2[:, 0:NPG]), r=[b_psd2], w=[b_lf])
            S.dma("sp", lfS[0:4, NPG:NK], lfn[b:b + 1, :].rearrange("o t -> t o"), w=[b_lf])
            cumsum(lfS[:, 0:NK], NK, Ds[:, 0:NK], Os[:, 0:NK])
            S.op("dve", lambda e: e.tensor_scalar(out=bqs[:, 0:NK], in0=Ds[:, 0:NK], scalar1=-1.0,
                                                  scalar2=Os[:, NPG:NK], op0=ALU.mult, op1=ALU.add),
                 r=[b_Dm, b_Of], w=[b_bq])
            vi = b % 2
            S.dma("pool", vn4[vi][0:4, 0:128], vn[b * 4:(b + 1) * 4, :], w=[b_vn4[vi]])
            ai = cnt["pso"] % 2; cnt["pso"] += 1
            qcol = qs[:, b * 4:(b + 1) * 4]
            for j in range(NPG):
                gi = ng % 4; ki_ = ng % 3; ng += 1
                S.op("pool", lambda e, gi=gi, ii=ii, j=j: e.indirect_dma_start(
                    out=gt_[gi][:], out_offset=None, in_=kvpool,
                    in_offset=bass.IndirectOffsetOnAxis(ap=idx[ii][:, j:j + 1], axis=0)),
                     r=[b_idx[ii]], w=[b_gt[gi]], dma=True)
                S.op("act", lambda e, gi=gi, ki_=ki_: e.copy(out=kb[ki_][:], in_=gt_[gi][:, 0:128]),
                     r=[b_gt[gi]], w=[b_kb[ki_]])
                S.op("dve", lambda e, gi=gi, ki_=ki_: e.tensor_copy(out=vb[ki_][:, 0:128], in_=gt_[gi][:, 128:256]),
                     r=[b_gt[gi]], w=[b_vb[ki_]])
                attend(kb[ki_][:], 128, qcol, 4, vb[ki_][:, 0:129], bqs[:, j:j + 1], None, ai, j == 0, False,
                       [b_kb[ki_], b_vb[ki_], b_qs, b_bq])
            attend(knb[:, b * 4:(b + 1) * 4], 4, qcol, 4, vn4[vi][0:4, 0:129], bqs[0:4, NPG:NK],
                   (idb[0:4, 0:4], n4b[0:4, 0:4]), ai, False, True, [b_kn, b_vn4[vi], b_qs, b_bq])
            finish(ai, 4, o_s[b * 4:(b + 1) * 4, :])
        S.emit()
    return nc


def _fox_consts():
    f32 = np.float32
    i = np.arange(128)
    return dict(c_tri=(i[:, None] <= i[None, :]).astype(f32), c_sut=(i[:, None] < i[None, :]).astype(f32),
                c_cneg=np.where(i[:, None] > i[None, :], NEG, 0.0).astype(f32), c_id=np.eye(128, dtype=f32),
                c_iota=i.astype(f32)[:, None],
                c_n4=np.where(np.arange(4)[:, None] > np.arange(4)[None, :], NEG, 0.0).astype(f32),
                c_tri4=(np.arange(4)[:, None] <= np.arange(4)[None, :]).astype(f32))


def launch_fox(L1, cache_fox_kv, cache_fox_logf, page_table):
    f32 = np.float32
    B, SEQ, DB, DS = L1["B"], L1["SEQ"], L1["DB"], L1["DS"]
    NTS = SEQ // 128
    ckv = np.asarray(cache_fox_kv)[0]
    clf = np.asarray(cache_fox_logf)[0]
    NPOOL = ckv.shape[0]
    pt = np.ascontiguousarray(np.asarray(page_table, np.int32))
    NPG = pt.shape[1]
    nc = build_fox(SEQ, DB, NPG, NPOOL)
    consts = _fox_consts()
    fkv_p = L1["fkv_p"].reshape(B, SEQ, 2, 8, 128)
    fkv_s = L1["fkv_s"].reshape(DB * DS, 2, 8, 128)
    q_p = L1["q_p"].reshape(B, SEQ, 16, 128)
    q_s = L1["q_s"].reshape(DB * DS, 16, 128)
    in_maps = []
    for c in range(8):
        m = dict(consts)
        m["qT_p"] = np.ascontiguousarray(q_p[:, :, 8 + c, :].transpose(0, 2, 1))
        m["kT_p"] = np.ascontiguousarray(fkv_p[:, :, 0, c, :].transpose(0, 2, 1))
        m["v_p"] = np.ascontiguousarray(fkv_p[:, :, 1, c, :])
        m["lfT_p"] = np.ascontiguousarray(L1["lf_p"][:, :, c].reshape(B * NTS, 128).T)
        m["qT_s"] = np.ascontiguousarray(q_s[:, 8 + c, :].T)
        m["kTn"] = np.ascontiguousarray(fkv_s[:, 0, c, :].T)
        m["vn"] = np.ascontiguousarray(fkv_s[:, 1, c, :])
        m["lfn"] = np.ascontiguousarray(L1["lf_s"][:, c].reshape(DB, DS))
        m["kvpool"] = np.ascontiguousarray(np.concatenate(
            [ckv[:, :, 0, c, :].transpose(0, 2, 1), ckv[:, :, 1, c, :]], axis=2)).reshape(NPOOL * 128, 256)
        m["lfpool"] = np.ascontiguousarray(clf[:, :, c])
        m["ptab"] = pt
        in_maps.append(m)
    res = run_bass_kernel_spmd(nc, in_maps, core_ids=list(range(8))).results
    o_p = np.stack([res[c]["o_p"] for c in range(8)], axis=2)
    o_s = np.stack([res[c]["o_s"] for c in range(8)], axis=1)
    return o_p, o_s


def build_nsa_p(SEQ):
    B = 2
    NTS = SEQ // 128
    NB = SEQ // 32
    NSEL = SEQ // 64
    assert NB <= 128 and NSEL <= 128
    nc = bass.Bass("TRN2", target_bir_lowering=False)
    dt = nc.dram_tensor
    qg = dt("qg", [B, 4, 128, SEQ], BF16, kind="ExternalInput").ap()
    qoT = dt("qoT", [B, 128, SEQ], BF16, kind="ExternalInput").ap()
    c_cbo = dt("c_cbo", [128, NTS], F32, kind="ExternalInput").ap()
    kcin = dt("kcin", [B, 2, 32, 128, NB], F32, kind="ExternalInput").ap()
    pe_in = dt("pe_in", [2, 128, 32], F32, kind="ExternalInput").ap()
    cw = dt("cw", [2, 128, 32, 128], F32, kind="ExternalInput").ap()
    ksT = dt("ksT", [B, 128, SEQ], F32, kind="ExternalInput").ap()
    kwT = dt("kwT", [B, 128, SEQ], F32, kind="ExternalInput").ap()
    vs = dt("vs", [B, SEQ, 128], F32, kind="ExternalInput").ap()
    vw = dt("vw", [B, SEQ, 128], F32, kind="ExternalInput").ap()
    gts = dt("gts", [B, SEQ, 3], F32, kind="ExternalInput").ap()
    c_id = dt("c_id", [128, 128], F32, kind="ExternalInput").ap()
    c_cneg = dt("c_cneg", [128, 128], F32, kind="ExternalInput").ap()
    c_wneg = dt("c_wneg", [128, 128], F32, kind="ExternalInput").ap()
    c_cm = dt("c_cm", [128, SEQ], F32, kind="ExternalInput").ap()
    c_cb = dt("c_cb", [128, NTS * 4], F32, kind="ExternalInput").ap()
    c_ab = dt("c_ab", [128, NTS], F32, kind="ExternalInput").ap()
    c_A = dt("c_A", [128, NSEL + 1], F32, kind="ExternalInput").ap()
    c_TA = dt("c_TA", [128, NTS * NSEL], F32, kind="ExternalInput").ap()
    c_TB = dt("c_TB", [128, NTS * NSEL], F32, kind="ExternalInput").ap()
    c_E = dt("c_E", [128, NTS * 128], F32, kind="ExternalInput").ap()
    o_n = dt("o_n", [B, SEQ, 128], F32, kind="ExternalOutput").ap()
    o_sel = dt("o_sel", [B, SEQ, NSEL], F32, kind="ExternalOutput").ap()

    with ExitStack() as st:
        S = Sched(nc, st)
        b_c = S.buf("consts")

        def cst(name, ap, shape, to_bf=False):
            t = S.sb(name, shape, F32)
            S.dma("sp", t[:], ap, w=[b_c])
            if to_bf:
                tb = S.sb(name + "b", shape, BF16)
                S.op("dve", lambda e: e.tensor_copy(out=tb[:], in_=t[:]), r=[b_c], w=[b_c])
                return tb
            return t
        idf = cst("idf", c_id, [128, 128])
        idb = S.sb("idb", [128, 128], BF16)
        S.op("dve", lambda e: e.tensor_copy(out=idb[:], in_=idf[:]), r=[b_c], w=[b_c])
        cnb = cst("cn", c_cneg, [128, 128], True)
        wnb = cst("wn", c_wneg, [128, 128], True)
        cb = cst("cb", c_cb, [128, NTS * 4])
        ab = cst("ab", c_ab, [128, NTS])
        Am = cst("Am", c_A, [128, NSEL + 1])
        TA = cst("TA", c_TA, [128, NTS * NSEL])
        TB = cst("TB", c_TB, [128, NTS * NSEL])
        cbo = cst("cbo", c_cbo, [128, NTS])
        cmb = S.sb("cmb", [128, SEQ], BF16)
        S.dma("pool", cmb[:], c_cm, w=[b_c])
        Eb = S.sb("Eb", [128, NTS * 128], BF16)
        S.dma("pool", Eb[:], c_E, w=[b_c])
        cwb = S.sb("cwb", [128, 2, 32, 128], BF16)
        for kv in range(2):
            S.dma("pool", cwb[:, kv], cw[kv], w=[b_c])
        peT = S.sb("peT", [128, 2, 32], F32)
        for kv in range(2):
            S.dma("sp", peT[:, kv, :], pe_in[kv], w=[b_c])

        qT = S.sb("qT", [128, 4, SEQ], BF16); b_q = S.buf()
        qo = S.sb("qo", [128, SEQ], BF16); b_qo = S.buf()
        ks = S.sb("ks", [128, SEQ], BF16); kw = S.sb("kw", [128, SEQ], BF16); b_k = S.buf()
        vsx = S.sb("vsx", [128, NTS, 132], BF16); vwx = S.sb("vwx", [128, NTS, 132], BF16); b_v = S.buf()
        kin = [S.sb(f"kin{i}", [128, NB], F32) for i in range(2)]; b_kin = [S.buf() for _ in range(2)]
        kib = [S.sb(f"kib{i}", [128, NB], BF16) for i in range(2)]; b_kib = [S.buf() for _ in range(2)]
        kcb = S.sb("kcb", [128, 128], BF16); b_kcb = S.buf()
        kcT = S.sb("kcT", [128, 128], BF16); b_kcT = S.buf()
        vcx = S.sb("vcx", [128, 132], BF16); b_vcx = S.buf()
        gt = S.sb("gt", [128, NTS, 3], F32); b_gt = S.buf()
        S.op("dve", lambda e: e.memset(vsx[:, :, 128:129], 1.0), w=[b_v])
        S.op("dve", lambda e: e.memset(vwx[:, :, 128:129], 1.0), w=[b_v])
        S.op("dve", lambda e: e.memset(vcx[:, 128:129], 1.0), w=[b_vcx])

        pst = [S.ps(f"pst{i}", [128, 128], F32) for i in range(2)]; b_pst = [S.buf() for _ in range(2)]
        pO = {k: S.ps("pO" + k, [128, 132], F32) for k in "csw"}; b_pO = {k: S.buf() for k in "csw"}
        psc = S.ps("psc", [128, NSEL + 1], F32); b_psc = S.buf()
        pmi = S.ps("pmi", [128, 128], F32); b_pmi = S.buf()
        pmb = S.ps("pmb", [128, 128], BF16); b_pmb = S.buf()
        ptl = [S.sb(f"ptl{i}", [128, 128], BF16) for i in range(3)]; b_ptl = [S.buf() for _ in range(3)]
        ptf = [S.sb(f"ptf{i}", [128, 128], F32) for i in range(2)]; b_ptf = [S.buf() for _ in range(2)]
        sc = S.sb("sc", [128, NSEL], F32); b_sc = S.buf()
        sw = S.sb("sw", [128, NSEL], F32); b_sw = S.buf()
        sw2 = S.sb("sw2", [128, NSEL], F32); b_sw2 = S.buf()
        mx = S.sb("mx", [128, 16], F32); b_mx = S.buf()
        rr = S.sb("rr", [128, 8], F32); b_rr = S.buf()
        selb = S.sb("selb", [128, NSEL], BF16); b_selb = S.buf()
        self_ = S.sb("self", [128, NSEL], F32); b_self = S.buf()
        negT = S.sb("negT", [128, 128], BF16); b_negT = S.buf()
        oc = S.sb("oc", [128, 132], F32); b_oc = S.buf()
        cf = S.sb("cf", [128, 8], F32); b_cf = S.buf()
        cnt = dict(pst=0, ptl=0, ptf=0)

        def unit(kt_ap, ns, q_ap, v_ap, bias_ap, masks, okey, first, last, rd):
            pi = cnt["pst"] % 2; cnt["pst"] += 1
            li = cnt["ptl"] % 3; cnt["ptl"] += 1
            nm = len(masks)
            S.op("pe", lambda e: e.matmul(pst[pi][0:ns, :], lhsT=kt_ap, rhs=q_ap, start=True, stop=(nm == 0)),
                 r=rd, w=[b_pst[pi]])
            for mi, (ml, mr, mrd) in enumerate(masks):
                S.op("pe", lambda e, ml=ml, mr=mr, mi=mi: e.matmul(pst[pi][0:ns, :], lhsT=ml, rhs=mr, start=False,
                                                                    stop=(mi == nm - 1)), r=[b_c] + mrd, w=[b_pst[pi]])
            S.op("act", lambda e: e.activation(out=ptl[li][0:ns, :], in_=pst[pi][0:ns, :], func=AF.Exp,
                                               bias=bias_ap, scale=SCALE), r=[b_pst[pi], b_c], w=[b_ptl[li]])
            S.op("pe", lambda e: e.matmul(pO[okey][:, 0:129], lhsT=ptl[li][0:ns, :], rhs=v_ap, start=first, stop=last),
                 r=[b_ptl[li]] + rd, w=[b_pO[okey]])

        for b in range(B):
            S.dma("sp", qT[:], qg[b].rearrange("h d s -> d h s"), w=[b_q])
            S.dma("sp", qo[:], qoT[b], w=[b_qo])
            S.dma("pool", ks[:], ksT[b], w=[b_k])
            S.dma("pool", kw[:], kwT[b], w=[b_k])
            S.dma("pool", vsx[:, :, 0:128], vs[b].rearrange("(n p) d -> p n d", p=128), w=[b_v])
            S.dma("pool", vwx[:, :, 0:128], vw[b].rearrange("(n p) d -> p n d", p=128), w=[b_v])
            S.dma("sp", gt[:], gts[b].rearrange("(n p) g -> p n g", p=128), w=[b_gt])
            for kv in range(2):
                for c in range(32):
                    i2 = c % 2
                    S.dma("sp", kin[i2][:], kcin[b, kv, c], w=[b_kin[i2]])
                    S.op("dve", lambda e, i2=i2, kv=kv, c=c: e.tensor_scalar_add(out=kib[i2][:], in0=kin[i2][:],
                                                                                 scalar1=peT[:, kv, c:c + 1]),
                         r=[b_kin[i2], b_c], w=[b_kib[i2]])
                    S.op("pe", lambda e, i2=i2, kv=kv, c=c: e.matmul(pmi[0:NB, :], lhsT=kib[i2][:], rhs=cwb[:, kv, c, :],
                                                                     start=(c == 0), stop=(c == 31)),
                         r=[b_kib[i2], b_c], w=[b_pmi])
                if kv == 0:
                    S.op("dve", lambda e: e.tensor_copy(out=kcb[0:NB, :], in_=pmi[0:NB, :]), r=[b_pmi], w=[b_kcb])
                    S.op("pe", lambda e: e.transpose(out=pmb[:, 0:NB], in_=kcb[0:NB, :], identity=idb[0:NB, 0:NB]),
                         r=[b_kcb, b_c], w=[b_pmb])
                    S.op("dve", lambda e: e.tensor_copy(out=kcT[:, 0:NB], in_=pmb[:, 0:NB]), r=[b_pmb], w=[b_kcT])
                else:
                    S.op("dve", lambda e: e.tensor_copy(out=vcx[0:NB, 0:128], in_=pmi[0:NB, :]), r=[b_pmi], w=[b_vcx])

            for qi in range(NTS):
                qs_ = slice(qi * 128, (qi + 1) * 128)
                for hh in range(4):
                    pi = cnt["pst"] % 2; cnt["pst"] += 1
                    fi = cnt["ptf"] % 2; cnt["ptf"] += 1
                    q_ap = qT[:, hh, qs_]
                    cm_ap = cmb[0:NB, qs_]
                    cb_ap = cb[0:NB, qi * 4 + hh:qi * 4 + hh + 1]
                    S.op("pe", lambda e, pi=pi, q_ap=q_ap: e.matmul(pst[pi][0:NB, :], lhsT=kcT[:, 0:NB], rhs=q_ap,
                                                                    start=True, stop=False), r=[b_kcT, b_q], w=[b_pst[pi]])
                    S.op("pe", lambda e, pi=pi, cm_ap=cm_ap: e.matmul(pst[pi][0:NB, :], lhsT=idb[0:NB, 0:NB], rhs=cm_ap,
                                                                      start=False, stop=True), r=[b_c], w=[b_pst[pi]])
                    S.op("act", lambda e, pi=pi, fi=fi, cb_ap=cb_ap: e.activation(
                        out=ptf[fi][0:NB, :], in_=pst[pi][0:NB, :], func=AF.Exp,
                        bias=cb_ap, scale=SCALE), r=[b_pst[pi], b_c], w=[b_ptf[fi]])
                    S.op("pe", lambda e, fi=fi: e.matmul(psc[:, :], lhsT=ptf[fi][0:NB, :], rhs=Am[0:NB, :],
                                                         start=True, stop=True), r=[b_ptf[fi], b_c], w=[b_psc])
                    S.op("dve", lambda e: e.tensor_scalar_max(out=rr[:, 0:1], in0=psc[:, NSEL:NSEL + 1], scalar1=1e-30),
                         r=[b_psc], w=[b_rr])
                    S.op("dve", lambda e: e.reciprocal(out=rr[:, 1:2], in_=rr[:, 0:1]), r=[b_rr], w=[b_rr])
                    if hh == 0:
                        S.op("dve", lambda e: e.tensor_scalar_mul(out=sc[:], in0=psc[:, 0:NSEL], scalar1=rr[:, 1:2]),
                             r=[b_psc, b_rr], w=[b_sc])
                    else:
                        S.op("dve", lambda e: e.scalar_tensor_tensor(out=sc[:], in0=psc[:, 0:NSEL], scalar=rr[:, 1:2],
                                                                     in1=sc[:], op0=ALU.mult, op1=ALU.add),
                             r=[b_psc, b_rr, b_sc], w=[b_sc])
                unit(kcT[:, 0:NB], NB, qo[:, qs_], vcx[0:NB, 0:129], cbo[0:NB, qi:qi + 1],
                     [(idb[0:NB, 0:NB], cmb[0:NB, qs_], [])], "c", True, True, [b_kcT, b_qo, b_vcx])
                so = slice(qi * NSEL, (qi + 1) * NSEL)
                S.op("dve", lambda e, so=so: e.tensor_tensor(out=sw[:], in0=sc[:], in1=TA[:, so], op=ALU.mult),
                     r=[b_sc, b_c], w=[b_sw])
                S.op("dve", lambda e, so=so: e.tensor_tensor(out=sw[:], in0=sw[:], in1=TB[:, so], op=ALU.add),
                     r=[b_sw, b_c], w=[b_sw])
                S.op("dve", lambda e: e.max(out=mx[:, 0:8], in_=sw[:]), r=[b_sw], w=[b_mx])
                S.op("dve", lambda e: e.match_replace(out=sw2[:], in_to_replace=mx[:, 0:8], in_values=sw[:], imm_value=-2.0),
                     r=[b_sw, b_mx], w=[b_sw2])
                S.op("dve", lambda e: e.max(out=mx[:, 8:16], in_=sw2[:]), r=[b_sw2], w=[b_mx])
                S.op("dve", lambda e: e.tensor_scalar(out=self_[:], in0=sw[:], scalar1=mx[:, 15:16], scalar2=None, op0=ALU.is_ge),
                     r=[b_sw, b_mx], w=[b_self])
                S.op("dve", lambda e: e.tensor_scalar(out=sw2[:], in0=sw[:], scalar1=0.0, scalar2=None, op0=ALU.is_ge),
                     r=[b_sw], w=[b_sw2])
                S.op("dve", lambda e: e.tensor_tensor(out=self_[:], in0=self_[:], in1=sw2[:], op=ALU.mult),
                     r=[b_self, b_sw2], w=[b_self])
                S.dma("sp", o_sel[b, qs_, :], self_[:], r=[b_self])
                S.op("dve", lambda e: e.tensor_copy(out=selb[:], in_=self_[:]), r=[b_self], w=[b_selb])
                S.op("pe", lambda e: e.transpose(out=pmb[0:NSEL, :], in_=selb[:], identity=idb[:]), r=[b_selb, b_c], w=[b_pmb])
                S.op("dve", lambda e: e.tensor_scalar(out=negT[0:NSEL, :], in0=pmb[0:NSEL, :], scalar1=-1.0, scalar2=-NEG,
                                                      op0=ALU.add, op1=ALU.mult), r=[b_pmb], w=[b_negT])
                for ki in range(qi + 1):
                    masks = [(Eb[0:NSEL, ki * 128:(ki + 1) * 128], negT[0:NSEL, :], [b_negT])]
                    if ki == qi:
                        masks.append((idb[:], cnb[:], []))
                    unit(ks[:, ki * 128:(ki + 1) * 128], 128, qo[:, qs_], vsx[:, ki, 0:129],
                         ab[:, ki - qi + NTS - 1:ki - qi + NTS], masks, "s", ki == 0, ki == qi, [b_k, b_qo, b_v])
                k0 = max(0, qi - 4)
                for ki in range(k0, qi + 1):
                    masks = []
                    if ki == qi - 4:
                        masks.append((idb[:], wnb[:], []))
                    if ki == qi:
                        masks.append((idb[:], cnb[:], []))
                    unit(kw[:, ki * 128:(ki + 1) * 128], 128, qo[:, qs_], vwx[:, ki, 0:129],
                         ab[:, ki - qi + NTS - 1:ki - qi + NTS], masks, "w", ki == k0, ki == qi, [b_k, b_qo, b_v])
                for bi, key in enumerate("csw"):
                    S.op("dve", lambda e, key=key, bi=bi: e.tensor_scalar_max(out=cf[:, bi:bi + 1], in0=pO[key][:, 128:129],
                                                                             scalar1=1e-30), r=[b_pO[key]], w=[b_cf])
                S.op("dve", lambda e: e.reciprocal(out=cf[:, 3:6], in_=cf[:, 0:3]), r=[b_cf], w=[b_cf])
                g_ap = gt[:, qi, :]
                S.op("dve", lambda e, g_ap=g_ap: e.tensor_tensor(out=cf[:, 3:6], in0=cf[:, 3:6], in1=g_ap, op=ALU.mult),
                     r=[b_cf, b_gt], w=[b_cf])
                S.op("dve", lambda e: e.tensor_scalar_mul(out=oc[:, 0:128], in0=pO["c"][:, 0:128], scalar1=cf[:, 3:4]),
                     r=[b_pO["c"], b_cf], w=[b_oc])
                for bi, key in ((1, "s"), (2, "w")):
                    S.op("dve", lambda e, key=key, bi=bi: e.scalar_tensor_tensor(
                        out=oc[:, 0:128], in0=pO[key][:, 0:128], scalar=cf[:, 3 + bi:4 + bi], in1=oc[:, 0:128],
                        op0=ALU.mult, op1=ALU.add), r=[b_pO[key], b_cf, b_oc], w=[b_oc])
                S.dma("sp", o_n[b, qs_, :], oc[:, 0:128], r=[b_oc])
        S.emit()
    return nc


def _nsa_consts(SEQ, head):
    f32 = np.float32
    NTS, NB, NSEL = SEQ // 128, SEQ // 32, SEQ // 64
    p = np.arange(128)
    slopes = np.exp2(-8.0 * np.arange(1, 9, dtype=np.float64) / 8)
    g = head // 4
    c_end = 32 * (p + 1) - 1
    t = np.arange(SEQ)
    cm = np.where(c_end[:, None] > t[None, :], NEG, 0.0).astype(f32)
    cb = np.zeros((128, NTS, 4), f32)
    for qi in range(NTS):
        for hh in range(4):
            cb[:, qi, hh] = slopes[4 * g + hh] * (c_end - 128 * qi)
    cb = np.clip(cb, -1e4, 1e4)
    cbo = np.ascontiguousarray(cb[:, :, head % 4])
    ab = np.zeros((128, NTS), f32)
    for d in range(-(NTS - 1), 1):
        ab[:, d + NTS - 1] = slopes[head] * (128 * d + p)
    A = np.zeros((128, NSEL + 1), f32)
    for n in range(min(128, NB)):
        A[n, n // 2] = 1.0
    A[:, NSEL] = 1.0
    TA = np.zeros((128, NTS, NSEL), f32); TB = np.zeros((128, NTS, NSEL), f32)
    blk = np.arange(NSEL)
    for qi in range(NTS):
        tq = 128 * qi + p
        cur = tq // 64
        vis = blk[None, :] <= cur[:, None]
        f0 = blk[None, :] == 0
        f1 = blk[None, :] == cur[:, None]
        f2 = blk[None, :] == (cur[:, None] - 1)
        forced = f0 | f1 | f2
        TA[:, qi] = (vis & ~forced).astype(f32)
        fv = np.where(f1, 10002.0, np.where(f2, 10001.0, 10000.0))
        TB[:, qi] = np.where(vis, np.where(forced, fv, 0.0), -1.0)
    E = np.zeros((128, NTS, 128), f32)
    for j in range(NTS):
        E[2 * j, j, 0:64] = 1.0
        E[2 * j + 1, j, 64:128] = 1.0
    i = np.arange(128)
    return dict(c_id=np.eye(128, dtype=f32), c_cneg=np.where(i[:, None] > i[None, :], NEG, 0.0).astype(f32),
                c_wneg=np.where(i[:, None] <= i[None, :], NEG, 0.0).astype(f32), c_cm=cm,
                c_cb=cb.reshape(128, NTS * 4), c_cbo=cbo, c_ab=ab, c_A=A, c_TA=TA.reshape(128, -1),
                c_TB=TB.reshape(128, -1), c_E=E.reshape(128, -1))


def launch_nsa_p(L1, cmp_pe, cmp_w):
    f32 = np.float32
    B, SEQ = L1["B"], L1["SEQ"]
    NB = SEQ // 32
    nc = build_nsa_p(SEQ)
    nkv = L1["nkv_p"].reshape(B, SEQ, 4, 2, 128)
    win = L1["win_p"].reshape(B, SEQ, 2, 2, 128)
    q = L1["q_p"].reshape(B, SEQ, 16, 128)
    gates = L1["g_p"].reshape(B, SEQ, 8, 3)
    pe = np.asarray(cmp_pe, f32)[0]
    cwt = np.asarray(cmp_w, f32)[0]
    in_maps = []
    for c in range(8):
        g = c // 4
        m = _nsa_consts(SEQ, c)
        m["qg"] = np.ascontiguousarray(q[:, :, 4 * g:4 * g + 4, :].transpose(0, 2, 3, 1))
        m["qoT"] = np.ascontiguousarray(q[:, :, c, :].transpose(0, 2, 1))
        xc = nkv[:, :, 0:2, g, :].reshape(B, NB, 32, 2, 128)
        m["kcin"] = np.ascontiguousarray(xc.transpose(0, 3, 2, 4, 1))
        m["pe_in"] = np.ascontiguousarray(pe.transpose(0, 2, 1))
        m["cw"] = np.ascontiguousarray(cwt.transpose(0, 2, 1, 3))
        m["ksT"] = np.ascontiguousarray(nkv[:, :, 2, g, :].transpose(0, 2, 1))
        m["vs"] = np.ascontiguousarray(nkv[:, :, 3, g, :])
        m["kwT"] = np.ascontiguousarray(win[:, :, 0, g, :].transpose(0, 2, 1))
        m["vw"] = np.ascontiguousarray(win[:, :, 1, g, :])
        m["gts"] = np.ascontiguousarray(gates[:, :, c, :])
        in_maps.append(m)
    res = run_bass_kernel_spmd(nc, in_maps, core_ids=list(range(8))).results
    o_n = np.stack([res[c]["o_n"] for c in range(8)], axis=2)
    sel = np.stack([res[c]["o_sel"] for c in range(8)], axis=2)
    return o_n, sel


def build_nsa_s(DB, NPG, NPOOL):
    P = NPG * 128
    NB = P // 32
    NBT = (NB + 127) // 128
    NSEL = NPG * 2 + 1
    assert NPG * 2 <= 128
    I32 = mybir.dt.int32
    nc = bass.Bass("TRN2", target_bir_lowering=False)
    dt = nc.dram_tensor
    qg = dt("qg", [128, 4, 128], BF16, kind="ExternalInput").ap()
    qoT = dt("qoT", [128, 128], BF16, kind="ExternalInput").ap()
    pool = dt("pool", [NPOOL * 128, 512], F32, kind="ExternalInput").ap()
    ptab = dt("ptab", [DB, NPG], I32, kind="ExternalInput").ap()
    pe_in = dt("pe_in", [2, 128, 32], F32, kind="ExternalInput").ap()
    cw = dt("cw", [2, 128, 32, 128], F32, kind="ExternalInput").ap()
    ksn = dt("ksn", [128, 128], F32, kind="ExternalInput").ap()
    vsn = dt("vsn", [128, 128], F32, kind="ExternalInput").ap()
    kwn = dt("kwn", [128, 128], F32, kind="ExternalInput").ap()
    vwn = dt("vwn", [128, 128], F32, kind="ExternalInput").ap()
    kwT = dt("kwT", [DB, 128, W_BUF], F32, kind="ExternalInput").ap()
    vw = dt("vw", [DB, W_BUF, 128], F32, kind="ExternalInput").ap()
    gts = dt("gts", [128, 3], F32, kind="ExternalInput").ap()
    c_id = dt("c_id", [128, 128], F32, kind="ExternalInput").ap()
    c_iota = dt("c_iota", [128, 1], F32, kind="ExternalInput").ap()
    c_n4 = dt("c_n4", [4, 4], F32, kind="ExternalInput").ap()
    c_w4 = dt("c_w4", [128, 4], F32, kind="ExternalInput").ap()
    c_cb = dt("c_cb", [128, NBT * 4], F32, kind="ExternalInput").ap()
    c_cbo = dt("c_cbo", [128, NBT], F32, kind="ExternalInput").ap()
    c_ab = dt("c_ab", [128, NPG + 1], F32, kind="ExternalInput").ap()
    c_aw = dt("c_aw", [128, 5], F32, kind="ExternalInput").ap()
    c_A = dt("c_A", [128, NBT * (NSEL + 1)], F32, kind="ExternalInput").ap()
    c_TA = dt("c_TA", [4, NSEL], F32, kind="ExternalInput").ap()
    c_TB = dt("c_TB", [4, NSEL], F32, kind="ExternalInput").ap()
    c_E = dt("c_E", [128, NPG * 128], F32, kind="ExternalInput").ap()
    o_n = dt("o_n", [128, 128], F32, kind="ExternalOutput").ap()
    o_sel = dt("o_sel", [128, NSEL], F32, kind="ExternalOutput").ap()

    with ExitStack() as st:
        S = Sched(nc, st)
        b_c = S.buf("consts")

        def cst(name, ap, shape, to_bf=False, eng="sp"):
            if to_bf:
                tb = S.sb(name + "b", shape, BF16)
                S.dma("pool", tb[:], ap, w=[b_c])
                return tb
            t = S.sb(name, shape, F32)
            S.dma(eng, t[:], ap, w=[b_c])
            return t
        idf = cst("idf", c_id, [128, 128]); idb = cst("id", c_id, [128, 128], True)
        iot = cst("iot", c_iota, [128, 1])
        n4b = cst("n4", c_n4, [4, 4], True); w4b = cst("w4", c_w4, [128, 4], True)
        cb = cst("cb", c_cb, [128, NBT * 4]); cbo = cst("cbo", c_cbo, [128, NBT])
        ab = cst("ab", c_ab, [128, NPG + 1]); aw = cst("aw", c_aw, [128, 5])
        Am = cst("Am", c_A, [128, NBT * (NSEL + 1)])
        TA = cst("TA", c_TA, [4, NSEL]); TB = cst("TB", c_TB, [4, NSEL])
        Eb = cst("E", c_E, [128, NPG * 128], True)
        cwb = S.sb("cwb", [128, 2, 32, 128], BF16)
        for kv in range(2):
            S.dma("pool", cwb[:, kv], cw[kv], w=[b_c])
        peb = S.sb("peb", [128, 2, 32], BF16)
        for kv in range(2):
            S.dma("pool", peb[:, kv, :], pe_in[kv], w=[b_c])
        onesb = S.sb("onesb", [1, 128], BF16)
        S.op("dve", lambda e: e.memset(onesb[:], 1.0), w=[b_c])
        qT = cst("qT", qg, [128, 4, 128], True)
        qo = cst("qo", qoT, [128, 128], True)
        ksnb = cst("ksn", ksn, [128, 128], True); kwnb = cst("kwn", kwn, [128, 128], True)
        gt = cst("gt", gts, [128, 3])

        pst = [S.ps(f"pst{i}", [128, 16], F32) for i in range(2)]; b_pst = [S.buf() for _ in range(2)]
        pO = {k: S.ps("pO" + k, [4, 132], F32) for k in "csw"}; b_pO = {k: S.buf() for k in "csw"}
        psc = S.ps("psc", [4, NSEL + 1], F32); b_psc = S.buf()
        pmi = S.ps("pmi", [128, 128], F32); b_pmi = S.buf()
        pmb = S.ps("pmb", [128, 128], BF16); b_pmb = S.buf()
        cnt = dict(pst=0, ptl=0, ptf=0, g=0)
        ptl = [S.sb(f"ptl{i}", [128, 4], BF16) for i in range(3)]; b_ptl = [S.buf() for _ in range(3)]
        ptf = [S.sb(f"ptf{i}", [128, 4], F32) for i in range(2)]; b_ptf = [S.buf() for _ in range(2)]

        pw = S.sb("pw", [1, 2, 128], BF16)
        for kv in range(2):
            for c in range(32):
                S.op("pe", lambda e, kv=kv, c=c: e.matmul(pmi[0:1, :], lhsT=peb[:, kv, c:c + 1], rhs=cwb[:, kv, c, :],
                                                          start=(c == 0), stop=(c == 31)), r=[b_c], w=[b_pmi])
            S.op("dve", lambda e, kv=kv: e.tensor_copy(out=pw[0:1, kv, :], in_=pmi[0:1, :]), r=[b_pmi], w=[b_c])

        pti = S.sb("pti", [128, NPG], I32); b_pti = S.buf()
        ptf32 = S.sb("ptf32", [128, NPG], F32); b_ptf32 = S.buf()
        idx = [S.sb(f"idx{i}", [128, NPG], I32) for i in range(2)]; b_idx = [S.buf() for _ in range(2)]
        gth = [S.sb(f"gth{i}", [128, 512], F32) for i in range(3)]; b_gth = [S.buf() for _ in range(3)]
        XT = S.sb("XT", [128, 2, P], BF16); b_XT = S.buf()
        KS = S.sb("KS", [128, P], BF16); b_KS = S.buf()
        VS = S.sb("VS", [128, NPG, 132], BF16); b_VS = S.buf()
        kcb = S.sb("kcb", [128, NBT, 128], BF16); b_kcb = S.buf()
        kcT = S.sb("kcT", [128, NBT * 128], BF16); b_kcT = S.buf()
        vcx = S.sb("vcx", [128, NBT, 132], BF16); b_vcx = S.buf()
        kwb = S.sb("kwb", [128, W_BUF], BF16); b_kwb = S.buf()
        vwx = S.sb("vwx", [128, W_BUF // 128, 132], BF16); b_vwx = S.buf()
        vn4 = S.sb("vn4", [4, 2, 132], BF16); b_vn4 = S.buf()
        sc = S.sb("sc", [4, NSEL], F32); b_sc = S.buf()
        sw = S.sb("sw", [4, NSEL], F32); b_sw = S.buf()
        sw2 = S.sb("sw2", [4, NSEL], F32); b_sw2 = S.buf()
        mx = S.sb("mx", [4, 16], F32); b_mx = S.buf()
        rr = S.sb("rr", [4, 8], F32); b_rr = S.buf()
        self_ = S.sb("self", [4, NSEL], F32); b_self = S.buf()
        selb = S.sb("selb", [4, 128], BF16); b_selb = S.buf()
        negT = S.sb("negT", [128, 4], BF16); b_negT = S.buf()
        oc = S.sb("oc", [4, 132], F32); b_oc = S.buf()
        cf = S.sb("cf", [4, 8], F32); b_cf = S.buf()
        gt4 = S.sb("gt4", [4, 3], F32); b_gt4 = S.buf()
        S.op("dve", lambda e: e.memset(VS[:, :, 128:129], 1.0), w=[b_VS])
        S.op("dve", lambda e: e.memset(vcx[:, :, 128:129], 1.0), w=[b_vcx])
        S.op("dve", lambda e: e.memset(vwx[:, :, 128:129], 1.0), w=[b_vwx])
        S.op("dve", lambda e: e.memset(vn4[:, :, 128:129], 1.0), w=[b_vn4])

        def unit(kt_ap, ns, q_ap, v_ap, bias_ap, masks, okey, first, last, rd):
            pi = cnt["pst"] % 2; cnt["pst"] += 1
            li = cnt["ptl"] % 3; cnt["ptl"] += 1
            nm = len(masks)
            S.op("pe", lambda e: e.matmul(pst[pi][0:ns, 0:4], lhsT=kt_ap, rhs=q_ap, start=True, stop=(nm == 0)),
                 r=rd, w=[b_pst[pi]])
            for mi, (ml, mr, mrd) in enumerate(masks):
                S.op("pe", lambda e, ml=ml, mr=mr, mi=mi: e.matmul(pst[pi][0:ns, 0:4], lhsT=ml, rhs=mr, start=False,
                                                                    stop=(mi == nm - 1)), r=[b_c] + mrd, w=[b_pst[pi]])
            S.op("act", lambda e: e.activation(out=ptl[li][0:ns, :], in_=pst[pi][0:ns, 0:4], func=AF.Exp,
                                               bias=bias_ap, scale=SCALE), r=[b_pst[pi], b_c], w=[b_ptl[li]])
            S.op("pe", lambda e: e.matmul(pO[okey][:, 0:129], lhsT=ptl[li][0:ns, :], rhs=v_ap, start=first, stop=last),
                 r=[b_ptl[li]] + rd, w=[b_pO[okey]])

        for b in range(DB):
            ii = b % 2
            tk = slice(b * 4, (b + 1) * 4)
            S.dma("sp", pti[:], ptab[b:b + 1, :].partition_broadcast(128), w=[b_pti])
            S.op("dve", lambda e: e.tensor_copy(out=ptf32[:], in_=pti[:]), r=[b_pti], w=[b_ptf32])
            S.op("dve", lambda e: e.tensor_scalar(out=ptf32[:], in0=ptf32[:], scalar1=128.0, scalar2=iot[:, 0:1],
                                                  op0=ALU.mult, op1=ALU.add), r=[b_ptf32, b_c], w=[b_ptf32])
            S.op("dve", lambda e, ii=ii: e.tensor_copy(out=idx[ii][:], in_=ptf32[:]), r=[b_ptf32], w=[b_idx[ii]])
            for j in range(NPG):
                gi = cnt["g"] % 3; cnt["g"] += 1
                S.op("pool", lambda e, gi=gi, ii=ii, j=j: e.indirect_dma_start(
                    out=gth[gi][:], out_offset=None, in_=pool,
                    in_offset=bass.IndirectOffsetOnAxis(ap=idx[ii][:, j:j + 1], axis=0)),
                     r=[b_idx[ii]], w=[b_gth[gi]], dma=True)
                pj = slice(j * 128, (j + 1) * 128)
                S.op("act", lambda e, gi=gi, pj=pj: e.copy(out=XT[:, 0, pj], in_=gth[gi][:, 0:128]), r=[b_gth[gi]], w=[b_XT])
                S.op("dve", lambda e, gi=gi, pj=pj: e.tensor_copy(out=XT[:, 1, pj], in_=gth[gi][:, 128:256]), r=[b_gth[gi]], w=[b_XT])
                S.op("act", lambda e, gi=gi, pj=pj: e.copy(out=KS[:, pj], in_=gth[gi][:, 256:384]), r=[b_gth[gi]], w=[b_KS])
                S.op("dve", lambda e, gi=gi, j=j: e.tensor_copy(out=VS[:, j, 0:128], in_=gth[gi][:, 384:512]), r=[b_gth[gi]], w=[b_VS])
            S.dma("pool", kwb[:], kwT[b], w=[b_kwb])
            S.dma("pool", vwx[:, :, 0:128], vw[b].rearrange("(n p) d -> p n d", p=128), w=[b_vwx])
            S.dma("pool", vn4[0:4, 0, 0:128], vsn[tk, :], w=[b_vn4])
            S.dma("pool", vn4[0:4, 1, 0:128], vwn[tk, :], w=[b_vn4])
            S.dma("sp", gt4[:], gts[tk, :], w=[b_gt4])
            for kv in range(2):
                for nt in range(NBT):
                    nb = min(128, NB - nt * 128)
                    for c in range(32):
                        lhs = XT[:, kv, nt * 4096 + c:nt * 4096 + c + 32 * (nb - 1) + 1:32]
                        S.op("pe", lambda e, lhs=lhs, kv=kv, c=c, nb=nb: e.matmul(
                            pmi[0:nb, :], lhsT=lhs, rhs=cwb[:, kv, c, :], start=(c == 0), stop=False),
                             r=[b_XT, b_c], w=[b_pmi])
                    S.op("pe", lambda e, kv=kv, nb=nb: e.matmul(pmi[0:nb, :], lhsT=onesb[0:1, 0:nb], rhs=pw[0:1, kv, :],
                                                                start=False, stop=True), r=[b_c], w=[b_pmi])
                    if kv == 0:
                        S.op("dve", lambda e, nt=nt, nb=nb: e.tensor_copy(out=kcb[0:nb, nt, :], in_=pmi[0:nb, :]),
                             r=[b_pmi], w=[b_kcb])
                        S.op("pe", lambda e, nt=nt, nb=nb: e.transpose(out=pmb[:, 0:nb], in_=kcb[0:nb, nt, :],
                                                                        identity=idb[0:nb, 0:nb]), r=[b_kcb, b_c], w=[b_pmb])
                        S.op("dve", lambda e, nt=nt, nb=nb: e.tensor_copy(out=kcT[:, nt * 128:nt * 128 + nb], in_=pmb[:, 0:nb]),
                             r=[b_pmb], w=[b_kcT])
                    else:
                        S.op("dve", lambda e, nt=nt, nb=nb: e.tensor_copy(out=vcx[0:nb, nt, 0:128], in_=pmi[0:nb, :]),
                             r=[b_pmi], w=[b_vcx])
            for hh in range(4):
                q_ap = qT[:, hh, tk]
                for nt in range(NBT):
                    nb = min(128, NB - nt * 128)
                    pi = cnt["pst"] % 2; cnt["pst"] += 1
                    fi = cnt["ptf"] % 2; cnt["ptf"] += 1
                    k_ap = kcT[:, nt * 128:nt * 128 + nb]
                    cb_ap = cb[0:nb, nt * 4 + hh:nt * 4 + hh + 1]
                    a_ap = Am[0:nb, nt * (NSEL + 1):(nt + 1) * (NSEL + 1)]
                    S.op("pe", lambda e, pi=pi, k_ap=k_ap, q_ap=q_ap, nb=nb: e.matmul(
                        pst[pi][0:nb, 0:4], lhsT=k_ap, rhs=q_ap, start=True, stop=True), r=[b_kcT, b_c], w=[b_pst[pi]])
                    S.op("act", lambda e, pi=pi, fi=fi, cb_ap=cb_ap, nb=nb: e.activation(
                        out=ptf[fi][0:nb, :], in_=pst[pi][0:nb, 0:4], func=AF.Exp, bias=cb_ap, scale=SCALE),
                         r=[b_pst[pi], b_c], w=[b_ptf[fi]])
                    S.op("pe", lambda e, fi=fi, a_ap=a_ap, nb=nb, nt=nt: e.matmul(
                        psc[:, :], lhsT=ptf[fi][0:nb, :], rhs=a_ap, start=(nt == 0), stop=(nt == NBT - 1)),
                         r=[b_ptf[fi], b_c], w=[b_psc])
                S.op("dve", lambda e: e.tensor_scalar_max(out=rr[:, 0:1], in0=psc[:, NSEL:NSEL + 1], scalar1=1e-30),
                     r=[b_psc], w=[b_rr])
                S.op("dve", lambda e: e.reciprocal(out=rr[:, 1:2], in_=rr[:, 0:1]), r=[b_rr], w=[b_rr])
                if hh == 0:
                    S.op("dve", lambda e: e.tensor_scalar_mul(out=sc[:], in0=psc[:, 0:NSEL], scalar1=rr[:, 1:2]),
                         r=[b_psc, b_rr], w=[b_sc])
                else:
                    S.op("dve", lambda e: e.scalar_tensor_tensor(out=sc[:], in0=psc[:, 0:NSEL], scalar=rr[:, 1:2],
                                                                 in1=sc[:], op0=ALU.mult, op1=ALU.add),
                         r=[b_psc, b_rr, b_sc], w=[b_sc])
            qo_ap = qo[:, tk]
            for nt in range(NBT):
                nb = min(128, NB - nt * 128)
                unit(kcT[:, nt * 128:nt * 128 + nb], nb, qo_ap, vcx[0:nb, nt, 0:129], cbo[0:nb, nt:nt + 1], [], "c",
                     nt == 0, nt == NBT - 1, [b_kcT, b_vcx, b_c])
            S.op("dve", lambda e: e.tensor_tensor(out=sw[:], in0=sc[:], in1=TA[:], op=ALU.mult), r=[b_sc, b_c], w=[b_sw])
            S.op("dve", lambda e: e.tensor_tensor(out=sw[:], in0=sw[:], in1=TB[:], op=ALU.add), r=[b_sw, b_c], w=[b_sw])
            S.op("dve", lambda e: e.max(out=mx[:, 0:8], in_=sw[:]), r=[b_sw], w=[b_mx])
            S.op("dve", lambda e: e.match_replace(out=sw2[:], in_to_replace=mx[:, 0:8], in_values=sw[:], imm_value=-2.0),
                 r=[b_sw, b_mx], w=[b_sw2])
            S.op("dve", lambda e: e.max(out=mx[:, 8:16], in_=sw2[:]), r=[b_sw2], w=[b_mx])
            S.op("dve", lambda e: e.tensor_scalar(out=self_[:], in0=sw[:], scalar1=mx[:, 15:16], scalar2=None, op0=ALU.is_ge),
                 r=[b_sw, b_mx], w=[b_self])
            S.op("dve", lambda e: e.tensor_scalar(out=sw2[:], in0=sw[:], scalar1=0.0, scalar2=None, op0=ALU.is_ge),
                 r=[b_sw], w=[b_sw2])
            S.op("dve", lambda e: e.tensor_tensor(out=self_[:], in0=self_[:], in1=sw2[:], op=ALU.mult),
                 r=[b_self, b_sw2], w=[b_self])
            S.dma("sp", o_sel[tk, :], self_[:], r=[b_self])
            nblk = NPG * 2
            S.op("dve", lambda e: e.memset(selb[:], 0.0), w=[b_selb])
            S.op("dve", lambda e: e.tensor_copy(out=selb[:, 0:nblk], in_=self_[:, 0:nblk]), r=[b_self], w=[b_selb])
            S.op("pe", lambda e: e.transpose(out=pmb[:, 0:4], in_=selb[0:4, :], identity=idb[0:4, 0:4]),
                 r=[b_selb, b_c], w=[b_pmb])
            S.op("dve", lambda e: e.tensor_scalar(out=negT[:, :], in0=pmb[:, 0:4], scalar1=-1.0, scalar2=-NEG,
                                                  op0=ALU.add, op1=ALU.mult), r=[b_pmb], w=[b_negT])
            for j in range(NPG):
                unit(KS[:, j * 128:(j + 1) * 128], 128, qo_ap, VS[:, j, 0:129], ab[:, j:j + 1],
                     [(Eb[0:nblk, j * 128:(j + 1) * 128], negT[0:nblk, :], [b_negT])], "s", j == 0, False,
                     [b_KS, b_VS, b_c])
            unit(ksnb[:, tk], 4, qo_ap, vn4[0:4, 0, 0:129], ab[0:4, NPG:NPG + 1], [(idb[0:4, 0:4], n4b[0:4, 0:4], [])],
                 "s", False, True, [b_c, b_vn4])
            for wi in range(W_BUF // 128):
                masks = [(idb[:], w4b[:, 0:4], [])] if wi == 0 else []
                unit(kwb[:, wi * 128:(wi + 1) * 128], 128, qo_ap, vwx[:, wi, 0:129], aw[:, wi:wi + 1], masks, "w",
                     wi == 0, False, [b_kwb, b_vwx, b_c])
            unit(kwnb[:, tk], 4, qo_ap, vn4[0:4, 1, 0:129], aw[0:4, 4:5], [(idb[0:4, 0:4], n4b[0:4, 0:4], [])],
                 "w", False, True, [b_c, b_vn4])
            for bi, key in enumerate("csw"):
                S.op("dve", lambda e, key=key, bi=bi: e.tensor_scalar_max(out=cf[:, bi:bi + 1], in0=pO[key][:, 128:129],
                                                                         scalar1=1e-30), r=[b_pO[key]], w=[b_cf])
            S.op("dve", lambda e: e.reciprocal(out=cf[:, 3:6], in_=cf[:, 0:3]), r=[b_cf], w=[b_cf])
            S.op("dve", lambda e: e.tensor_tensor(out=cf[:, 3:6], in0=cf[:, 3:6], in1=gt4[:], op=ALU.mult),
                 r=[b_cf, b_gt4], w=[b_cf])
            S.op("dve", lambda e: e.tensor_scalar_mul(out=oc[:, 0:128], in0=pO["c"][:, 0:128], scalar1=cf[:, 3:4]),
                 r=[b_pO["c"], b_cf], w=[b_oc])
            for bi, key in ((1, "s"), (2, "w")):
                S.op("dve", lambda e, key=key, bi=bi: e.scalar_tensor_tensor(
                    out=oc[:, 0:128], in0=pO[key][:, 0:128], scalar=cf[:, 3 + bi:4 + bi], in1=oc[:, 0:128],
                    op0=ALU.mult, op1=ALU.add), r=[b_pO[key], b_cf, b_oc], w=[b_oc])
            S.dma("sp", o_n[tk, :], oc[:, 0:128], r=[b_oc])
        S.emit()
    return nc


def _nsa_s_consts(head, NPG):
    f32 = np.float32
    P = NPG * 128
    NB = P // 32
    NBT = (NB + 127) // 128
    NSEL = NPG * 2 + 1
    p = np.arange(128)
    slopes = np.exp2(-8.0 * np.arange(1, 9, dtype=np.float64) / 8)
    g = head // 4
    cb = np.zeros((128, NBT, 4), f32)
    for nt in range(NBT):
        c_end = 32 * (nt * 128 + p + 1) - 1
        for hh in range(4):
            cb[:, nt, hh] = slopes[4 * g + hh] * (c_end - P)
    cbo = np.ascontiguousarray(cb[:, :, head % 4])
    ab = np.zeros((128, NPG + 1), f32)
    for j in range(NPG):
        ab[:, j] = slopes[head] * (128 * j + p - P)
    ab[:, NPG] = slopes[head] * p
    aw = np.zeros((128, 5), f32)
    for wi in range(4):
        aw[:, wi] = slopes[head] * (-W_BUF + 128 * wi + p)
    aw[:, 4] = slopes[head] * p
    A = np.zeros((128, NBT, NSEL + 1), f32)
    for nt in range(NBT):
        for n in range(128):
            if nt * 128 + n < NB:
                A[n, nt, (nt * 128 + n) // 2] = 1.0
    A[:, :, NSEL] = 1.0
    blk = np.arange(NSEL)
    cur = NPG * 2
    f1 = blk == cur; f2 = blk == cur - 1; f0 = blk == 0
    forced = f0 | f1 | f2
    TA = np.tile((~forced).astype(f32)[None], (4, 1))
    TB = np.tile(np.where(forced, np.where(f1, 10002.0, np.where(f2, 10001.0, 10000.0)), 0.0).astype(f32)[None], (4, 1))
    E = np.zeros((128, NPG, 128), f32)
    for j in range(NPG):
        E[2 * j, j, 0:64] = 1.0
        E[2 * j + 1, j, 64:128] = 1.0
    t4 = np.arange(4)
    return dict(c_id=np.eye(128, dtype=f32), c_iota=p.astype(f32)[:, None],
                c_n4=np.where(t4[:, None] > t4[None, :], NEG, 0.0).astype(f32),
                c_w4=np.where(p[:, None] <= t4[None, :], NEG, 0.0).astype(f32),
                c_cb=cb.reshape(128, -1), c_cbo=cbo, c_ab=ab, c_aw=aw, c_A=A.reshape(128, -1), c_TA=TA, c_TB=TB,
                c_E=E.reshape(128, -1))


def launch_nsa_s(L1, cache_nsa_kv, state_nsa_win, page_table, cmp_pe, cmp_w):
    f32 = np.float32
    DB, DS = L1["DB"], L1["DS"]
    ck = np.asarray(cache_nsa_kv)[0]
    NPOOL = ck.shape[0]
    pt = np.ascontiguousarray(np.asarray(page_table, np.int32))
    NPG = pt.shape[1]
    nc = build_nsa_s(DB, NPG, NPOOL)
    nkv = L1["nkv_s"].reshape(DB * DS, 4, 2, 128)
    win_new = L1["win_s"].reshape(DB, W_BUF, 2, 2, 128)[:, W_BUF - DS:]
    win_new = win_new.reshape(DB * DS, 2, 2, 128)
    stw = np.asarray(state_nsa_win, f32)[0]
    q = L1["q_s"].reshape(DB * DS, 16, 128)
    gates = L1["g_s"].reshape(DB * DS, 8, 3)
    pe = np.asarray(cmp_pe, f32)[0]; cwt = np.asarray(cmp_w, f32)[0]
    in_maps = []
    for c in range(8):
        g = c // 4
        m = _nsa_s_consts(c, NPG)
        m["qg"] = np.ascontiguousarray(q[:, 4 * g:4 * g + 4, :].transpose(2, 1, 0))
        m["qoT"] = np.ascontiguousarray(q[:, c, :].T)
        tr = lambda a: a.transpose(0, 2, 1)
        m["pool"] = np.ascontiguousarray(np.concatenate(
            [tr(ck[:, :, 0, g, :]), tr(ck[:, :, 1, g, :]), tr(ck[:, :, 2, g, :]), ck[:, :, 3, g, :]], axis=2)
        ).reshape(NPOOL * 128, 512)
        m["ptab"] = pt
        m["pe_in"] = np.ascontiguousarray(pe.transpose(0, 2, 1))
        m["cw"] = np.ascontiguousarray(cwt.transpose(0, 2, 1, 3))
        m["ksn"] = np.ascontiguousarray(nkv[:, 2, g, :].T)
        m["vsn"] = np.ascontiguousarray(nkv[:, 3, g, :])
        m["kwn"] = np.ascontiguousarray(win_new[:, 0, g, :].T)
        m["vwn"] = np.ascontiguousarray(win_new[:, 1, g, :])
        m["kwT"] = np.ascontiguousarray(stw[:, :, 0, g, :].transpose(0, 2, 1))
        m["vw"] = np.ascontiguousarray(stw[:, :, 1, g, :])
        m["gts"] = np.ascontiguousarray(gates[:, c, :])
        in_maps.append(m)
    res = run_bass_kernel_spmd(nc, in_maps, core_ids=list(range(8))).results
    o_n = np.stack([res[c]["o_n"] for c in range(8)], axis=1)
    sel = np.stack([res[c]["o_sel"] for c in range(8)], axis=1)
    return o_n, sel


def build_out(SEQ):
    TOK = SEQ // 4
    NT = TOK // 128
    NTT = NT + 1
    nc = bass.Bass("TRN2", target_bir_lowering=False)
    dt = nc.dram_tensor
    x_own = dt("x_own", [TOK, D_MODEL], F32, kind="ExternalInput").ap()
    x_s = dt("x_s", [128, D_MODEL], F32, kind="ExternalInput").ap()
    o_own = dt("o_own", [TOK, D_MODEL], F32, kind="ExternalInput").ap()
    o_s = dt("o_s", [128, D_MODEL], F32, kind="ExternalInput").ap()
    z_own = dt("z_own", [TOK, D_MODEL], BF16, kind="ExternalInput").ap()
    z_s = dt("z_s", [128, D_MODEL], BF16, kind="ExternalInput").ap()
    w_out = dt("w_out", [D_MODEL, D_MODEL], F32, kind="ExternalInput").ap()
    g_fin = dt("norm_final", [1, D_MODEL], F32, kind="ExternalInput").ap()
    ident = dt("ident", [128, 128], F32, kind="ExternalInput").ap()
    y_own = dt("y_own", [TOK, D_MODEL], F32, kind="ExternalOutput").ap()
    y_s = dt("y_s", [128, D_MODEL], F32, kind="ExternalOutput").ap()
    w_v = w_out.rearrange("(k p) c -> p k c", p=128)
    with ExitStack() as st:
        S = Sched(nc, st)
        b_c = S.buf("consts")
        idb = S.sb("idb", [128, 128], BF16)
        S.dma("pool", idb[:], ident, w=[b_c])
        gam = S.sb("gam", [128, D_MODEL], F32)
        S.dma("sp", gam[:], g_fin.partition_broadcast(128), w=[b_c])
        wo = S.sb("wo", [128, 16, D_MODEL], BF16)
        for cg in range(4):
            S.dma("pool", wo[:, :, cg * 512:(cg + 1) * 512], w_v[:, :, cg * 512:(cg + 1) * 512], w=[b_c])
        xt = [S.sb(f"xt{i}", [128, D_MODEL], F32) for i in range(2)]; b_xt = [S.buf() for _ in range(2)]
        ot = [S.sb(f"ot{i}", [128, D_MODEL], F32) for i in range(2)]; b_ot = [S.buf() for _ in range(2)]
        zt = [S.sb(f"zt{i}", [128, D_MODEL], BF16) for i in range(2)]; b_zt = [S.buf() for _ in range(2)]
        mixb = [S.sb(f"mix{i}", [128, D_MODEL], BF16) for i in range(2)]; b_mix = [S.buf() for _ in range(2)]
        mT = [S.sb(f"mT{i}", [128, 16, 128], BF16) for i in range(2)]; b_mT = [S.buf() for _ in range(2)]
        yt = [S.sb(f"yt{i}", [128, D_MODEL], F32) for i in range(2)]; b_yt = [S.buf() for _ in range(2)]
        sq = S.sb("sq", [128, D_MODEL], BF16); b_sq = S.buf()
        ss = [S.sb(f"ss{i}", [128, 4], F32) for i in range(2)]; b_ss = [S.buf() for _ in range(2)]
        pT = [S.ps(f"pT{i}", [128, 512], BF16) for i in range(2)]; b_pT = [S.buf() for _ in range(2)]
        acc = [S.ps(f"acc{i}", [128, 512], F32) for i in range(2)]; b_acc = [S.buf() for _ in range(2)]
        na = 0
        for tt in range(NTT):
            i = tt % 2
            samp = tt == NT
            rows = slice(tt * 128, (tt + 1) * 128)
            S.dma("sp", xt[i][:], x_s if samp else x_own[rows, :], w=[b_xt[i]])
            S.dma("sp", ot[i][:], o_s if samp else o_own[rows, :], w=[b_ot[i]])
            S.dma("sp", zt[i][:], z_s if samp else z_own[rows, :], w=[b_zt[i]])
            S.op("dve", lambda e, i=i: e.tensor_tensor(out=mixb[i][:], in0=ot[i][:], in1=zt[i][:], op=ALU.mult),
                 r=[b_ot[i], b_zt[i]], w=[b_mix[i]])
            for g4 in range(4):
                p = g4 % 2
                for q in range(4):
                    k = g4 * 4 + q
                    S.op("pe", lambda e, i=i, p=p, q=q, k=k: e.transpose(
                        out=pT[p][:, q * 128:(q + 1) * 128], in_=mixb[i][:, k * 128:(k + 1) * 128], identity=idb[:]),
                         r=[b_mix[i], b_c], w=[b_pT[p]])
                dst = mT[i][:, g4 * 4:(g4 + 1) * 4, :]
                srcp = pT[p][:].rearrange("p (q t) -> p q t", q=4)
                if g4 % 2 == 0:
                    S.op("act", lambda e, dst=dst, srcp=srcp: e.copy(out=dst, in_=srcp), r=[b_pT[p]], w=[b_mT[i]])
                else:
                    S.op("dve", lambda e, dst=dst, srcp=srcp: e.tensor_copy(out=dst, in_=srcp), r=[b_pT[p]], w=[b_mT[i]])
            for cg in range(4):
                a = na % 2; na += 1
                cs = slice(cg * 512, (cg + 1) * 512)
                for k in range(16):
                    S.op("pe", lambda e, a=a, i=i, k=k, cs=cs: e.matmul(acc[a][:], lhsT=mT[i][:, k, :], rhs=wo[:, k, cs],
                                                                        start=(k == 0), stop=(k == 15)),
                         r=[b_mT[i], b_c], w=[b_acc[a]])
                S.op("dve", lambda e, a=a, i=i, cs=cs: e.tensor_tensor(out=yt[i][:, cs], in0=acc[a][:], in1=xt[i][:, cs], op=ALU.add),
                     r=[b_acc[a], b_xt[i]], w=[b_yt[i]])
            S.op("dve", lambda e, i=i: e.memset(ss[i][:], 0.0), w=[b_ss[i]])
            S.op("act", lambda e, i=i: e.activation(out=sq[:], in_=yt[i][:], func=AF.Square, accum_out=ss[i][:, 0:1]),
                 r=[b_yt[i]], w=[b_sq, b_ss[i]])
            S.op("act", lambda e, i=i: e.activation(out=ss[i][:, 1:2], in_=ss[i][:, 0:1], func=AF.Sqrt,
                                                    scale=1.0 / D_MODEL, bias=RMS_EPS), r=[b_ss[i]], w=[b_ss[i]])
            S.op("dve", lambda e, i=i: e.reciprocal(out=ss[i][:, 2:3], in_=ss[i][:, 1:2]), r=[b_ss[i]], w=[b_ss[i]])
            S.op("dve", lambda e, i=i: e.scalar_tensor_tensor(out=yt[i][:], in0=yt[i][:], scalar=ss[i][:, 2:3], in1=gam[:],
                                                              op0=ALU.mult, op1=ALU.mult),
                 r=[b_yt[i], b_ss[i], b_c], w=[b_yt[i]])
            S.dma("sp", y_s if samp else y_own[rows, :], yt[i][:], r=[b_yt[i]])
        S.emit()
    return nc


def launch_out(L1, x_prompt, x_sample, o_p, o_s, w_out, norm_final):
    f32 = np.float32
    B, SEQ, DB, DS = L1["B"], L1["SEQ"], L1["DB"], L1["DS"]
    TOK = SEQ // 4
    nc = build_out(SEQ)
    x_prompt = np.asarray(x_prompt, f32)
    common = dict(x_s=np.ascontiguousarray(np.asarray(x_sample, f32).reshape(DB * DS, D_MODEL)),
                  o_s=np.ascontiguousarray(o_s.reshape(DB * DS, D_MODEL)), z_s=np.ascontiguousarray(L1["z_s"]),
                  w_out=np.ascontiguousarray(np.asarray(w_out, f32)[0]),
                  norm_final=np.asarray(norm_final, f32).reshape(1, D_MODEL), ident=np.eye(128, dtype=f32))
    o_p = o_p.reshape(B, SEQ, D_MODEL)
    in_maps = []
    for c in range(8):
        b, r = c // 4, c % 4
        sl = slice(r * TOK, (r + 1) * TOK)
        m = dict(common)
        m["x_own"] = np.ascontiguousarray(x_prompt[b, sl])
        m["o_own"] = np.ascontiguousarray(o_p[b, sl])
        m["z_own"] = np.ascontiguousarray(L1["z_p"][b, sl])
        in_maps.append(m)
    res = run_bass_kernel_spmd(nc, in_maps, core_ids=list(range(8))).results
    y_p = np.stack([np.concatenate([res[b * 4 + r]["y_own"] for r in range(4)], axis=0) for b in range(2)])
    y_s = res[0]["y_s"].reshape(DB, DS, D_MODEL)
    return y_p, y_s


def kernel(x_prompt, x_sample, cache_nsa_kv, cache_fox_kv, cache_fox_logf, state_nsa_win, page_table,
           norm_in, w_in, b_gate, b_forget, cmp_pe, cmp_w, w_out, norm_final):
    L1 = launch1(x_prompt, x_sample, state_nsa_win, norm_in, w_in, b_gate, b_forget)
    B, SEQ, DB, DS = L1["B"], L1["SEQ"], L1["DB"], L1["DS"]
    of_p, of_s = launch_fox(L1, cache_fox_kv, cache_fox_logf, page_table)
    on_p, _ = launch_nsa_p(L1, cmp_pe, cmp_w)
    on_s, _ = launch_nsa_s4(L1, cache_nsa_kv, state_nsa_win, page_table, cmp_pe, cmp_w)
    o_p = np.concatenate([on_p, of_p], axis=2)
    o_s = np.concatenate([on_s, of_s], axis=1)
    y_p, y_s = launch_out(L1, x_prompt, x_sample, o_p, o_s, w_out, norm_final)
    nkv_p = L1["nkv_p"].reshape(1, B, SEQ, 4, 2, 128)
    fkv_p = L1["fkv_p"].reshape(1, B, SEQ, 2, 8, 128)
    lf_p = L1["lf_p"].reshape(1, B, SEQ, 8)
    wk = min(W_BUF, SEQ)
    win_p = L1["win_p"][:, SEQ - wk:].reshape(1, B, wk, 2, 2, 128)
    nkv_s = L1["nkv_s"].reshape(1, DB, DS, 4, 2, 128)
    fkv_s = L1["fkv_s"].reshape(1, DB, DS, 2, 8, 128)
    lf_s = L1["lf_s"].reshape(1, DB, DS, 8)
    win_s = L1["win_s"].reshape(1, DB, W_BUF, 2, 2, 128)
    return (y_p, y_s, nkv_p, nkv_s, fkv_p, fkv_s, lf_p, lf_s, win_p, win_s)


def build_nsa_s4(NSQ, NPG, NPOOL):
    P = NPG * 128
    NB = P // 32
    NBT = (NB + 127) // 128
    NSEL = NPG * 2 + 1
    nblk = NPG * 2
    assert nblk <= 128
    I32 = mybir.dt.int32
    NQ = NSQ * 16
    nc = bass.Bass("TRN2", target_bir_lowering=False)
    dt = nc.dram_tensor
    qg = dt("qg", [128, NQ], BF16, kind="ExternalInput").ap()
    pool = dt("pool", [NPOOL * 128, 512], F32, kind="ExternalInput").ap()
    ptab = dt("ptab", [NSQ, NPG], I32, kind="ExternalInput").ap()
    pe_in = dt("pe_in", [2, 128, 32], F32, kind="ExternalInput").ap()
    cw = dt("cw", [2, 128, 32, 128], F32, kind="ExternalInput").ap()
    ksn = dt("ksn", [128, NSQ * 4], F32, kind="ExternalInput").ap()
    vsn = dt("vsn", [NSQ * 4, 128], F32, kind="ExternalInput").ap()
    kwn = dt("kwn", [128, NSQ * 4], F32, kind="ExternalInput").ap()
    vwn = dt("vwn", [NSQ * 4, 128], F32, kind="ExternalInput").ap()
    kwT = dt("kwT", [NSQ, 128, W_BUF], F32, kind="ExternalInput").ap()
    vw = dt("vw", [NSQ, W_BUF, 128], F32, kind="ExternalInput").ap()
    gts = dt("gts", [NQ, 3], F32, kind="ExternalInput").ap()
    c_id = dt("c_id", [128, 128], F32, kind="ExternalInput").ap()
    c_iota = dt("c_iota", [128, 1], F32, kind="ExternalInput").ap()
    c_n4 = dt("c_n4", [4, 16], F32, kind="ExternalInput").ap()
    c_w4 = dt("c_w4", [128, 16], F32, kind="ExternalInput").ap()
    c_cb = dt("c_cb", [128, NBT * 4], F32, kind="ExternalInput").ap()
    c_ab = dt("c_ab", [128, 4 * (NPG + 1)], F32, kind="ExternalInput").ap()
    c_aw = dt("c_aw", [128, 20], F32, kind="ExternalInput").ap()
    c_A = dt("c_A", [128, NBT * (NSEL + 1)], F32, kind="ExternalInput").ap()
    c_TA = dt("c_TA", [4, NSEL], F32, kind="ExternalInput").ap()
    c_TB = dt("c_TB", [4, NSEL], F32, kind="ExternalInput").ap()
    c_E = dt("c_E", [128, NPG * 128], F32, kind="ExternalInput").ap()
    o_n = dt("o_n", [NQ, 128], F32, kind="ExternalOutput").ap()
    o_sel = dt("o_sel", [NSQ * 4, NSEL], F32, kind="ExternalOutput").ap()

    with ExitStack() as st:
        S = Sched(nc, st)
        b_c = S.buf("consts")

        def cst(name, ap, shape, to_bf=False):
            if to_bf:
                tb = S.sb(name + "b", shape, BF16)
                S.dma("pool", tb[:], ap, w=[b_c])
                return tb
            t = S.sb(name, shape, F32)
            S.dma("sp", t[:], ap, w=[b_c])
            return t
        idb = cst("id", c_id, [128, 128], True)
        iot = cst("iot", c_iota, [128, 1])
        n4b = cst("n4", c_n4, [4, 16], True); w4b = cst("w4", c_w4, [128, 16], True)
        cb = cst("cb", c_cb, [128, NBT * 4])
        ab = cst("ab", c_ab, [128, 4 * (NPG + 1)]); aw = cst("aw", c_aw, [128, 20])
        Am = cst("Am", c_A, [128, NBT * (NSEL + 1)])
        TA = cst("TA", c_TA, [4, NSEL]); TB = cst("TB", c_TB, [4, NSEL])
        Eb = cst("E", c_E, [128, NPG * 128], True)
        cwb = S.sb("cwb", [128, 2, 32, 128], BF16)
        peb = S.sb("peb", [128, 2, 32], BF16)
        for kv in range(2):
            S.dma("pool", cwb[:, kv], cw[kv], w=[b_c])
            S.dma("pool", peb[:, kv, :], pe_in[kv], w=[b_c])
        onesb = S.sb("onesb", [1, 128], BF16)
        S.op("dve", lambda e: e.memset(onesb[:], 1.0), w=[b_c])
        qT = cst("qT", qg, [128, NQ], True)
        ksnb = cst("ksn", ksn, [128, NSQ * 4], True); kwnb = cst("kwn", kwn, [128, NSQ * 4], True)

        pst = [S.ps(f"pst{i}", [128, 16], F32) for i in range(2)]; b_pst = [S.buf() for _ in range(2)]
        pO = {k: S.ps("pO" + k, [16, 132], F32) for k in "csw"}; b_pO = {k: S.buf() for k in "csw"}
        psc = S.ps("psc", [4, NSEL + 1], F32); b_psc = S.buf()
        pmi = S.ps("pmi", [128, 128], F32); b_pmi = S.buf()
        pmb = S.ps("pmb", [128, 128], BF16); b_pmb = S.buf()
        cnt = dict(pst=0, ptl=0, ptf=0, g=0)
        ptl = [S.sb(f"ptl{i}", [128, 16], BF16) for i in range(3)]; b_ptl = [S.buf() for _ in range(3)]
        ptf = [S.sb(f"ptf{i}", [128, 16], F32) for i in range(2)]; b_ptf = [S.buf() for _ in range(2)]

        pw = S.sb("pw", [1, 2, 128], BF16)
        for kv in range(2):
            for c in range(32):
                S.op("pe", lambda e, kv=kv, c=c: e.matmul(pmi[0:1, :], lhsT=peb[:, kv, c:c + 1], rhs=cwb[:, kv, c, :],
                                                          start=(c == 0), stop=(c == 31)), r=[b_c], w=[b_pmi])
            S.op("dve", lambda e, kv=kv: e.tensor_copy(out=pw[0:1, kv, :], in_=pmi[0:1, :]), r=[b_pmi], w=[b_c])

        pti = S.sb("pti", [128, NPG], I32); b_pti = S.buf()
        ptf32 = S.sb("ptf32", [128, NPG], F32); b_ptf32 = S.buf()
        idx = [S.sb(f"idx{i}", [128, NPG], I32) for i in range(2)]; b_idx = [S.buf() for _ in range(2)]
        gth = [S.sb(f"gth{i}", [128, 512], F32) for i in range(6)]; b_gth = [S.buf() for _ in range(6)]
        XT = S.sb("XT", [128, 2, P], BF16); b_XT = S.buf()
        KS = S.sb("KS", [128, P], BF16); b_KS = S.buf()
        VS = S.sb("VS", [128, NPG, 132], BF16); b_VS = S.buf()
        kcb = S.sb("kcb", [128, NBT, 128], BF16); b_kcb = S.buf()
        kcT = S.sb("kcT", [128, NBT * 128], BF16); b_kcT = S.buf()
        vcx = S.sb("vcx", [128, NBT, 132], BF16); b_vcx = S.buf()
        kwb = S.sb("kwb", [128, W_BUF], BF16); b_kwb = S.buf()
        vwx = S.sb("vwx", [128, W_BUF // 128, 132], BF16); b_vwx = S.buf()
        vn4 = S.sb("vn4", [4, 2, 132], BF16); b_vn4 = S.buf()
        sc = S.sb("sc", [4, NSEL], F32); b_sc = S.buf()
        sw = S.sb("sw", [4, NSEL], F32); b_sw = S.buf()
        sw2 = S.sb("sw2", [4, NSEL], F32); b_sw2 = S.buf()
        mx = S.sb("mx", [4, 16], F32); b_mx = S.buf()
        rr = S.sb("rr", [4, 8], F32); b_rr = S.buf()
        self_ = S.sb("self", [4, NSEL], F32); b_self = S.buf()
        selb = S.sb("selb", [4, 128], BF16); b_selb = S.buf()
        negT = S.sb("negT", [128, 16], BF16); b_negT = S.buf()
        oc = S.sb("oc", [16, 132], F32); b_oc = S.buf()
        cf = S.sb("cf", [16, 8], F32); b_cf = S.buf()
        gt16 = S.sb("gt16", [16, 3], F32); b_gt16 = S.buf()
        S.op("dve", lambda e: e.memset(VS[:, :, 128:129], 1.0), w=[b_VS])
        S.op("dve", lambda e: e.memset(vcx[:, :, 128:129], 1.0), w=[b_vcx])
        S.op("dve", lambda e: e.memset(vwx[:, :, 128:129], 1.0), w=[b_vwx])
        S.op("dve", lambda e: e.memset(vn4[:, :, 128:129], 1.0), w=[b_vn4])

        def unit16(kt_ap, ns, q_ap, v_ap, bias_aps, masks, okey, first, last, rd, out_f32=None):
            pi = cnt["pst"] % 2; cnt["pst"] += 1
            li = cnt["ptl"] % 3; cnt["ptl"] += 1
            nm = len(masks)
            S.op("pe", lambda e: e.matmul(pst[pi][0:ns, 0:16], lhsT=kt_ap, rhs=q_ap, start=True, stop=(nm == 0)),
                 r=rd, w=[b_pst[pi]])
            for mi, (ml, mr, mrd) in enumerate(masks):
                S.op("pe", lambda e, ml=ml, mr=mr, mi=mi: e.matmul(pst[pi][0:ns, 0:16], lhsT=ml, rhs=mr, start=False,
                                                                    stop=(mi == nm - 1)), r=[b_c] + mrd, w=[b_pst[pi]])
            for hh in range(4):
                hs = slice(hh * 4, hh * 4 + 4)
                S.op("act", lambda e, hs=hs, hh=hh: e.activation(out=ptl[li][0:ns, hs], in_=pst[pi][0:ns, hs], func=AF.Exp,
                                                                 bias=bias_aps[hh], scale=SCALE),
                     r=[b_pst[pi], b_c], w=[b_ptl[li]])
            S.op("pe", lambda e: e.matmul(pO[okey][:, 0:129], lhsT=ptl[li][0:ns, :], rhs=v_ap, start=first, stop=last),
                 r=[b_ptl[li]] + rd, w=[b_pO[okey]])

        for sq_ in range(NSQ):
            ii = sq_ % 2
            tk = slice(sq_ * 4, (sq_ + 1) * 4)
            q16 = qT[:, sq_ * 16:(sq_ + 1) * 16]
            S.dma("sp", pti[:], ptab[sq_:sq_ + 1, :].partition_broadcast(128), w=[b_pti])
            S.op("dve", lambda e: e.tensor_copy(out=ptf32[:], in_=pti[:]), r=[b_pti], w=[b_ptf32])
            S.op("dve", lambda e: e.tensor_scalar(out=ptf32[:], in0=ptf32[:], scalar1=128.0, scalar2=iot[:, 0:1],
                                                  op0=ALU.mult, op1=ALU.add), r=[b_ptf32, b_c], w=[b_ptf32])
            S.op("dve", lambda e, ii=ii: e.tensor_copy(out=idx[ii][:], in_=ptf32[:]), r=[b_ptf32], w=[b_idx[ii]])
            for j in range(NPG):
                gi = cnt["g"] % 6; cnt["g"] += 1
                S.op("pool", lambda e, gi=gi, ii=ii, j=j: e.indirect_dma_start(
                    out=gth[gi][:], out_offset=None, in_=pool,
                    in_offset=bass.IndirectOffsetOnAxis(ap=idx[ii][:, j:j + 1], axis=0)),
                     r=[b_idx[ii]], w=[b_gth[gi]], dma=True)
                pj = slice(j * 128, (j + 1) * 128)
                S.op("act", lambda e, gi=gi, pj=pj: e.copy(out=XT[:, 0, pj], in_=gth[gi][:, 0:128]), r=[b_gth[gi]], w=[b_XT])
                S.op("dve", lambda e, gi=gi, pj=pj: e.tensor_copy(out=XT[:, 1, pj], in_=gth[gi][:, 128:256]), r=[b_gth[gi]], w=[b_XT])
                S.op("act", lambda e, gi=gi, pj=pj: e.copy(out=KS[:, pj], in_=gth[gi][:, 256:384]), r=[b_gth[gi]], w=[b_KS])
                S.op("dve", lambda e, gi=gi, j=j: e.tensor_copy(out=VS[:, j, 0:128], in_=gth[gi][:, 384:512]), r=[b_gth[gi]], w=[b_VS])
            S.dma("pool", kwb[:], kwT[sq_], w=[b_kwb])
            S.dma("pool", vwx[:, :, 0:128], vw[sq_].rearrange("(n p) d -> p n d", p=128), w=[b_vwx])
            S.dma("pool", vn4[0:4, 0, 0:128], vsn[tk, :], w=[b_vn4])
            S.dma("pool", vn4[0:4, 1, 0:128], vwn[tk, :], w=[b_vn4])
            S.dma("sp", gt16[:], gts[sq_ * 16:(sq_ + 1) * 16, :], w=[b_gt16])
            for kv in range(2):
                for nt in range(NBT):
                    nb = min(128, NB - nt * 128)
                    for c in range(32):
                        lhs = XT[:, kv, nt * 4096 + c:nt * 4096 + c + 32 * (nb - 1) + 1:32]
                        S.op("pe", lambda e, lhs=lhs, kv=kv, c=c, nb=nb: e.matmul(
                            pmi[0:nb, :], lhsT=lhs, rhs=cwb[:, kv, c, :], start=(c == 0), stop=False),
                             r=[b_XT, b_c], w=[b_pmi])
                    S.op("pe", lambda e, kv=kv, nb=nb: e.matmul(pmi[0:nb, :], lhsT=onesb[0:1, 0:nb], rhs=pw[0:1, kv, :],
                                                                start=False, stop=True), r=[b_c], w=[b_pmi])
                    if kv == 0:
                        S.op("dve", lambda e, nt=nt, nb=nb: e.tensor_copy(out=kcb[0:nb, nt, :], in_=pmi[0:nb, :]),
                             r=[b_pmi], w=[b_kcb])
                        S.op("pe", lambda e, nt=nt, nb=nb: e.transpose(out=pmb[:, 0:nb], in_=kcb[0:nb, nt, :],
                                                                        identity=idb[0:nb, 0:nb]), r=[b_kcb, b_c], w=[b_pmb])
                        S.op("dve", lambda e, nt=nt, nb=nb: e.tensor_copy(out=kcT[:, nt * 128:nt * 128 + nb], in_=pmb[:, 0:nb]),
                             r=[b_pmb], w=[b_kcT])
                    else:
                        S.op("dve", lambda e, nt=nt, nb=nb: e.tensor_copy(out=vcx[0:nb, nt, 0:128], in_=pmi[0:nb, :]),
                             r=[b_pmi], w=[b_vcx])
            ptfs = []
            for nt in range(NBT):
                nb = min(128, NB - nt * 128)
                pi = cnt["pst"] % 2; cnt["pst"] += 1
                fi = cnt["ptf"] % 2; cnt["ptf"] += 1
                assert NBT <= 2
                k_ap = kcT[:, nt * 128:nt * 128 + nb]
                S.op("pe", lambda e, pi=pi, k_ap=k_ap, nb=nb, q16=q16: e.matmul(
                    pst[pi][0:nb, 0:16], lhsT=k_ap, rhs=q16, start=True, stop=True), r=[b_kcT, b_c], w=[b_pst[pi]])
                for hh in range(4):
                    hs = slice(hh * 4, hh * 4 + 4)
                    cb_ap = cb[0:nb, nt * 4 + hh:nt * 4 + hh + 1]
                    S.op("act", lambda e, pi=pi, fi=fi, hs=hs, cb_ap=cb_ap, nb=nb: e.activation(
                        out=ptf[fi][0:nb, hs], in_=pst[pi][0:nb, hs], func=AF.Exp, bias=cb_ap, scale=SCALE),
                         r=[b_pst[pi], b_c], w=[b_ptf[fi]])
                ptfs.append((fi, nb, nt))
            for hh in range(4):
                hs = slice(hh * 4, hh * 4 + 4)
                for (fi, nb, nt) in ptfs:
                    a_ap = Am[0:nb, nt * (NSEL + 1):(nt + 1) * (NSEL + 1)]
                    S.op("pe", lambda e, fi=fi, a_ap=a_ap, nb=nb, nt=nt, hs=hs: e.matmul(
                        psc[:, :], lhsT=ptf[fi][0:nb, hs], rhs=a_ap, start=(nt == 0), stop=(nt == NBT - 1)),
                         r=[b_ptf[fi], b_c], w=[b_psc])
                S.op("dve", lambda e: e.tensor_scalar_max(out=rr[:, 0:1], in0=psc[:, NSEL:NSEL + 1], scalar1=1e-30),
                     r=[b_psc], w=[b_rr])
                S.op("dve", lambda e: e.reciprocal(out=rr[:, 1:2], in_=rr[:, 0:1]), r=[b_rr], w=[b_rr])
                if hh == 0:
                    S.op("dve", lambda e: e.tensor_scalar_mul(out=sc[:], in0=psc[:, 0:NSEL], scalar1=rr[:, 1:2]),
                         r=[b_psc, b_rr], w=[b_sc])
                else:
                    S.op("dve", lambda e: e.scalar_tensor_tensor(out=sc[:], in0=psc[:, 0:NSEL], scalar=rr[:, 1:2],
                                                                 in1=sc[:], op0=ALU.mult, op1=ALU.add),
                         r=[b_psc, b_rr, b_sc], w=[b_sc])
            for (fi, nb, nt) in ptfs:
                li = cnt["ptl"] % 3; cnt["ptl"] += 1
                S.op("dve", lambda e, li=li, fi=fi, nb=nb: e.tensor_copy(out=ptl[li][0:nb, :], in_=ptf[fi][0:nb, :]),
                     r=[b_ptf[fi]], w=[b_ptl[li]])
                S.op("pe", lambda e, li=li, nb=nb, nt=nt: e.matmul(pO["c"][:, 0:129], lhsT=ptl[li][0:nb, :],
                                                                    rhs=vcx[0:nb, nt, 0:129], start=(nt == 0),
                                                                    stop=(nt == NBT - 1)),
                     r=[b_ptl[li], b_vcx], w=[b_pO["c"]])
            S.op("dve", lambda e: e.tensor_tensor(out=sw[:], in0=sc[:], in1=TA[:], op=ALU.mult), r=[b_sc, b_c], w=[b_sw])
            S.op("dve", lambda e: e.tensor_tensor(out=sw[:], in0=sw[:], in1=TB[:], op=ALU.add), r=[b_sw, b_c], w=[b_sw])
            S.op("dve", lambda e: e.max(out=mx[:, 0:8], in_=sw[:]), r=[b_sw], w=[b_mx])
            S.op("dve", lambda e: e.match_replace(out=sw2[:], in_to_replace=mx[:, 0:8], in_values=sw[:], imm_value=-2.0),
                 r=[b_sw, b_mx], w=[b_sw2])
            S.op("dve", lambda e: e.max(out=mx[:, 8:16], in_=sw2[:]), r=[b_sw2], w=[b_mx])
            S.op("dve", lambda e: e.tensor_scalar(out=self_[:], in0=sw[:], scalar1=mx[:, 15:16], scalar2=None, op0=ALU.is_ge),
                 r=[b_sw, b_mx], w=[b_self])
            S.op("dve", lambda e: e.tensor_scalar(out=sw2[:], in0=sw[:], scalar1=0.0, scalar2=None, op0=ALU.is_ge),
                 r=[b_sw], w=[b_sw2])
            S.op("dve", lambda e: e.tensor_tensor(out=self_[:], in0=self_[:], in1=sw2[:], op=ALU.mult),
                 r=[b_self, b_sw2], w=[b_self])
            S.dma("sp", o_sel[tk, :], self_[:], r=[b_self])
            S.op("dve", lambda e: e.memset(selb[:], 0.0), w=[b_selb])
            S.op("dve", lambda e: e.tensor_copy(out=selb[:, 0:nblk], in_=self_[:, 0:nblk]), r=[b_self], w=[b_selb])
            S.op("pe", lambda e: e.transpose(out=pmb[:, 0:4], in_=selb[0:4, :], identity=idb[0:4, 0:4]),
                 r=[b_selb, b_c], w=[b_pmb])
            for hh in range(4):
                S.op("dve", lambda e, hh=hh: e.tensor_scalar(out=negT[:, hh * 4:hh * 4 + 4], in0=pmb[:, 0:4], scalar1=-1.0,
                                                             scalar2=-NEG, op0=ALU.add, op1=ALU.mult),
                     r=[b_pmb], w=[b_negT])
            for j in range(NPG):
                unit16(KS[:, j * 128:(j + 1) * 128], 128, q16, VS[:, j, 0:129],
                       [ab[:, hh * (NPG + 1) + j:hh * (NPG + 1) + j + 1] for hh in range(4)],
                       [(Eb[0:nblk, j * 128:(j + 1) * 128], negT[0:nblk, :], [b_negT])], "s", j == 0, False,
                       [b_KS, b_VS, b_c])
            unit16(ksnb[:, tk], 4, q16, vn4[0:4, 0, 0:129],
                   [ab[0:4, hh * (NPG + 1) + NPG:hh * (NPG + 1) + NPG + 1] for hh in range(4)],
                   [(idb[0:4, 0:4], n4b[0:4, :], [])], "s", False, True, [b_c, b_vn4])
            for wi in range(W_BUF // 128):
                masks = [(idb[:], w4b[:, :], [])] if wi == 0 else []
                unit16(kwb[:, wi * 128:(wi + 1) * 128], 128, q16, vwx[:, wi, 0:129],
                       [aw[:, hh * 5 + wi:hh * 5 + wi + 1] for hh in range(4)], masks, "w", wi == 0, False,
                       [b_kwb, b_vwx, b_c])
            unit16(kwnb[:, tk], 4, q16, vn4[0:4, 1, 0:129], [aw[0:4, hh * 5 + 4:hh * 5 + 5] for hh in range(4)],
                   [(idb[0:4, 0:4], n4b[0:4, :], [])], "w", False, True, [b_c, b_vn4])
            for bi, key in enumerate("csw"):
                S.op("dve", lambda e, key=key, bi=bi: e.tensor_scalar_max(out=cf[:, bi:bi + 1], in0=pO[key][:, 128:129],
                                                                         scalar1=1e-30), r=[b_pO[key]], w=[b_cf])
            S.op("dve", lambda e: e.reciprocal(out=cf[:, 3:6], in_=cf[:, 0:3]), r=[b_cf], w=[b_cf])
            S.op("dve", lambda e: e.tensor_tensor(out=cf[:, 3:6], in0=cf[:, 3:6], in1=gt16[:], op=ALU.mult),
                 r=[b_cf, b_gt16], w=[b_cf])
            S.op("dve", lambda e: e.tensor_scalar_mul(out=oc[:, 0:128], in0=pO["c"][:, 0:128], scalar1=cf[:, 3:4]),
                 r=[b_pO["c"], b_cf], w=[b_oc])
            for bi, key in ((1, "s"), (2, "w")):
                S.op("dve", lambda e, key=key, bi=bi: e.scalar_tensor_tensor(
                    out=oc[:, 0:128], in0=pO[key][:, 0:128], scalar=cf[:, 3 + bi:4 + bi], in1=oc[:, 0:128],
                    op0=ALU.mult, op1=ALU.add), r=[b_pO[key], b_cf, b_oc], w=[b_oc])
            S.dma("sp", o_n[sq_ * 16:(sq_ + 1) * 16, :], oc[:, 0:128], r=[b_oc])
        S.emit()
    return nc


def launch_nsa_s4(L1, cache_nsa_kv, state_nsa_win, page_table, cmp_pe, cmp_w):
    f32 = np.float32
    DB, DS = L1["DB"], L1["DS"]
    NSQ = DB // 4
    ck = np.asarray(cache_nsa_kv)[0]
    NPOOL = ck.shape[0]
    pt = np.ascontiguousarray(np.asarray(page_table, np.int32))
    NPG = pt.shape[1]
    nc = build_nsa_s4(NSQ, NPG, NPOOL)
    nkv = L1["nkv_s"].reshape(DB, DS, 4, 2, 128)
    win_new = L1["win_s"].reshape(DB, W_BUF, 2, 2, 128)[:, W_BUF - DS:]
    stw = np.asarray(state_nsa_win, f32)[0]
    q = L1["q_s"].reshape(DB, DS, 16, 128)
    gates = L1["g_s"].reshape(DB, DS, 8, 3)
    pe = np.asarray(cmp_pe, f32)[0]; cwt = np.asarray(cmp_w, f32)[0]
    slopes = np.exp2(-8.0 * np.arange(1, 9, dtype=np.float64) / 8)
    p = np.arange(128); t4 = np.arange(4)
    in_maps = []
    for c in range(8):
        g, r = c // 4, c % 4
        sq = slice(r * NSQ, (r + 1) * NSQ)
        base = _nsa_s_consts(4 * g, NPG)
        m = {k: base[k] for k in ("c_id", "c_iota", "c_cb", "c_A", "c_TA", "c_TB", "c_E")}
        m["c_n4"] = np.tile(np.where(t4[:, None] > t4[None, :], NEG, 0.0).astype(f32), (1, 4))
        m["c_w4"] = np.tile(np.where(p[:, None] <= t4[None, :], NEG, 0.0).astype(f32), (1, 4))
        ab = np.zeros((128, 4, NPG + 1), f32); aw = np.zeros((128, 4, 5), f32)
        P = NPG * 128
        for hh in range(4):
            sl = slopes[4 * g + hh]
            for j in range(NPG):
                ab[:, hh, j] = sl * (128 * j + p - P)
            ab[:, hh, NPG] = sl * p
            for wi in range(4):
                aw[:, hh, wi] = sl * (-W_BUF + 128 * wi + p)
            aw[:, hh, 4] = sl * p
        m["c_ab"] = ab.reshape(128, -1); m["c_aw"] = aw.reshape(128, -1)
        m["qg"] = np.ascontiguousarray(q[sq, :, 4 * g:4 * g + 4, :].transpose(3, 0, 2, 1)).reshape(128, NSQ * 16)
        tr = lambda a: a.transpose(0, 2, 1)
        m["pool"] = np.ascontiguousarray(np.concatenate(
            [tr(ck[:, :, 0, g, :]), tr(ck[:, :, 1, g, :]), tr(ck[:, :, 2, g, :]), ck[:, :, 3, g, :]], axis=2)
        ).reshape(NPOOL * 128, 512)
        m["ptab"] = np.ascontiguousarray(pt[sq])
        m["pe_in"] = np.ascontiguousarray(pe.transpose(0, 2, 1))
        m["cw"] = np.ascontiguousarray(cwt.transpose(0, 2, 1, 3))
        m["ksn"] = np.ascontiguousarray(nkv[sq, :, 2, g, :].reshape(NSQ * DS, 128).T)
        m["vsn"] = np.ascontiguousarray(nkv[sq, :, 3, g, :].reshape(NSQ * DS, 128))
        m["kwn"] = np.ascontiguousarray(win_new[sq, :, 0, g, :].reshape(NSQ * DS, 128).T)
        m["vwn"] = np.ascontiguousarray(win_new[sq, :, 1, g, :].reshape(NSQ * DS, 128))
        m["kwT"] = np.ascontiguousarray(stw[sq, :, 0, g, :].transpose(0, 2, 1))
        m["vw"] = np.ascontiguousarray(stw[sq, :, 1, g, :])
        m["gts"] = np.ascontiguousarray(gates[sq, :, 4 * g:4 * g + 4, :].transpose(0, 2, 1, 3)).reshape(NSQ * 16, 3)
        in_maps.append(m)
    res = run_bass_kernel_spmd(nc, in_maps, core_ids=list(range(8))).results
    o = np.zeros((DB, DS, 8, 128), f32)
    for c in range(8):
        g, r = c // 4, c % 4
        o[r * NSQ:(r + 1) * NSQ, :, 4 * g:4 * g + 4, :] = res[c]["o_n"].reshape(NSQ, 4, DS, 128).transpose(0, 2, 1, 3)
    sel = np.stack([res[4 * (h // 4)]["o_sel"] for h in range(8)], axis=1) if False else None
    return o.reshape(DB * DS, 8, 128), sel
```
